# Optimizing a Trainium2 kernel written in Bass

```python
import math
import jax, jax.numpy as jnp
from jax import lax
import numpy as np

D_MODEL = 4096
BATCH = 1
SEQ = 8192
DEPTH = 1

MEM_TOKENS = 256
HEAD_DIM = 128
MIX_WIDTH = D_MODEL
DIFF_HEADS = MIX_WIDTH // 2 // HEAD_DIM
DIFF_QK_DIM = HEAD_DIM // 2
DIFF_V_DIM = HEAD_DIM
SB_HEADS = MIX_WIDTH // 2 // HEAD_DIM
SB_DIM = HEAD_DIM
DIFF_WIDTH = DIFF_HEADS * DIFF_V_DIM
SB_WIDTH = SB_HEADS * SB_DIM
IN_WIDTH = 3 * DIFF_WIDTH + 3 * SB_WIDTH
XATTN_HEADS = 4
XATTN_DIM = HEAD_DIM
XATTN_WIDTH = XATTN_HEADS * XATTN_DIM
D_FF = 7 * D_MODEL // 2
ROPE_THETA = 500000.0
ROT_DIM = DIFF_QK_DIM // 4
Q_BLOCK = 128
NORM_EPS = 1e-6

kernel_name = "hymba_diff_stickbreaking_macaron_sandwich"


def rmsnorm(x, gain):
    xf = x.astype(jnp.float32)
    xf = xf * lax.rsqrt(jnp.mean(xf * xf, axis=-1, keepdims=True) + NORM_EPS)
    return xf.astype(x.dtype) * gain


def swiglu(x, w_gate, w_up, w_down):
    return (jax.nn.silu(x @ w_gate) * (x @ w_up)) @ w_down


def rope_cos_sin(positions):
    inv_freq = ROPE_THETA ** (-jnp.arange(0, ROT_DIM, 2, dtype=jnp.float32) / ROT_DIM)
    ang = positions.astype(jnp.float32)[..., None] * inv_freq
    return jnp.cos(ang)[:, None], jnp.sin(ang)[:, None]


def partial_rope(x, cos, sin):
    half = ROT_DIM // 2
    x1 = x[..., :half].astype(jnp.float32)
    x2 = x[..., half:ROT_DIM].astype(jnp.float32)
    rot = jnp.concatenate([x1 * cos - x2 * sin, x2 * cos + x1 * sin], axis=-1).astype(x.dtype)
    return jnp.concatenate([rot, x[..., ROT_DIM:]], axis=-1)


def to_blocks(q):
    b, h, s, d = q.shape
    return jnp.moveaxis(q.reshape(b, h, s // Q_BLOCK, Q_BLOCK, d), 2, 0)


def from_blocks(o):
    nb, b, h, qb, d = o.shape
    return jnp.moveaxis(o, 0, 2).reshape(b, h, nb * qb, d)


def diff_attention(q1, q2, k1, k2, v, lam):
    s_len = k1.shape[2]
    scale = DIFF_QK_DIM ** -0.5
    kpos = jnp.arange(s_len)

    def one_block(args):
        q1b, q2b, blk = args
        qpos = blk * Q_BLOCK + jnp.arange(Q_BLOCK)
        causal = kpos[None, :] <= qpos[:, None]

        def probs(qb, k):
            sc = jnp.einsum('bhqd,bhkd->bhqk', qb, k).astype(jnp.float32) * scale
            return jax.nn.softmax(jnp.where(causal, sc, -jnp.inf), axis=-1)

        w = probs(q1b, k1) - lam * probs(q2b, k2)
        return jnp.einsum('bhqk,bhkd->bhqd', w.astype(v.dtype), v)

    nb = s_len // Q_BLOCK
    out = lax.map(one_block, (to_blocks(q1), to_blocks(q2), jnp.arange(nb)))
    return from_blocks(out)


def stick_breaking_attention(q, k, v):
    s_len = k.shape[2]
    scale = SB_DIM ** -0.5
    kpos = jnp.arange(s_len)

    def one_block(args):
        qb, blk = args
        qpos = blk * Q_BLOCK + jnp.arange(Q_BLOCK)
        strict = kpos[None, :] < qpos[:, None]
        z = jnp.einsum('bhqd,bhkd->bhqk', qb, k).astype(jnp.float32) * scale
        log_1m_beta = jnp.where(strict, jax.nn.log_sigmoid(-z), 0.0)
        suffix = lax.cumsum(log_1m_beta, axis=3, reverse=True) - log_1m_beta
        a = jnp.where(strict, jnp.exp(jax.nn.log_sigmoid(z) + suffix), 0.0)
        return jnp.einsum('bhqk,bhkd->bhqd', a.astype(v.dtype), v)

    nb = s_len // Q_BLOCK
    out = lax.map(one_block, (to_blocks(q), jnp.arange(nb)))
    return from_blocks(out)


def token_mixer(hn, positions, w_in, w_out, lambda_q1, lambda_k1, lambda_q2, lambda_k2,
                diff_subln, sb_norm, lambda_init):
    b, s, _ = hn.shape
    proj = hn @ w_in
    dq, dk, dv, sq, sk, sv = jnp.split(proj, 6, axis=-1)
    heads = lambda t, h, d: t.reshape(b, s, h, d).transpose(0, 2, 1, 3)

    cos, sin = rope_cos_sin(positions)
    dq = dq.reshape(b, s, DIFF_HEADS, 2, DIFF_QK_DIM).transpose(0, 2, 1, 3, 4)
    dk = dk.reshape(b, s, DIFF_HEADS, 2, DIFF_QK_DIM).transpose(0, 2, 1, 3, 4)
    q1 = partial_rope(dq[..., 0, :], cos, sin)
    q2 = partial_rope(dq[..., 1, :], cos, sin)
    k1 = partial_rope(dk[..., 0, :], cos, sin)
    k2 = partial_rope(dk[..., 1, :], cos, sin)
    vd = heads(dv, DIFF_HEADS, DIFF_V_DIM)
    lam = (jnp.exp(jnp.sum(lambda_q1.astype(jnp.float32) * lambda_k1.astype(jnp.float32)))
           - jnp.exp(jnp.sum(lambda_q2.astype(jnp.float32) * lambda_k2.astype(jnp.float32)))
           + lambda_init)
    diff_out = diff_attention(q1, q2, k1, k2, vd, lam)
    diff_out = rmsnorm(diff_out, diff_subln) * (1.0 - lambda_init)

    sb_out = stick_breaking_attention(heads(sq, SB_HEADS, SB_DIM), heads(sk, SB_HEADS, SB_DIM),
                                      heads(sv, SB_HEADS, SB_DIM))
    sb_out = rmsnorm(sb_out, sb_norm)

    merged = jnp.concatenate([
        diff_out.transpose(0, 2, 1, 3).reshape(b, s, DIFF_WIDTH),
        sb_out.transpose(0, 2, 1, 3).reshape(b, s, SB_WIDTH)], axis=-1)
    return merged @ w_out


def memory_cross_attention(hn, mem_n, w_q, w_kv, w_o):
    b, s, _ = hn.shape
    q = (hn @ w_q).reshape(b, s, XATTN_HEADS, XATTN_DIM)
    k, v = jnp.split((mem_n @ w_kv).reshape(b, MEM_TOKENS, 2, XATTN_HEADS, XATTN_DIM), 2, axis=2)
    k, v = k[:, :, 0], v[:, :, 0]
    sc = jnp.einsum('bshd,bmhd->bhsm', q, k).astype(jnp.float32) * XATTN_DIM ** -0.5
    p = jax.nn.softmax(sc, axis=-1).astype(v.dtype)
    o = jnp.einsum('bhsm,bmhd->bshd', p, v).reshape(b, s, XATTN_WIDTH)
    return o @ w_o


def setup_inputs(seed: int = 0) -> dict:
    key = jax.random.key(seed)
    ks = jax.random.split(key, 32)
    L = DEPTH
    nrm = lambda k, shape, fan_in: jax.random.normal(k, shape, jnp.float32) * fan_in ** -0.5
    gain = lambda k, d: 1.0 + 0.02 * jax.random.normal(k, (L, d), jnp.float32)
    offsets = jax.random.randint(ks[2], (BATCH, 1), 0, 4096, dtype=jnp.int32)
    positions = offsets + jnp.arange(SEQ, dtype=jnp.int32)[None, :]
    return {
        "x": jax.random.normal(ks[0], (BATCH, SEQ, D_MODEL), jnp.float32),
        "mem": jax.random.normal(ks[1], (BATCH, MEM_TOKENS, D_MODEL), jnp.float32),
        "positions": positions,
        "ffn1_norm_pre": gain(ks[3], D_MODEL),
        "ffn1_norm_post": gain(ks[4], D_MODEL),
        "ffn1_w_gate": nrm(ks[5], (L, D_MODEL, D_FF), D_MODEL),
        "ffn1_w_up": nrm(ks[6], (L, D_MODEL, D_FF), D_MODEL),
        "ffn1_w_down": nrm(ks[7], (L, D_FF, D_MODEL), D_FF),
        "mix_norm_pre": gain(ks[8], D_MODEL),
        "mix_norm_post": gain(ks[9], D_MODEL),
        "w_in": nrm(ks[10], (L, D_MODEL, IN_WIDTH), D_MODEL),
        "w_out": nrm(ks[11], (L, MIX_WIDTH, D_MODEL), MIX_WIDTH),
        "lambda_q1": 0.1 * jax.random.normal(ks[12], (L, DIFF_QK_DIM), jnp.float32),
        "lambda_k1": 0.1 * jax.random.normal(ks[13], (L, DIFF_QK_DIM), jnp.float32),
        "lambda_q2": 0.1 * jax.random.normal(ks[14], (L, DIFF_QK_DIM), jnp.float32),
        "lambda_k2": 0.1 * jax.random.normal(ks[15], (L, DIFF_QK_DIM), jnp.float32),
        "diff_subln": gain(ks[16], DIFF_V_DIM),
        "sb_norm": gain(ks[17], SB_DIM),
        "xattn_norm_pre": gain(ks[18], D_MODEL),
        "xattn_norm_post": gain(ks[19], D_MODEL),
        "mem_norm": gain(ks[20], D_MODEL),
        "xattn_w_q": nrm(ks[21], (L, D_MODEL, XATTN_WIDTH), D_MODEL),
        "xattn_w_kv": nrm(ks[22], (L, D_MODEL, 2 * XATTN_WIDTH), D_MODEL),
        "xattn_w_o": nrm(ks[23], (L, XATTN_WIDTH, D_MODEL), XATTN_WIDTH),
        "ffn2_norm_pre": gain(ks[24], D_MODEL),
        "ffn2_norm_post": gain(ks[25], D_MODEL),
        "ffn2_w_gate": nrm(ks[26], (L, D_MODEL, D_FF), D_MODEL),
        "ffn2_w_up": nrm(ks[27], (L, D_MODEL, D_FF), D_MODEL),
        "ffn2_w_down": nrm(ks[28], (L, D_FF, D_MODEL), D_FF),
    }


def reference(x, mem, positions, ffn1_norm_pre, ffn1_norm_post, ffn1_w_gate, ffn1_w_up,
              ffn1_w_down, mix_norm_pre, mix_norm_post, w_in, w_out, lambda_q1, lambda_k1,
              lambda_q2, lambda_k2, diff_subln, sb_norm, xattn_norm_pre, xattn_norm_post,
              mem_norm, xattn_w_q, xattn_w_kv, xattn_w_o, ffn2_norm_pre, ffn2_norm_post,
              ffn2_w_gate, ffn2_w_up, ffn2_w_down):
    h = x
    for l in range(DEPTH):
        lambda_init = 0.8 - 0.6 * math.exp(-0.3 * l)
        y = swiglu(rmsnorm(h, ffn1_norm_pre[l]), ffn1_w_gate[l], ffn1_w_up[l], ffn1_w_down[l])
        h = h + 0.5 * rmsnorm(y, ffn1_norm_post[l])
        y = token_mixer(rmsnorm(h, mix_norm_pre[l]), positions, w_in[l], w_out[l],
                        lambda_q1[l], lambda_k1[l], lambda_q2[l], lambda_k2[l],
                        diff_subln[l], sb_norm[l], lambda_init)
        h = h + rmsnorm(y, mix_norm_post[l])
        y = memory_cross_attention(rmsnorm(h, xattn_norm_pre[l]), rmsnorm(mem, mem_norm[l]),
                                   xattn_w_q[l], xattn_w_kv[l], xattn_w_o[l])
        h = h + rmsnorm(y, xattn_norm_post[l])
        y = swiglu(rmsnorm(h, ffn2_norm_pre[l]), ffn2_w_gate[l], ffn2_w_up[l], ffn2_w_down[l])
        h = h + 0.5 * rmsnorm(y, ffn2_norm_post[l])
    return h
```

```python
import math
import contextlib
import numpy as np
import concourse.bass as bass
import concourse.mybir as mybir
from concourse.bass_utils import run_bass_kernel_spmd

F32 = mybir.dt.float32
BF16 = mybir.dt.bfloat16
I32 = mybir.dt.int32
AF = mybir.ActivationFunctionType
ALU = mybir.AluOpType

FULL = dict(D=4096, FF=14336, SEQ=8192, HD=16, HS=16, MEM=256, XH=4, TT=256)
ROPE_THETA = 500000.0
EPS = 1e-6
NDMA = 16


class Buf:
    __slots__ = ("w", "r")

    def __init__(self):
        self.w = None
        self.r = []


class Ctx:
    def __init__(self, nc, es):
        self.nc = nc
        self.es = es
        self.eng = {"pe": nc.tensor, "act": nc.scalar, "dve": nc.vector, "pool": nc.gpsimd, "sp": nc.sync}
        self.sem = {k: es.enter_context(nc.semaphore("sem_" + k)) for k in ["pe", "act", "dve", "pool"]}
        self.dsem = [es.enter_context(nc.semaphore(f"dsem{i}")) for i in range(NDMA)]
        for i, s in enumerate(self.dsem):
            self.sem[f"d{i}"] = s
        self.bufs = []
        self.reset()

    def reset(self):
        self.cnt = {k: 0 for k in self.sem}
        self.waited = {e: {} for e in self.eng}
        self.dnext = 0
        for b in self.bufs:
            b.w = None
            b.r = []

    def buf(self):
        b = Buf()
        self.bufs.append(b)
        return b

    def bufs_n(self, n):
        return [self.buf() for _ in range(n)]

    def _deps(self, eng, reads, writes):
        deps = {}

        def add(tok, same_ok):
            if tok is None:
                return
            sm, v = tok
            if sm == eng and not same_ok:
                return
            if eng == "pe" and sm == "pe":
                return
            if deps.get(sm, 0) < v:
                deps[sm] = v

        for b in reads:
            add(b.w, True)
        for b in writes:
            add(b.w, False)
            for t in b.r:
                add(t, False)
        w = self.waited[eng]
        for sm, v in deps.items():
            if w.get(sm, 0) < v:
                self.eng[eng].wait_ge(self.sem[sm], v)
                w[sm] = v

    def _commit(self, tok, reads, writes):
        for b in reads:
            b.r.append(tok)
            if len(b.r) > 12:
                b.r = b.r[-12:] if False else self._compact(b.r)
        for b in writes:
            b.w = tok
            b.r = []

    @staticmethod
    def _compact(toks):
        best = {}
        for sm, v in toks:
            if best.get(sm, 0) < v:
                best[sm] = v
        return list(best.items())

    def op(self, eng, fn, reads=(), writes=()):
        self._deps(eng, reads, writes)
        ins = fn()
        self.cnt[eng] += 1
        ins.then_inc(self.sem[eng], 1)
        self._commit((eng, self.cnt[eng]), reads, writes)

    def pe_group(self, fns, reads=(), writes=()):
        self._deps("pe", reads, writes)
        ins = None
        for fn in fns:
            ins = fn()
        self.cnt["pe"] += 1
        ins.then_inc(self.sem["pe"], 1)
        self._commit(("pe", self.cnt["pe"]), reads, writes)

    def dma(self, out, in_, reads=(), writes=(), q="sp"):
        k = self.dnext
        self.dnext = (self.dnext + 1) % NDMA
        name = f"d{k}"
        prev = self.cnt[name]
        w = self.waited[q]
        if prev and w.get(name, 0) < prev:
            self.eng[q].wait_ge(self.sem[name], prev)
            w[name] = prev
        self._deps(q, reads, writes)
        self.eng[q].dma_start(out=out, in_=in_).then_inc(self.sem[name], 16)
        self.cnt[name] += 16
        self._commit((name, self.cnt[name]), reads, writes)

    def end_iter(self):
        w = self.waited["sp"]
        for i in range(NDMA):
            name = f"d{i}"
            if self.cnt[name] and w.get(name, 0) < self.cnt[name]:
                self.nc.sync.wait_ge(self.sem[name], self.cnt[name])
        for e in ["pe", "act", "dve", "pool"]:
            if self.cnt[e]:
                self.nc.sync.wait_ge(self.sem[e], self.cnt[e])
        self.nc.all_engine_barrier()
        for s in self.sem.values():
            self.nc.sync.sem_clear(s)
        self.nc.all_engine_barrier()
        self.reset()


def tile_w(W, ktp):
    K, N = W.shape
    kb = K // (128 * ktp)
    a = W.reshape(kb, ktp, 128, N // 128, 128)
    a = a.transpose(3, 0, 2, 1, 4)
    return np.ascontiguousarray(a).reshape(N // 128, kb, 128, ktp * 128)


def gain_t(g):
    g = np.asarray(g).reshape(-1)
    return np.ascontiguousarray(g.reshape(-1, 128).T)


def build(cfg, lambda_init):
    D, FF, SEQ, HD, HS, MEM, XH, TT = (cfg[k] for k in ["D", "FF", "SEQ", "HD", "HS", "MEM", "XH", "TT"])
    DC, FC, NTT, NH = D // 128, FF // 128, SEQ // TT, HD + HS
    TS = TT // 128
    assert MEM == TT
    ktpD = min(16, DC)
    ktpF = min(8, FC // 2)
    ktpH = min(16, NH)
    nc = bass.Bass("TRN2", target_bir_lowering=False)

    def din(name, shape, dt=F32):
        return nc.dram_tensor(name, list(shape), dt, kind="ExternalInput").ap()

    def dscr(name, shape, dt):
        return nc.dram_tensor(name, list(shape), dt, kind="Internal").ap()

    x = din("x", [SEQ, D])
    mem = din("mem", [MEM, D])
    posb = din("posb", [128, SEQ], I32)
    gains = din("gains", [128, 9 * DC])
    hg = din("hg", [128, 2])
    lamv = din("lamv", [128, 4 * 64])
    ropet = din("ropet", [128, 2])
    masks = din("masks", [4 * 128, TT])
    ident_d = din("ident", [128, 128])
    tri_d = din("tri", [128, 128])
    wg1 = din("wg1", [FC, DC // ktpD, 128, ktpD * 128])
    wu1 = din("wu1", [FC, DC // ktpD, 128, ktpD * 128])
    wd1 = din("wd1", [DC, FC // ktpF, 128, ktpF * 128])
    wg2 = din("wg2", [FC, DC // ktpD, 128, ktpD * 128])
    wu2 = din("wu2", [FC, DC // ktpD, 128, ktpD * 128])
    wd2 = din("wd2", [DC, FC // ktpF, 128, ktpF * 128])
    win = din("win", [6 * (NH // 2), DC // ktpD, 128, ktpD * 128])
    wsw = din("wsw", [2 * HD, DC // ktpD, 128, ktpD * 128])
    wout = din("wout", [DC, NH // ktpH, 128, ktpH * 128])
    wq = din("wq", [XH, DC // ktpD, 128, ktpD * 128])
    wkv = din("wkv", [2 * XH, DC // ktpD, 128, ktpD * 128])
    wo = din("wo", [DC, 1, 128, XH * 128])
    out = nc.dram_tensor("out", [SEQ, D], F32, kind="ExternalOutput").ap()

    h1T = dscr("h1T", [NTT * 128, DC * TT], F32)
    QT = dscr("QT", [NH * 128, SEQ], BF16)
    KT = dscr("KT", [NH * 128, SEQ], BF16)
    Vd = dscr("Vd", [NH * 128, SEQ], BF16)
    MT = dscr("MT", [NH * 128, SEQ], BF16)

    es = contextlib.ExitStack()
    with es:
        cx = Ctx(nc, es)

        def sb(name, shape, dt):
            return es.enter_context(nc.sbuf_tensor(name, list(shape), dt))

        def psum(name, dt=F32, n=512):
            return es.enter_context(nc.psum_tensor(name, [128, n], dt))

        hT = sb("hT", [128, DC, TT], F32); hT_b = cx.bufs_n(DC)
        yT = sb("yT", [128, DC, TT], F32); yT_b = cx.bufs_n(DC)
        hnT = sb("hnT", [128, DC, TT], BF16); hnT_b = cx.bufs_n(DC)
        NACT = max(FC // 2, NH, 3 * (SEQ // TT))
        actT = sb("actT", [128, NACT, TT], BF16); actT_b = cx.bufs_n(NACT)
        xin = yT[:].rearrange("p c t -> p (c t)")[:, 0:D]
        NSTG, NWB = 2, 4
        stg = [sb(f"stg{i}", [128, 16 * 128], F32) for i in range(NSTG)]; stg_b = cx.bufs_n(NSTG)
        wbf = [sb(f"wbf{i}", [128, 16 * 128], BF16) for i in range(NWB)]; wbf_b = cx.bufs_n(NWB)
        NTMP = 6
        tmp = [sb(f"tmp{i}", [128, TT], F32) for i in range(NTMP)]; tmp_b = cx.bufs_n(NTMP)
        tmpb = [sb(f"tmpb{i}", [128, TT], BF16) for i in range(NTMP)]; tmpb_b = cx.bufs_n(NTMP)
        rstd = sb("rstd", [128, TT], F32); rstd_b = cx.buf()
        lnv = sb("lnv", [128, TT], F32); lnv_b = cx.buf()
        gains_s = sb("gains_s", [128, 9 * DC], F32); c_b = cx.buf()
        hg_s = sb("hg_s", [128, 2], F32)
        hg2_s = sb("hg2_s", [128, 2], F32)
        lam_s = sb("lam_s", [128, 256], F32)
        lam_t = sb("lam_t", [128, 8], F32)
        rope_s = sb("rope_s", [128, 2], F32)
        masks_s = sb("masks_s", [128, 4, TT], F32)
        masks_bf = sb("masks_bf", [128, 4, TT], BF16)
        ident = sb("ident_s", [128, 128], F32)
        identb = sb("identb", [128, 128], BF16)
        tri32 = sb("tri32", [128, 128], F32)
        trib = sb("trib", [128, 128], BF16)
        ones32 = sb("ones32", [128, 128], F32)
        onesb = sb("onesb", [128, 128], BF16)
        cst = sb("cst", [128, 4], F32)
        Ct = sb("Ct", [128, TT], F32); St = sb("St", [128, TT], F32); cs_b = cx.buf()
        posi = sb("posi", [128, TT], I32); posf = sb("posf", [128, TT], F32); pos_b = cx.buf()
        ki = sb("ki", [128, TT], I32); ki_b = cx.buf()
        KmT = sb("KmT", [128, XH, MEM], BF16); Vm = sb("Vm", [128, MEM // 128, XH * 128], BF16); km_b = cx.buf()
        qx = sb("qx", [128, XH, TT], BF16); qx_b = cx.bufs_n(XH)
        ox = sb("ox", [128, XH, TT], BF16); ox_b = cx.bufs_n(XH)
        vstg = [sb(f"vstg{i}", [128, 128], BF16) for i in range(2)]; vstg_b = cx.bufs_n(2)
        NQ = SEQ // TT
        KTs = actT[:, 0:NQ, :].rearrange("p c t -> p (c t)")
        QTs = actT[:, NQ:2 * NQ, :].rearrange("p c t -> p (c t)")
        Vs = actT[:, 2 * NQ:3 * NQ, :].rearrange("p c t -> p (c t)").rearrange("p (k d) -> p k d", d=128)
        kqv_b = cx.bufs_n(3)
        vst_b = cx.buf()
        ofull = sb("ofull", [128, TT], F32); ofull_b = cx.buf()
        NPS = 7
        ps = [psum(f"ps{i}") for i in range(NPS)]; ps_b = cx.bufs_n(NPS)
        psb = psum("psb", BF16, 1024); psb_b = cx.buf()
        psrr = [0]
        castrr = [0]
        CAST_ENG = ["dve", "act"]

        def next_ps():
            i = psrr[0]
            psrr[0] = (i + 1) % 3
            return i

        ring = {"tmp": 0, "tmpb": 0, "xin": 0, "ostg": 0, "vstg": 0}

        def rr(name, n):
            i = ring[name]
            ring[name] = (i + 1) % n
            return i

        G = lambda k: gains_s[:, k * DC:(k + 1) * DC]
        eps_c, one_c, npi_c = cst[:, 0:1], cst[:, 1:2], cst[:, 2:3]

        wf = {"q": [], "issued": 0, "consumed": 0, "n": 0}
        LOOK = 3

        def plan_lin(W, n, KT_, ktp, kb0=0):
            for kb in range(KT_ // ktp):
                wf["q"].append((W, n, kb0 + kb, ktp))

        def w_issue(upto):
            while wf["issued"] <= min(upto, len(wf["q"]) - 1):
                W, n, kb, ktp = wf["q"][wf["issued"]]
                i = wf["n"] + wf["issued"]
                s, b = i % NSTG, i % NWB
                cx.dma(stg[s][:, :ktp * 128], W[n, kb], writes=[stg_b[s]])
                if i % 2 == 0:
                    cx.op("dve", lambda s=s, b=b, ktp=ktp: nc.vector.tensor_copy(out=wbf[b][:, :ktp * 128], in_=stg[s][:, :ktp * 128]),
                          reads=[stg_b[s]], writes=[wbf_b[b]])
                else:
                    cx.op("act", lambda s=s, b=b, ktp=ktp: nc.scalar.copy(out=wbf[b][:, :ktp * 128], in_=stg[s][:, :ktp * 128]),
                          reads=[stg_b[s]], writes=[wbf_b[b]])
                wf["issued"] += 1

        def w_get(W, n, kb, ktp):
            if wf["consumed"] == len(wf["q"]):
                wf["q"].append((W, n, kb, ktp))
            e = wf["q"][wf["consumed"]]
            assert e[0] is W and e[1:] == (n, kb, ktp), ("weight plan mismatch", e[1:], (n, kb, ktp))
            w_issue(wf["consumed"] + LOOK)
            b = (wf["n"] + wf["consumed"]) % NWB
            wf["consumed"] += 1
            if wf["consumed"] == len(wf["q"]):
                wf["n"] += len(wf["q"]); wf["q"] = []; wf["issued"] = 0; wf["consumed"] = 0
            return b

        def lin_chunk(W, n, inT, in_b, KT_, ktp, pst, pst_b, kb0=0):
            nkb = KT_ // ktp
            for kb in range(nkb):
                b = w_get(W, n, kb0 + kb, ktp)
                fns = []
                for j in range(ktp):
                    k = kb * ktp + j
                    fns.append(lambda b=b, j=j, k=k: nc.tensor.matmul(
                        pst[:, :TT], wbf[b][:, j * 128:(j + 1) * 128], inT[:, k, :],
                        start=(k == 0), stop=(k == KT_ - 1)))
                cx.pe_group(fns, reads=[wbf_b[b]] + [in_b[kb * ktp + j] for j in range(ktp)], writes=[pst_b])

        def rms_stats(srcT, src_b, nch, div):
            a = 4
            for c in range(nch):
                t = rr("tmp", NTMP)
                cx.op("act", lambda c=c, t=t: nc.scalar.activation(out=tmp[t][:], in_=srcT[:, c, :], func=AF.Square),
                      reads=[src_b[c]], writes=[tmp_b[t]])
                cx.op("pe", lambda c=c, t=t: nc.tensor.matmul(ps[a][:, :TT], ones32[:], tmp[t][:],
                                                              start=(c == 0), stop=(c == nch - 1)),
                      reads=[tmp_b[t]], writes=[ps_b[a]])
            cx.op("act", lambda: nc.scalar.activation(out=lnv[:], in_=ps[a][:, :TT], func=AF.Ln, scale=1.0 / div, bias=eps_c),
                  reads=[ps_b[a]], writes=[lnv_b])
            cx.op("act", lambda: nc.scalar.activation(out=rstd[:], in_=lnv[:], func=AF.Exp, scale=-0.5),
                  reads=[lnv_b], writes=[rstd_b])

        def norm_to(srcT, src_b, gain, dstT, dst_b):
            rms_stats(srcT, src_b, DC, float(D))
            for c in range(DC):
                cx.op("dve", lambda c=c: nc.vector.scalar_tensor_tensor(
                    out=dstT[:, c, :], in0=srcT[:, c, :], scalar=gain[:, c:c + 1], in1=rstd[:],
                    op0=ALU.mult, op1=ALU.mult), reads=[src_b[c], rstd_b], writes=[dst_b[c]])

        def post_res(gain, factor):
            rms_stats(yT, yT_b, DC, float(D))
            for c in range(DC):
                t = rr("tmp", NTMP)
                cx.op("dve", lambda c=c, t=t: nc.vector.scalar_tensor_tensor(
                    out=tmp[t][:], in0=yT[:, c, :], scalar=gain[:, c:c + 1], in1=rstd[:],
                    op0=ALU.mult, op1=ALU.mult), reads=[yT_b[c], rstd_b], writes=[tmp_b[t]])
                cx.op("dve", lambda c=c, t=t: nc.vector.scalar_tensor_tensor(
                    out=hT[:, c, :], in0=tmp[t][:], scalar=float(factor), in1=hT[:, c, :],
                    op0=ALU.mult, op1=ALU.add), reads=[tmp_b[t], hT_b[c]], writes=[hT_b[c]])

        def load_T(src, row0, dstT, dst_b, ntok_tiles):
            for ts in range(ntok_tiles):
                cx.dma(xin, src[bass.ds(row0 + ts * 128, 128), :], writes=yT_b)
                for cg in range(max(1, DC // 4)):
                    ncg = min(4, DC)
                    p = next_ps()
                    cx.pe_group([lambda q=q, p=p, cg=cg: nc.tensor.transpose(
                        ps[p][:, q * 128:(q + 1) * 128], xin[:, (cg * 4 + q) * 128:(cg * 4 + q + 1) * 128], ident[:]) for q in range(ncg)],
                        reads=yT_b, writes=[ps_b[p]])
                    for q in range(ncg):
                        c = cg * 4 + q
                        cx.op("act", lambda q=q, c=c, p=p, ts=ts: nc.scalar.copy(
                            out=dstT[:, c, ts * 128:(ts + 1) * 128], in_=ps[p][:, q * 128:(q + 1) * 128]),
                            reads=[ps_b[p]], writes=[dst_b[c]])

        def ffn(Wg, Wu, Wd, gpre, gpost):
            FH = FC // 2
            for half in range(2):
                for n in range(FH):
                    plan_lin(Wg, half * FH + n, DC, ktpD)
                    plan_lin(Wu, half * FH + n, DC, ktpD)
                for c in range(DC):
                    plan_lin(Wd, c, FH, ktpF, kb0=half * (FH // ktpF))
            norm_to(hT, hT_b, gpre, hnT, hnT_b)
            for half in range(2):
                for n in range(FH):
                    pg, pu = next_ps(), next_ps()
                    lin_chunk(Wg, half * FH + n, hnT, hnT_b, DC, ktpD, ps[pg], ps_b[pg])
                    lin_chunk(Wu, half * FH + n, hnT, hnT_b, DC, ktpD, ps[pu], ps_b[pu])
                    t = rr("tmp", NTMP)
                    cx.op("act", lambda pg=pg, t=t: nc.scalar.activation(out=tmp[t][:], in_=ps[pg][:, :TT], func=AF.Silu),
                          reads=[ps_b[pg]], writes=[tmp_b[t]])
                    cx.op("dve", lambda pu=pu, t=t, n=n: nc.vector.tensor_tensor(
                        out=actT[:, n, :], in0=tmp[t][:], in1=ps[pu][:, :TT], op=ALU.mult),
                        reads=[tmp_b[t], ps_b[pu]], writes=[actT_b[n]])
                for c in range(DC):
                    p = next_ps()
                    lin_chunk(Wd, c, actT, actT_b, FH, ktpF, ps[p], ps_b[p], kb0=half * (FH // ktpF))
                    if half == 0:
                        cx.op("act", lambda c=c, p=p: nc.scalar.copy(out=yT[:, c, :], in_=ps[p][:, :TT]),
                              reads=[ps_b[p]], writes=[yT_b[c]])
                    else:
                        cx.op("dve", lambda c=c, p=p: nc.vector.tensor_tensor(out=yT[:, c, :], in0=ps[p][:, :TT], in1=yT[:, c, :], op=ALU.add),
                              reads=[ps_b[p], yT_b[c]], writes=[yT_b[c]])
            post_res(gpost, 0.5)

        def to_tokmajor_store(srcb_ap, src_buf, dram_rows_fn):
            for ts in range(TS):
                cx.op("pe", lambda ts=ts: nc.tensor.transpose(psb[:, ts * 128:(ts + 1) * 128],
                                                              srcb_ap[:, ts * 128:(ts + 1) * 128], identb[:]),
                      reads=[src_buf], writes=[psb_b])
                v = rr("vstg", 2)
                cx.op("act", lambda ts=ts, v=v: nc.scalar.copy(out=vstg[v][:], in_=psb[:, ts * 128:(ts + 1) * 128]),
                      reads=[psb_b], writes=[vstg_b[v]])
                cx.dma(dram_rows_fn(ts), vstg[v][:], reads=[vstg_b[v]])

        cx.dma(gains_s[:], gains[:, :], writes=[c_b])
        cx.dma(hg_s[:], hg[:, :], writes=[c_b])
        cx.dma(lam_s[:], lamv[:, :], writes=[c_b])
        cx.dma(rope_s[:], ropet[:, :], writes=[c_b])
        cx.dma(masks_s[:], masks.rearrange("(m p) t -> p m t", p=128), writes=[c_b])
        cx.dma(ident[:], ident_d[:, :], writes=[c_b])
        cx.dma(tri32[:], tri_d[:, :], writes=[c_b])
        V = nc.vector
        cx.op("dve", lambda: V.memset(ones32[:], 1.0), writes=[c_b])
        cx.op("dve", lambda: V.memset(onesb[:], 1.0), writes=[c_b])
        cx.op("dve", lambda: V.memset(cst[:, 0:1], EPS), writes=[c_b])
        cx.op("dve", lambda: V.memset(cst[:, 1:2], 1.0), writes=[c_b])
        cx.op("dve", lambda: V.memset(cst[:, 2:3], -math.pi), writes=[c_b])
        cx.op("dve", lambda: V.memset(cst[:, 3:4], 0.0), writes=[c_b])
        cx.op("dve", lambda: V.tensor_copy(out=identb[:], in_=ident[:]), reads=[c_b], writes=[c_b])
        cx.op("dve", lambda: V.tensor_copy(out=trib[:], in_=tri32[:]), reads=[c_b], writes=[c_b])
        cx.op("dve", lambda: V.tensor_copy(out=masks_bf[:], in_=masks_s[:]), reads=[c_b], writes=[c_b])
        cx.op("dve", lambda: V.tensor_tensor(out=lam_s[:, 0:64], in0=lam_s[:, 0:64], in1=lam_s[:, 64:128], op=ALU.mult), reads=[c_b], writes=[c_b])
        cx.op("dve", lambda: V.tensor_tensor(out=lam_s[:, 128:192], in0=lam_s[:, 128:192], in1=lam_s[:, 192:256], op=ALU.mult), reads=[c_b], writes=[c_b])
        cx.op("dve", lambda: V.tensor_reduce(out=lam_t[:, 0:1], in_=lam_s[:, 0:64], op=ALU.add, axis=mybir.AxisListType.X), reads=[c_b], writes=[c_b])
        cx.op("dve", lambda: V.tensor_reduce(out=lam_t[:, 1:2], in_=lam_s[:, 128:192], op=ALU.add, axis=mybir.AxisListType.X), reads=[c_b], writes=[c_b])
        cx.op("act", lambda: nc.scalar.activation(out=lam_t[:, 2:4], in_=lam_t[:, 0:2], func=AF.Exp), reads=[c_b], writes=[c_b])
        cx.op("dve", lambda: V.tensor_tensor(out=lam_t[:, 4:5], in0=lam_t[:, 3:4], in1=lam_t[:, 2:3], op=ALU.subtract), reads=[c_b], writes=[c_b])
        cx.op("dve", lambda: V.tensor_scalar(out=lam_t[:, 5:6], in0=lam_t[:, 4:5], scalar1=-float(lambda_init), scalar2=None, op0=ALU.add), reads=[c_b], writes=[c_b])
        neglam = lam_t[:, 5:6]
        cx.op("dve", lambda: V.tensor_scalar(out=hg2_s[:, 0:1], in0=hg_s[:, 0:1], scalar1=float(1.0 - lambda_init), scalar2=None, op0=ALU.mult), reads=[c_b], writes=[c_b])
        cx.op("dve", lambda: V.tensor_copy(out=hg2_s[:, 1:2], in_=hg_s[:, 1:2]), reads=[c_b], writes=[c_b])

        load_T(mem, 0, hT, hT_b, MEM // 128)
        norm_to(hT, hT_b, G(8), hnT, hnT_b)
        for h in range(XH):
            p = next_ps()
            lin_chunk(wkv, h, hnT, hnT_b, DC, ktpD, ps[p], ps_b[p])
            cx.op("act", lambda h=h, p=p: nc.scalar.copy(out=KmT[:, h, :], in_=ps[p][:, :MEM]), reads=[ps_b[p]], writes=[km_b])
        for h in range(XH):
            p = next_ps()
            lin_chunk(wkv, XH + h, hnT, hnT_b, DC, ktpD, ps[p], ps_b[p])
            t = rr("tmpb", NTMP)
            cx.op("act", lambda p=p, t=t: nc.scalar.copy(out=tmpb[t][:], in_=ps[p][:, :MEM]), reads=[ps_b[p]], writes=[tmpb_b[t]])
            for mt in range(MEM // 128):
                cx.op("pe", lambda t=t, mt=mt: nc.tensor.transpose(psb[:, mt * 128:(mt + 1) * 128], tmpb[t][:, mt * 128:(mt + 1) * 128], identb[:]),
                      reads=[tmpb_b[t]], writes=[psb_b])
                cx.op("act", lambda h=h, mt=mt: nc.scalar.copy(out=Vm[:, mt, h * 128:(h + 1) * 128], in_=psb[:, mt * 128:(mt + 1) * 128]),
                      reads=[psb_b], writes=[km_b])
        cx.end_iter()

        with nc.Fori(0, NTT) as i:
            tok0 = i * TT
            load_T(x, tok0, hT, hT_b, TS)
            ffn(wg1, wu1, wd1, G(0), G(1))
            norm_to(hT, hT_b, G(2), hnT, hnT_b)
            cx.dma(h1T[bass.ds(i * 128, 128), :], hT[:].rearrange("p c t -> p (c t)"), reads=hT_b)
            cx.dma(posi[:], posb[:, bass.ds(tok0, TT)], writes=[pos_b])
            cx.op("dve", lambda: V.tensor_copy(out=posf[:], in_=posi[:]), reads=[pos_b], writes=[pos_b])
            for which, dst in ((0, St), (1, Ct)):
                t = rr("tmp", NTMP)
                cx.op("dve", lambda t=t, which=which: V.tensor_scalar(
                    out=tmp[t][:], in0=posf[:], scalar1=rope_s[:, 0:1], scalar2=(0.5 if which == 0 else 0.75),
                    op0=ALU.mult, op1=ALU.add), reads=[pos_b, c_b], writes=[tmp_b[t]])
                cx.op("dve", lambda t=t: V.tensor_copy(out=ki[:], in_=tmp[t][:]), reads=[tmp_b[t]], writes=[ki_b])
                t2 = rr("tmp", NTMP)
                cx.op("dve", lambda t2=t2: V.tensor_copy(out=tmp[t2][:], in_=ki[:]), reads=[ki_b], writes=[tmp_b[t2]])
                cx.op("dve", lambda t=t, t2=t2: V.tensor_tensor(out=tmp[t][:], in0=tmp[t][:], in1=tmp[t2][:], op=ALU.subtract),
                      reads=[tmp_b[t], tmp_b[t2]], writes=[tmp_b[t]])
                cx.op("dve", lambda t=t, t2=t2: V.scalar_tensor_tensor(out=tmp[t2][:], in0=tmp[t][:], scalar=0.0, in1=tmp[t][:],
                                                                      op0=ALU.is_lt, op1=ALU.add),
                      reads=[tmp_b[t]], writes=[tmp_b[t2]])
                cx.op("act", lambda t2=t2, dst=dst: nc.scalar.activation(out=dst[:], in_=tmp[t2][:], func=AF.Sin,
                                                                        scale=2.0 * math.pi, bias=npi_c),
                      reads=[tmp_b[t2], c_b], writes=[cs_b])
            cx.op("dve", lambda: V.tensor_scalar(out=St[:], in0=St[:], scalar1=rope_s[:, 1:2], scalar2=None, op0=ALU.mult),
                  reads=[cs_b, c_b], writes=[cs_b])
            for grp in range(6):
                for hh in range(HD if grp < 3 else HS):
                    plan_lin(win, (grp * HD + hh) if grp < 3 else (3 * HD + (grp - 3) * HS + hh), DC, ktpD)
                    if grp in (0, 1):
                        plan_lin(wsw, grp * HD + hh, DC, ktpD)
            Vst = actT[:, 2 * NH:3 * NH, :].rearrange("p c t -> p (c t)").rearrange("p (s n d) -> p s n d", s=TS, n=NH, d=128)
            for grp in range(6):
                nh = HD if grp < 3 else HS
                for hh in range(nh):
                    n = (grp * HD + hh) if grp < 3 else (3 * HD + (grp - 3) * HS + hh)
                    head = hh if grp < 3 else HD + hh
                    p = next_ps()
                    lin_chunk(win, n, hnT, hnT_b, DC, ktpD, ps[p], ps_b[p])
                    if grp in (0, 1):
                        slot = head if grp == 0 else NH + head
                        p2 = next_ps()
                        lin_chunk(wsw, grp * HD + hh, hnT, hnT_b, DC, ktpD, ps[p2], ps_b[p2])
                        t = rr("tmp", NTMP); t2 = rr("tmp", NTMP)
                        cx.op("dve", lambda t=t, p=p: V.tensor_tensor(out=tmp[t][:], in0=ps[p][:, :TT], in1=Ct[:], op=ALU.mult),
                              reads=[ps_b[p], cs_b], writes=[tmp_b[t]])
                        cx.op("dve", lambda t2=t2, p2=p2: V.tensor_tensor(out=tmp[t2][:], in0=ps[p2][:, :TT], in1=St[:], op=ALU.mult),
                              reads=[ps_b[p2], cs_b], writes=[tmp_b[t2]])
                        cx.op("dve", lambda t=t, t2=t2, slot=slot: V.tensor_tensor(out=actT[:, slot, :], in0=tmp[t][:], in1=tmp[t2][:], op=ALU.add),
                              reads=[tmp_b[t], tmp_b[t2]], writes=[actT_b[slot]])
                    elif grp in (3, 4):
                        slot = head if grp == 3 else NH + head
                        cx.op("act", lambda p=p, slot=slot: nc.scalar.copy(out=actT[:, slot, :], in_=ps[p][:, :TT]),
                              reads=[ps_b[p]], writes=[actT_b[slot]])
                    else:
                        tb = rr("tmpb", NTMP)
                        cx.op("act", lambda p=p, tb=tb: nc.scalar.copy(out=tmpb[tb][:], in_=ps[p][:, :TT]),
                              reads=[ps_b[p]], writes=[tmpb_b[tb]])
                        for ts in range(TS):
                            cx.op("pe", lambda ts=ts, tb=tb: nc.tensor.transpose(psb[:, ts * 128:(ts + 1) * 128],
                                                                              tmpb[tb][:, ts * 128:(ts + 1) * 128], identb[:]),
                                  reads=[tmpb_b[tb]], writes=[psb_b])
                            cx.op("act", lambda ts=ts, head=head: nc.scalar.copy(out=actT[:, 2 * NH + head, ts * 128:(ts + 1) * 128],
                                                                               in_=psb[:, ts * 128:(ts + 1) * 128]),
                                  reads=[psb_b], writes=[actT_b[2 * NH + head]])
            HB = min(16, NH)
            for n0 in range(0, NH, HB):
                for which, (dst, qq) in enumerate(((QT, "sp"), (KT, "act"), (Vd, "act"))):
                    cx.dma(dst[n0 * 128:(n0 + HB) * 128, bass.ds(tok0, TT)].rearrange("(n p) t -> p n t", p=128),
                           actT[:, which * NH + n0:which * NH + n0 + HB, :],
                           reads=actT_b[which * NH + n0:which * NH + n0 + HB], q=qq)
            cx.end_iter()

        NQG = SEQ // TT
        assert NQG <= DC
        sc_d = 64 ** -0.5
        sc_s = 128 ** -0.5

        def load_head(hidx):
            cx.dma(KTs, KT[bass.ds(hidx * 128, 128), :], writes=[kqv_b[0]])
            cx.dma(QTs, QT[bass.ds(hidx * 128, 128), :], writes=[kqv_b[1]])
            cx.dma(actT[:, 2 * NQ:3 * NQ, :].rearrange("p c t -> p (c t)"), Vd[bass.ds(hidx * 128, 128), :], writes=[kqv_b[2]])

        def head_norm_store(hidx, g, gcol):
            a = 3
            t = rr("tmp", NTMP)
            cx.op("act", lambda t=t: nc.scalar.activation(out=tmp[t][:], in_=ofull[:], func=AF.Square),
                  reads=[ofull_b], writes=[tmp_b[t]])
            cx.op("pe", lambda t=t: nc.tensor.matmul(ps[a][:, :TT], ones32[:], tmp[t][:], start=True, stop=True),
                  reads=[tmp_b[t]], writes=[ps_b[a]])
            cx.op("act", lambda: nc.scalar.activation(out=lnv[:], in_=ps[a][:, :TT], func=AF.Ln, scale=1.0 / 128, bias=eps_c),
                  reads=[ps_b[a]], writes=[lnv_b])
            cx.op("act", lambda: nc.scalar.activation(out=rstd[:], in_=lnv[:], func=AF.Exp, scale=-0.5),
                  reads=[lnv_b], writes=[rstd_b])
            cx.op("dve", lambda: V.scalar_tensor_tensor(out=hnT[:, g, :], in0=ofull[:], scalar=hg2_s[:, gcol:gcol + 1],
                                                       in1=rstd[:], op0=ALU.mult, op1=ALU.mult),
                  reads=[ofull_b, rstd_b, c_b], writes=[hnT_b[g]])
            if g == NQG - 1:
                cx.dma(MT[bass.ds(hidx * 128, 128), :], hnT[:, 0:NQG, :].rearrange("p c t -> p (c t)"), reads=hnT_b[0:NQG])

        def skew(tasks, stageA, stageB, lag):
            n = len(tasks)
            st = {}
            for i in range(n + lag):
                if i < n:
                    st[i] = stageA(tasks[i])
                j = i - lag
                if j >= 0:
                    stageB(tasks[j], st.pop(j))

        with nc.Fori(0, HD) as hd:
            load_head(hd)
            tasks = [(g, kt, comp) for g in range(NQG) for kt in range((g + 1) * TS) for comp in range(2)]

            def dA(task):
                g, kt, comp = task
                qs = slice(g * TT, (g + 1) * TT); ks = slice(kt * 128, (kt + 1) * 128)
                dg = kt - g * TS
                p = next_ps()
                lo = comp * 64
                cx.op("pe", lambda: nc.tensor.matmul(ps[p][:, :TT], KTs[lo:lo + 64, ks], QTs[lo:lo + 64, qs], start=True, stop=True),
                      reads=[kqv_b[0], kqv_b[1]], writes=[ps_b[p]])
                tb = rr("tmpb", NTMP)
                cx.op("act", lambda: nc.scalar.activation(out=tmpb[tb][:], in_=ps[p][:, :TT], func=AF.Exp, scale=sc_d),
                      reads=[ps_b[p]], writes=[tmpb_b[tb]])
                if dg >= 0:
                    cx.op("dve", lambda: V.tensor_tensor(out=tmpb[tb][:], in0=tmpb[tb][:], in1=masks_bf[:, dg, :], op=ALU.mult),
                          reads=[tmpb_b[tb], c_b], writes=[tmpb_b[tb]])
                return tb

            def dB(task, tb):
                g, kt, comp = task
                nkt = (g + 1) * TS
                po, pl = (3, 4) if comp == 0 else (5, 6)
                cx.op("pe", lambda: nc.tensor.matmul(ps[po][:, :TT], Vs[:, kt, :], tmpb[tb][:], start=(kt == 0), stop=(kt == nkt - 1)),
                      reads=[tmpb_b[tb], kqv_b[2]], writes=[ps_b[po]])
                cx.op("pe", lambda: nc.tensor.matmul(ps[pl][:, :TT], onesb[:], tmpb[tb][:], start=(kt == 0), stop=(kt == nkt - 1)),
                      reads=[tmpb_b[tb]], writes=[ps_b[pl]])
                if kt == nkt - 1 and comp == 1:
                    t1 = rr("tmp", NTMP); t2 = rr("tmp", NTMP)
                    for (po_, pl_, t) in ((3, 4, t1), (5, 6, t2)):
                        cx.op("dve", lambda: V.reciprocal(out=tmp[t][:], in_=ps[pl_][:, :TT]), reads=[ps_b[pl_]], writes=[tmp_b[t]])
                        cx.op("dve", lambda: V.tensor_tensor(out=tmp[t][:], in0=ps[po_][:, :TT], in1=tmp[t][:], op=ALU.mult),
                              reads=[ps_b[po_], tmp_b[t]], writes=[tmp_b[t]])
                    cx.op("dve", lambda: V.scalar_tensor_tensor(out=ofull[:], in0=tmp[t2][:], scalar=neglam, in1=tmp[t1][:],
                                                               op0=ALU.mult, op1=ALU.add),
                          reads=[tmp_b[t1], tmp_b[t2], c_b], writes=[ofull_b])
                    head_norm_store(hd, g, 0)

            skew(tasks, dA, dB, 2)
            cx.end_iter()

        with nc.Fori(0, HS) as hs:
            hidx = hs + HD
            load_head(hidx)
            PO, PC = 4, 5
            tasks = [(g, kt) for g in range(NQG) for kt in range((g + 1) * TS - 1, -1, -1)]

            def sA(task):
                g, kt = task
                qs = slice(g * TT, (g + 1) * TT); ks = slice(kt * 128, (kt + 1) * 128)
                dg = kt - g * TS
                p = next_ps()
                cx.op("pe", lambda: nc.tensor.matmul(ps[p][:, :TT], KTs[:, ks], QTs[:, qs], start=True, stop=True),
                      reads=[kqv_b[0], kqv_b[1]], writes=[ps_b[p]])
                te = rr("tmp", NTMP); tl = rr("tmp", NTMP)
                cx.op("act", lambda: nc.scalar.activation(out=tmp[te][:], in_=ps[p][:, :TT], func=AF.Exp, scale=-sc_s),
                      reads=[ps_b[p]], writes=[tmp_b[te]])
                cx.op("act", lambda: nc.scalar.activation(out=tmp[tl][:], in_=tmp[te][:], func=AF.Ln, scale=1.0, bias=one_c),
                      reads=[tmp_b[te], c_b], writes=[tmp_b[tl]])
                tb = rr("tmpb", NTMP)
                if dg >= 0:
                    cx.op("dve", lambda: V.scalar_tensor_tensor(out=tmp[te][:], in0=ps[p][:, :TT], scalar=-sc_s, in1=tmp[tl][:],
                                                               op0=ALU.mult, op1=ALU.subtract),
                          reads=[ps_b[p], tmp_b[tl]], writes=[tmp_b[te]])
                    cx.op("dve", lambda: V.tensor_tensor(out=tmpb[tb][:], in0=tmp[te][:], in1=masks_s[:, 2 + dg, :], op=ALU.mult),
                          reads=[tmp_b[te], c_b], writes=[tmpb_b[tb]])
                else:
                    cx.op("dve", lambda: V.scalar_tensor_tensor(out=tmpb[tb][:], in0=ps[p][:, :TT], scalar=-sc_s, in1=tmp[tl][:],
                                                               op0=ALU.mult, op1=ALU.subtract),
                          reads=[ps_b[p], tmp_b[tl]], writes=[tmpb_b[tb]])
                return (tl, tb)

            def sB(task, st):
                g, kt = task
                tl, tb = st
                nkt = (g + 1) * TS
                dg = kt - g * TS
                first, last = (kt == nkt - 1), (kt == 0)
                p2 = next_ps()
                cx.op("pe", lambda: nc.tensor.matmul(ps[p2][:, :TT], trib[:], tmpb[tb][:], start=True, stop=True),
                      reads=[tmpb_b[tb], c_b], writes=[ps_b[p2]])
                cx.op("dve", lambda: V.tensor_tensor(out=tmp[tl][:], in0=ps[p2][:, :TT], in1=tmp[tl][:], op=ALU.subtract),
                      reads=[ps_b[p2], tmp_b[tl]], writes=[tmp_b[tl]])
                if not first:
                    cx.op("dve", lambda: V.tensor_tensor(out=tmp[tl][:], in0=ps[PC][:, :TT], in1=tmp[tl][:], op=ALU.add),
                          reads=[ps_b[PC], tmp_b[tl]], writes=[tmp_b[tl]])
                ta = rr("tmpb", NTMP)
                cx.op("act", lambda: nc.scalar.activation(out=tmpb[ta][:], in_=tmp[tl][:], func=AF.Exp),
                      reads=[tmp_b[tl]], writes=[tmpb_b[ta]])
                if dg >= 0:
                    cx.op("dve", lambda: V.tensor_tensor(out=tmpb[ta][:], in0=tmpb[ta][:], in1=masks_bf[:, 2 + dg, :], op=ALU.mult),
                          reads=[tmpb_b[ta], c_b], writes=[tmpb_b[ta]])
                cx.op("pe", lambda: nc.tensor.matmul(ps[PO][:, :TT], Vs[:, kt, :], tmpb[ta][:], start=first, stop=last),
                      reads=[tmpb_b[ta], kqv_b[2]], writes=[ps_b[PO]])
                if not last:
                    cx.op("pe", lambda: nc.tensor.matmul(ps[PC][:, :TT], onesb[:], tmpb[tb][:], start=first, stop=(kt == 1)),
                          reads=[tmpb_b[tb]], writes=[ps_b[PC]])
                else:
                    cx.op("act", lambda: nc.scalar.copy(out=ofull[:], in_=ps[PO][:, :TT]), reads=[ps_b[PO]], writes=[ofull_b])
                    head_norm_store(hidx, g, 1)

            skew(tasks, sA, sB, 1)
            cx.end_iter()

        sc_x = 128 ** -0.5
        with nc.Fori(0, NTT) as i:
            tok0 = i * TT
            cx.dma(hT[:].rearrange("p c t -> p (c t)"), h1T[bass.ds(i * 128, 128), :], writes=hT_b)
            for n0 in range(0, NH, 16):
                n1 = min(NH, n0 + 16)
                cx.dma(actT[:, n0:n1, :], MT[n0 * 128:n1 * 128, bass.ds(tok0, TT)].rearrange("(n p) t -> p n t", p=128),
                       writes=actT_b[n0:n1])
            for c in range(DC):
                plan_lin(wout, c, NH, ktpH)
            for c in range(DC):
                p = next_ps()
                lin_chunk(wout, c, actT, actT_b, NH, ktpH, ps[p], ps_b[p])
                cx.op("act", lambda c=c, p=p: nc.scalar.copy(out=yT[:, c, :], in_=ps[p][:, :TT]), reads=[ps_b[p]], writes=[yT_b[c]])
            post_res(G(3), 1.0)
            norm_to(hT, hT_b, G(4), hnT, hnT_b)
            for h in range(XH):
                plan_lin(wq, h, DC, ktpD)
            for h in range(XH):
                p = next_ps()
                lin_chunk(wq, h, hnT, hnT_b, DC, ktpD, ps[p], ps_b[p])
                cx.op("act", lambda h=h, p=p: nc.scalar.copy(out=qx[:, h, :], in_=ps[p][:, :TT]), reads=[ps_b[p]], writes=[qx_b[h]])
            for h in range(XH):
                PO, PL = 4, 5
                nmt = MEM // 128
                for mt in range(nmt):
                    p = next_ps()
                    cx.op("pe", lambda p=p, h=h, mt=mt: nc.tensor.matmul(ps[p][:, :TT], KmT[:, h, mt * 128:(mt + 1) * 128], qx[:, h, :], start=True, stop=True),
                          reads=[km_b, qx_b[h]], writes=[ps_b[p]])
                    tb = rr("tmpb", NTMP)
                    cx.op("act", lambda p=p, tb=tb: nc.scalar.activation(out=tmpb[tb][:], in_=ps[p][:, :TT], func=AF.Exp, scale=sc_x),
                          reads=[ps_b[p]], writes=[tmpb_b[tb]])
                    cx.op("pe", lambda tb=tb, h=h, mt=mt: nc.tensor.matmul(ps[PO][:, :TT], Vm[:, mt, h * 128:(h + 1) * 128], tmpb[tb][:],
                                                                         start=(mt == 0), stop=(mt == nmt - 1)),
                          reads=[tmpb_b[tb], km_b], writes=[ps_b[PO]])
                    cx.op("pe", lambda tb=tb, mt=mt: nc.tensor.matmul(ps[PL][:, :TT], onesb[:], tmpb[tb][:], start=(mt == 0), stop=(mt == nmt - 1)),
                          reads=[tmpb_b[tb]], writes=[ps_b[PL]])
                t = rr("tmp", NTMP)
                cx.op("dve", lambda t=t: V.reciprocal(out=tmp[t][:], in_=ps[PL][:, :TT]), reads=[ps_b[PL]], writes=[tmp_b[t]])
                cx.op("dve", lambda t=t, h=h: V.tensor_tensor(out=ox[:, h, :], in0=ps[PO][:, :TT], in1=tmp[t][:], op=ALU.mult),
                      reads=[ps_b[PO], tmp_b[t]], writes=[ox_b[h]])
            for c in range(DC):
                plan_lin(wo, c, XH, XH)
            for c in range(DC):
                p = next_ps()
                lin_chunk(wo, c, ox, ox_b, XH, XH, ps[p], ps_b[p])
                cx.op("act", lambda c=c, p=p: nc.scalar.copy(out=yT[:, c, :], in_=ps[p][:, :TT]), reads=[ps_b[p]], writes=[yT_b[c]])
            post_res(G(5), 1.0)
            ffn(wg2, wu2, wd2, G(6), G(7))
            for ts in range(TS):
                for cg in range(max(1, DC // 4)):
                    ncg = min(4, DC)
                    p = next_ps()
                    cx.pe_group([lambda q=q, cg=cg, ts=ts, p=p: nc.tensor.transpose(
                        ps[p][:, q * 128:(q + 1) * 128], hT[:, cg * 4 + q, ts * 128:(ts + 1) * 128], ident[:]) for q in range(ncg)],
                        reads=[hT_b[cg * 4 + q] for q in range(ncg)], writes=[ps_b[p]])
                    cx.op("act", lambda cg=cg, p=p, ncg=ncg: nc.scalar.copy(out=xin[:, cg * 512:cg * 512 + ncg * 128], in_=ps[p][:, :ncg * 128]),
                          reads=[ps_b[p]], writes=yT_b)
                cx.dma(out[bass.ds(tok0 + ts * 128, 128), :], xin, reads=yT_b)
            cx.end_iter()
    return nc


def prep_inputs(cfg, inp, lambda_init):
    D, FF, SEQ, HD, HS, MEM, XH, TT = (cfg[k] for k in ["D", "FF", "SEQ", "HD", "HS", "MEM", "XH", "TT"])
    DC, FC, NH = D // 128, FF // 128, HD + HS
    ktpD, ktpF, ktpH = min(16, DC), min(8, FC // 2), min(16, NH)
    f = lambda a: np.ascontiguousarray(np.asarray(a, dtype=np.float32))
    m = {}
    m["x"] = f(inp["x"]).reshape(SEQ, D)
    m["mem"] = f(inp["mem"]).reshape(MEM, D)
    m["posb"] = np.ascontiguousarray(np.broadcast_to(np.asarray(inp["positions"]).reshape(1, SEQ).astype(np.int32), (128, SEQ)))
    gl = ["ffn1_norm_pre", "ffn1_norm_post", "mix_norm_pre", "mix_norm_post", "xattn_norm_pre", "xattn_norm_post",
          "ffn2_norm_pre", "ffn2_norm_post", "mem_norm"]
    m["gains"] = np.ascontiguousarray(np.concatenate([gain_t(f(inp[k])[0]) for k in gl], axis=1))
    m["hg"] = np.ascontiguousarray(np.stack([f(inp["diff_subln"])[0], f(inp["sb_norm"])[0]], axis=1))
    lv = np.concatenate([f(inp[k])[0] for k in ["lambda_q1", "lambda_k1", "lambda_q2", "lambda_k2"]])
    m["lamv"] = np.ascontiguousarray(np.broadcast_to(lv[None, :], (128, 256)))
    rt = np.zeros((128, 2), np.float32)
    inv = ROPE_THETA ** (-np.arange(0, 16, 2, dtype=np.float32) / 16)
    for base in (0, 64):
        rt[base:base + 8, 0] = inv / np.float32(2 * math.pi)
        rt[base + 8:base + 16, 0] = inv / np.float32(2 * math.pi)
        rt[base:base + 8, 1] = -1.0
        rt[base + 8:base + 16, 1] = 1.0
    m["ropet"] = rt
    kp = np.arange(128)[:, None]
    qf = np.arange(TT)[None, :]
    mk = [((kp + 128 * j) <= qf) for j in range(TT // 128)] + [((kp + 128 * j) < qf) for j in range(TT // 128)]
    m["masks"] = np.ascontiguousarray(np.concatenate(mk, axis=0).astype(np.float32))
    m["ident"] = np.eye(128, dtype=np.float32)
    m["tri"] = (np.arange(128)[:, None] > np.arange(128)[None, :]).astype(np.float32)
    m["wg1"] = tile_w(f(inp["ffn1_w_gate"])[0], ktpD); m["wu1"] = tile_w(f(inp["ffn1_w_up"])[0], ktpD)
    m["wd1"] = tile_w(f(inp["ffn1_w_down"])[0], ktpF)
    m["wg2"] = tile_w(f(inp["ffn2_w_gate"])[0], ktpD); m["wu2"] = tile_w(f(inp["ffn2_w_up"])[0], ktpD)
    m["wd2"] = tile_w(f(inp["ffn2_w_down"])[0], ktpF)
    w_in = f(inp["w_in"])[0]
    m["win"] = tile_w(w_in, ktpD)
    perm = np.arange(128)
    for base in (0, 64):
        perm[base:base + 8] = np.arange(base + 8, base + 16)
        perm[base + 8:base + 16] = np.arange(base, base + 8)
    cols = np.concatenate([h * 128 + perm for h in range(2 * HD)])
    m["wsw"] = tile_w(np.ascontiguousarray(w_in[:, cols]), ktpD)
    m["wout"] = tile_w(f(inp["w_out"])[0], ktpH)
    m["wq"] = tile_w(f(inp["xattn_w_q"])[0], ktpD)
    m["wkv"] = tile_w(f(inp["xattn_w_kv"])[0], ktpD)
    m["wo"] = tile_w(f(inp["xattn_w_o"])[0], XH)
    return m


def run(cfg, inp, lambda_init):
    nc = build(cfg, lambda_init)
    m = prep_inputs(cfg, inp, lambda_init)
    res = run_bass_kernel_spmd(nc, [m], core_ids=[0])
    return res.results[0]["out"]


def kernel(**inputs):
    lambda_init = 0.8 - 0.6 * math.exp(-0.3 * 0)
    o = run(FULL, inputs, lambda_init)
    return np.asarray(o, dtype=np.float32).reshape(1, FULL["SEQ"], FULL["D"])
```

```python
import math
import contextlib
import numpy as np
import concourse.bass as bass
import concourse.mybir as mybir
from concourse.bass_utils import run_bass_kernel_spmd

F32 = mybir.dt.float32
BF16 = mybir.dt.bfloat16
I32 = mybir.dt.int32
AF = mybir.ActivationFunctionType
ALU = mybir.AluOpType

FULL = dict(D=4096, FF=14336, SEQ=8192, HD=16, HS=16, MEM=256, XH=4, TT=256)
ROPE_THETA = 500000.0
EPS = 1e-6
NDMA = 16


class Buf:
    __slots__ = ("w", "r")

    def __init__(self):
        self.w = None
        self.r = []


class Ctx:
    def __init__(self, nc, es):
        self.nc = nc
        self.es = es
        self.eng = {"pe": nc.tensor, "act": nc.scalar, "dve": nc.vector, "pool": nc.gpsimd, "sp": nc.sync}
        self.sem = {k: es.enter_context(nc.semaphore("sem_" + k)) for k in ["pe", "act", "dve", "pool"]}
        self.dsem = [es.enter_context(nc.semaphore(f"dsem{i}")) for i in range(NDMA)]
        for i, s in enumerate(self.dsem):
            self.sem[f"d{i}"] = s
        self.bufs = []
        self.reset()

    def reset(self):
        self.cnt = {k: 0 for k in self.sem}
        self.waited = {e: {} for e in self.eng}
        self.dnext = 0
        for b in self.bufs:
            b.w = None
            b.r = []

    def buf(self):
        b = Buf()
        self.bufs.append(b)
        return b

    def bufs_n(self, n):
        return [self.buf() for _ in range(n)]

    def _deps(self, eng, reads, writes):
        deps = {}

        def add(tok, same_ok):
            if tok is None:
                return
            sm, v = tok
            if sm == eng and not same_ok:
                return
            if eng == "pe" and sm == "pe":
                return
            if deps.get(sm, 0) < v:
                deps[sm] = v

        for b in reads:
            add(b.w, True)
        for b in writes:
            add(b.w, False)
            for t in b.r:
                add(t, False)
        w = self.waited[eng]
        for sm, v in deps.items():
            if w.get(sm, 0) < v:
                self.eng[eng].wait_ge(self.sem[sm], v)
                w[sm] = v

    def _commit(self, tok, reads, writes):
        for b in reads:
            b.r.append(tok)
            if len(b.r) > 12:
                b.r = b.r[-12:] if False else self._compact(b.r)
        for b in writes:
            b.w = tok
            b.r = []

    @staticmethod
    def _compact(toks):
        best = {}
        for sm, v in toks:
            if best.get(sm, 0) < v:
                best[sm] = v
        return list(best.items())

    def op(self, eng, fn, reads=(), writes=()):
        self._deps(eng, reads, writes)
        ins = fn()
        self.cnt[eng] += 1
        ins.then_inc(self.sem[eng], 1)
        self._commit((eng, self.cnt[eng]), reads, writes)

    def pe_group(self, fns, reads=(), writes=()):
        self._deps("pe", reads, writes)
        ins = None
        for fn in fns:
            ins = fn()
        self.cnt["pe"] += 1
        ins.then_inc(self.sem["pe"], 1)
        self._commit(("pe", self.cnt["pe"]), reads, writes)

    def dma(self, out, in_, reads=(), writes=(), q="sp"):
        k = self.dnext
        self.dnext = (self.dnext + 1) % NDMA
        name = f"d{k}"
        prev = self.cnt[name]
        w = self.waited[q]
        if prev and w.get(name, 0) < prev:
            self.eng[q].wait_ge(self.sem[name], prev)
            w[name] = prev
        self._deps(q, reads, writes)
        self.eng[q].dma_start(out=out, in_=in_).then_inc(self.sem[name], 16)
        self.cnt[name] += 16
        self._commit((name, self.cnt[name]), reads, writes)

    def end_iter(self):
        w = self.waited["sp"]
        for i in range(NDMA):
            name = f"d{i}"
            if self.cnt[name] and w.get(name, 0) < self.cnt[name]:
                self.nc.sync.wait_ge(self.sem[name], self.cnt[name])
        for e in ["pe", "act", "dve", "pool"]:
            if self.cnt[e]:
                self.nc.sync.wait_ge(self.sem[e], self.cnt[e])
        self.nc.all_engine_barrier()
        for s in self.sem.values():
            self.nc.sync.sem_clear(s)
        self.nc.all_engine_barrier()
        self.reset()


def tile_w(W, ktp):
    K, N = W.shape
    kb = K // (128 * ktp)
    a = W.reshape(kb, ktp, 128, N // 128, 128)
    a = a.transpose(3, 0, 2, 1, 4)
    return np.ascontiguousarray(a).reshape(N // 128, kb, 128, ktp * 128)


def gain_t(g):
    g = np.asarray(g).reshape(-1)
    return np.ascontiguousarray(g.reshape(-1, 128).T)


def build(cfg, lambda_init):
    D, FF, SEQ, HD, HS, MEM, XH, TT = (cfg[k] for k in ["D", "FF", "SEQ", "HD", "HS", "MEM", "XH", "TT"])
    DC, FC, NTT, NH = D // 128, FF // 128, SEQ // TT, HD + HS
    TS = TT // 128
    assert MEM == TT
    ktpD = min(16, DC)
    ktpF = min(8, FC // 2)
    ktpH = min(16, NH)
    nc = bass.Bass("TRN2", target_bir_lowering=False)

    def din(name, shape, dt=F32):
        return nc.dram_tensor(name, list(shape), dt, kind="ExternalInput").ap()

    def dscr(name, shape, dt):
        return nc.dram_tensor(name, list(shape), dt, kind="Internal").ap()

    x = din("x", [SEQ, D])
    mem = din("mem", [MEM, D])
    posb = din("posb", [128, SEQ], I32)
    gains = din("gains", [128, 9 * DC])
    hg = din("hg", [128, 2])
    lamv = din("lamv", [128, 4 * 64])
    ropet = din("ropet", [128, 2])
    masks = din("masks", [4 * 128, TT])
    ident_d = din("ident", [128, 128])
    tri_d = din("tri", [128, 128])
    wg1 = din("wg1", [FC, DC // ktpD, 128, ktpD * 128])
    wu1 = din("wu1", [FC, DC // ktpD, 128, ktpD * 128])
    wd1 = din("wd1", [DC, FC // ktpF, 128, ktpF * 128])
    wg2 = din("wg2", [FC, DC // ktpD, 128, ktpD * 128])
    wu2 = din("wu2", [FC, DC // ktpD, 128, ktpD * 128])
    wd2 = din("wd2", [DC, FC // ktpF, 128, ktpF * 128])
    win = din("win", [6 * (NH // 2), DC // ktpD, 128, ktpD * 128])
    wsw = din("wsw", [2 * HD, DC // ktpD, 128, ktpD * 128])
    wout = din("wout", [DC, NH // ktpH, 128, ktpH * 128])
    wq = din("wq", [XH, DC // ktpD, 128, ktpD * 128])
    wkv = din("wkv", [2 * XH, DC // ktpD, 128, ktpD * 128])
    wo = din("wo", [DC, 1, 128, XH * 128])
    out = nc.dram_tensor("out", [SEQ, D], F32, kind="ExternalOutput").ap()

    h1T = dscr("h1T", [NTT * 128, DC * TT], F32)
    QT = dscr("QT", [NH * 128, SEQ], BF16)
    KT = dscr("KT", [NH * 128, SEQ], BF16)
    Vd = dscr("Vd", [NH * 128, SEQ], BF16)
    MT = dscr("MT", [NH * 128, SEQ], BF16)

    es = contextlib.ExitStack()
    with es:
        cx = Ctx(nc, es)

        def sb(name, shape, dt):
            return es.enter_context(nc.sbuf_tensor(name, list(shape), dt))

        def psum(name, dt=F32, n=512):
            return es.enter_context(nc.psum_tensor(name, [128, n], dt))

        hT = sb("hT", [128, DC, TT], F32); hT_b = cx.bufs_n(DC)
        yT = sb("yT", [128, DC, TT], F32); yT_b = cx.bufs_n(DC)
        hnT = sb("hnT", [128, DC, TT], BF16); hnT_b = cx.bufs_n(DC)
        NACT = max(FC // 2, NH, 3 * (SEQ // TT), 3 * NH, 2 * XH)
        actT = sb("actT", [128, NACT, TT], BF16); actT_b = cx.bufs_n(NACT)
        xin = yT[:].rearrange("p c t -> p (c t)")[:, 0:D]
        NSTG, NWB = 4, 4
        stg = [sb(f"stg{i}", [128, 16 * 128], F32) for i in range(NSTG)]; stg_b = cx.bufs_n(NSTG)
        wbf = [sb(f"wbf{i}", [128, 16 * 128], BF16) for i in range(NWB)]; wbf_b = cx.bufs_n(NWB)
        NTMP = 6
        tmp = [sb(f"tmp{i}", [128, TT], F32) for i in range(NTMP)]; tmp_b = cx.bufs_n(NTMP)
        tmpb = [sb(f"tmpb{i}", [128, TT], BF16) for i in range(NTMP)]; tmpb_b = cx.bufs_n(NTMP)
        rstd = sb("rstd", [128, TT], F32); rstd_b = cx.buf()
        lnv = sb("lnv", [128, TT], F32); lnv_b = cx.buf()
        gains_s = sb("gains_s", [128, 9 * DC], F32); c_b = cx.buf()
        hg_s = sb("hg_s", [128, 2], F32)
        hg2_s = sb("hg2_s", [128, 2], F32)
        lam_s = sb("lam_s", [128, 256], F32)
        lam_t = sb("lam_t", [128, 8], F32)
        rope_s = sb("rope_s", [128, 2], F32)
        masks_s = sb("masks_s", [128, 4, TT], F32)
        masks_bf = sb("masks_bf", [128, 4, TT], BF16)
        ident = sb("ident_s", [128, 128], F32)
        identb = sb("identb", [128, 128], BF16)
        tri32 = sb("tri32", [128, 128], F32)
        trib = sb("trib", [128, 128], BF16)
        ones32 = sb("ones32", [128, 128], F32)
        onesb = sb("onesb", [128, 128], BF16)
        cst = sb("cst", [128, 4], F32)
        Ct = sb("Ct", [128, TT], F32); St = sb("St", [128, TT], F32); cs_b = cx.buf()
        posi = sb("posi", [128, TT], I32); posf = sb("posf", [128, TT], F32); pos_b = cx.buf()
        ki = sb("ki", [128, TT], I32); ki_b = cx.buf()
        KmT = sb("KmT", [128, XH, MEM], BF16); Vm = sb("Vm", [128, MEM // 128, XH * 128], BF16); km_b = cx.buf()
        qx = actT[:, 0:XH, :]; qx_b = actT_b[0:XH]
        ox = actT[:, XH:2 * XH, :]; ox_b = actT_b[XH:2 * XH]
        NQ = SEQ // TT
        KTs = actT[:, 0:NQ, :].rearrange("p c t -> p (c t)")
        QTs = actT[:, NQ:2 * NQ, :].rearrange("p c t -> p (c t)")
        Vs = actT[:, 2 * NQ:3 * NQ, :].rearrange("p c t -> p (c t)").rearrange("p (k d) -> p k d", d=128)
        kqv_b = cx.bufs_n(3)
        vst_b = cx.buf()
        ofull = sb("ofull", [128, TT], F32); ofull_b = cx.buf()
        NPS = 7
        ps = [psum(f"ps{i}") for i in range(NPS)]; ps_b = cx.bufs_n(NPS)
        psb = psum("psb", BF16, 1024); psb_b = cx.buf()
        psrr = [0]
        castrr = [0]
        CAST_ENG = ["dve", "act"]

        def next_ps():
            i = psrr[0]
            psrr[0] = (i + 1) % 3
            return i

        ring = {"tmp": 0, "tmpb": 0, "xin": 0, "ostg": 0, "vstg": 0}

        def rr(name, n):
            i = ring[name]
            ring[name] = (i + 1) % n
            return i

        G = lambda k: gains_s[:, k * DC:(k + 1) * DC]
        eps_c, one_c, npi_c = cst[:, 0:1], cst[:, 1:2], cst[:, 2:3]

        wf = {"q": [], "issued": 0, "consumed": 0, "n": 0}
        LOOK = 3

        def plan_lin(W, n, KT_, ktp, kb0=0):
            for kb in range(KT_ // ktp):
                wf["q"].append((W, n, kb0 + kb, ktp))

        def w_issue(upto):
            while wf["issued"] <= min(upto, len(wf["q"]) - 1):
                W, n, kb, ktp = wf["q"][wf["issued"]]
                i = wf["n"] + wf["issued"]
                s, b = i % NSTG, i % NWB
                cx.dma(stg[s][:, :ktp * 128], W[n, kb], writes=[stg_b[s]])
                if i % 2 == 0:
                    cx.op("dve", lambda s=s, b=b, ktp=ktp: nc.vector.tensor_copy(out=wbf[b][:, :ktp * 128], in_=stg[s][:, :ktp * 128]),
                          reads=[stg_b[s]], writes=[wbf_b[b]])
                else:
                    cx.op("act", lambda s=s, b=b, ktp=ktp: nc.scalar.copy(out=wbf[b][:, :ktp * 128], in_=stg[s][:, :ktp * 128]),
                          reads=[stg_b[s]], writes=[wbf_b[b]])
                wf["issued"] += 1

        def w_get(W, n, kb, ktp):
            if wf["consumed"] == len(wf["q"]):
                wf["q"].append((W, n, kb, ktp))
            e = wf["q"][wf["consumed"]]
            assert e[0] is W and e[1:] == (n, kb, ktp), ("weight plan mismatch", e[1:], (n, kb, ktp))
            w_issue(wf["consumed"] + LOOK)
            b = (wf["n"] + wf["consumed"]) % NWB
            wf["consumed"] += 1
            if wf["consumed"] == len(wf["q"]):
                wf["n"] += len(wf["q"]); wf["q"] = []; wf["issued"] = 0; wf["consumed"] = 0
            return b

        def lin_chunk(W, n, inT, in_b, KT_, ktp, pst, pst_b, kb0=0):
            nkb = KT_ // ktp
            for kb in range(nkb):
                b = w_get(W, n, kb0 + kb, ktp)
                fns = []
                for j in range(ktp):
                    k = kb * ktp + j
                    fns.append(lambda b=b, j=j, k=k: nc.tensor.matmul(
                        pst[:, :TT], wbf[b][:, j * 128:(j + 1) * 128], inT[:, k, :],
                        start=(k == 0), stop=(k == KT_ - 1)))
                cx.pe_group(fns, reads=[wbf_b[b]] + [in_b[kb * ktp + j] for j in range(ktp)], writes=[pst_b])

        def rms_stats(srcT, src_b, nch, div):
            a = 4
            for c in range(nch):
                t = rr("tmp", NTMP)
                cx.op("act", lambda c=c, t=t: nc.scalar.activation(out=tmp[t][:], in_=srcT[:, c, :], func=AF.Square),
                      reads=[src_b[c]], writes=[tmp_b[t]])
                cx.op("pe", lambda c=c, t=t: nc.tensor.matmul(ps[a][:, :TT], ones32[:], tmp[t][:],
                                                              start=(c == 0), stop=(c == nch - 1)),
                      reads=[tmp_b[t]], writes=[ps_b[a]])
            cx.op("act", lambda: nc.scalar.activation(out=lnv[:], in_=ps[a][:, :TT], func=AF.Ln, scale=1.0 / div, bias=eps_c),
                  reads=[ps_b[a]], writes=[lnv_b])
            cx.op("act", lambda: nc.scalar.activation(out=rstd[:], in_=lnv[:], func=AF.Exp, scale=-0.5),
                  reads=[lnv_b], writes=[rstd_b])

        def norm_to(srcT, src_b, gain, dstT, dst_b):
            rms_stats(srcT, src_b, DC, float(D))
            for c in range(DC):
                cx.op("dve", lambda c=c: nc.vector.scalar_tensor_tensor(
                    out=dstT[:, c, :], in0=srcT[:, c, :], scalar=gain[:, c:c + 1], in1=rstd[:],
                    op0=ALU.mult, op1=ALU.mult), reads=[src_b[c], rstd_b], writes=[dst_b[c]])

        def post_res(gain, factor):
            rms_stats(yT, yT_b, DC, float(D))
            for c in range(DC):
                t = rr("tmp", NTMP)
                cx.op("dve", lambda c=c, t=t: nc.vector.scalar_tensor_tensor(
                    out=tmp[t][:], in0=yT[:, c, :], scalar=gain[:, c:c + 1], in1=rstd[:],
                    op0=ALU.mult, op1=ALU.mult), reads=[yT_b[c], rstd_b], writes=[tmp_b[t]])
                cx.op("dve", lambda c=c, t=t: nc.vector.scalar_tensor_tensor(
                    out=hT[:, c, :], in0=tmp[t][:], scalar=float(factor), in1=hT[:, c, :],
                    op0=ALU.mult, op1=ALU.add), reads=[tmp_b[t], hT_b[c]], writes=[hT_b[c]])

        def load_T(src, row0, dstT, dst_b, ntok_tiles):
            for ts in range(ntok_tiles):
                cx.dma(xin, src[bass.ds(row0 + ts * 128, 128), :], writes=yT_b)
                for cg in range(max(1, DC // 4)):
                    ncg = min(4, DC)
                    p = next_ps()
                    cx.pe_group([lambda q=q, p=p, cg=cg: nc.tensor.transpose(
                        ps[p][:, q * 128:(q + 1) * 128], xin[:, (cg * 4 + q) * 128:(cg * 4 + q + 1) * 128], ident[:]) for q in range(ncg)],
                        reads=yT_b, writes=[ps_b[p]])
                    for q in range(ncg):
                        c = cg * 4 + q
                        cx.op("act", lambda q=q, c=c, p=p, ts=ts: nc.scalar.copy(
                            out=dstT[:, c, ts * 128:(ts + 1) * 128], in_=ps[p][:, q * 128:(q + 1) * 128]),
                            reads=[ps_b[p]], writes=[dst_b[c]])

        def ffn(Wg, Wu, Wd, gpre, gpost):
            FH = FC // 2
            for half in range(2):
                for n in range(FH):
                    plan_lin(Wg, half * FH + n, DC, ktpD)
                    plan_lin(Wu, half * FH + n, DC, ktpD)
                for c in range(DC):
                    plan_lin(Wd, c, FH, ktpF, kb0=half * (FH // ktpF))
            norm_to(hT, hT_b, gpre, hnT, hnT_b)
            for half in range(2):
                for n in range(FH):
                    pg, pu = next_ps(), next_ps()
                    lin_chunk(Wg, half * FH + n, hnT, hnT_b, DC, ktpD, ps[pg], ps_b[pg])
                    lin_chunk(Wu, half * FH + n, hnT, hnT_b, DC, ktpD, ps[pu], ps_b[pu])
                    t = rr("tmp", NTMP)
                    cx.op("act", lambda pg=pg, t=t: nc.scalar.activation(out=tmp[t][:], in_=ps[pg][:, :TT], func=AF.Silu),
                          reads=[ps_b[pg]], writes=[tmp_b[t]])
                    cx.op("dve", lambda pu=pu, t=t, n=n: nc.vector.tensor_tensor(
                        out=actT[:, n, :], in0=tmp[t][:], in1=ps[pu][:, :TT], op=ALU.mult),
                        reads=[tmp_b[t], ps_b[pu]], writes=[actT_b[n]])
                for c in range(DC):
                    p = next_ps()
                    lin_chunk(Wd, c, actT, actT_b, FH, ktpF, ps[p], ps_b[p], kb0=half * (FH // ktpF))
                    if half == 0:
                        cx.op("act", lambda c=c, p=p: nc.scalar.copy(out=yT[:, c, :], in_=ps[p][:, :TT]),
                              reads=[ps_b[p]], writes=[yT_b[c]])
                    else:
                        cx.op("dve", lambda c=c, p=p: nc.vector.tensor_tensor(out=yT[:, c, :], in0=ps[p][:, :TT], in1=yT[:, c, :], op=ALU.add),
                              reads=[ps_b[p], yT_b[c]], writes=[yT_b[c]])
            post_res(gpost, 0.5)

        def to_tokmajor_store(srcb_ap, src_buf, dram_rows_fn):
            for ts in range(TS):
                cx.op("pe", lambda ts=ts: nc.tensor.transpose(psb[:, ts * 128:(ts + 1) * 128],
                                                              srcb_ap[:, ts * 128:(ts + 1) * 128], identb[:]),
                      reads=[src_buf], writes=[psb_b])
                v = rr("vstg", 2)
                cx.op("act", lambda ts=ts, v=v: nc.scalar.copy(out=vstg[v][:], in_=psb[:, ts * 128:(ts + 1) * 128]),
                      reads=[psb_b], writes=[vstg_b[v]])
                cx.dma(dram_rows_fn(ts), vstg[v][:], reads=[vstg_b[v]])

        cx.dma(gains_s[:], gains[:, :], writes=[c_b])
        cx.dma(hg_s[:], hg[:, :], writes=[c_b])
        cx.dma(lam_s[:], lamv[:, :], writes=[c_b])
        cx.dma(rope_s[:], ropet[:, :], writes=[c_b])
        cx.dma(masks_s[:], masks.rearrange("(m p) t -> p m t", p=128), writes=[c_b])
        cx.dma(ident[:], ident_d[:, :], writes=[c_b])
        cx.dma(tri32[:], tri_d[:, :], writes=[c_b])
        V = nc.vector
        cx.op("dve", lambda: V.memset(ones32[:], 1.0), writes=[c_b])
        cx.op("dve", lambda: V.memset(onesb[:], 1.0), writes=[c_b])
        cx.op("dve", lambda: V.memset(cst[:, 0:1], EPS), writes=[c_b])
        cx.op("dve", lambda: V.memset(cst[:, 1:2], 1.0), writes=[c_b])
        cx.op("dve", lambda: V.memset(cst[:, 2:3], -math.pi), writes=[c_b])
        cx.op("dve", lambda: V.memset(cst[:, 3:4], 0.0), writes=[c_b])
        cx.op("dve", lambda: V.tensor_copy(out=identb[:], in_=ident[:]), reads=[c_b], writes=[c_b])
        cx.op("dve", lambda: V.tensor_copy(out=trib[:], in_=tri32[:]), reads=[c_b], writes=[c_b])
        cx.op("dve", lambda: V.tensor_copy(out=masks_bf[:], in_=masks_s[:]), reads=[c_b], writes=[c_b])
        cx.op("dve", lambda: V.tensor_tensor(out=lam_s[:, 0:64], in0=lam_s[:, 0:64], in1=lam_s[:, 64:128], op=ALU.mult), reads=[c_b], writes=[c_b])
        cx.op("dve", lambda: V.tensor_tensor(out=lam_s[:, 128:192], in0=lam_s[:, 128:192], in1=lam_s[:, 192:256], op=ALU.mult), reads=[c_b], writes=[c_b])
        cx.op("dve", lambda: V.tensor_reduce(out=lam_t[:, 0:1], in_=lam_s[:, 0:64], op=ALU.add, axis=mybir.AxisListType.X), reads=[c_b], writes=[c_b])
        cx.op("dve", lambda: V.tensor_reduce(out=lam_t[:, 1:2], in_=lam_s[:, 128:192], op=ALU.add, axis=mybir.AxisListType.X), reads=[c_b], writes=[c_b])
        cx.op("act", lambda: nc.scalar.activation(out=lam_t[:, 2:4], in_=lam_t[:, 0:2], func=AF.Exp), reads=[c_b], writes=[c_b])
        cx.op("dve", lambda: V.tensor_tensor(out=lam_t[:, 4:5], in0=lam_t[:, 3:4], in1=lam_t[:, 2:3], op=ALU.subtract), reads=[c_b], writes=[c_b])
        cx.op("dve", lambda: V.tensor_scalar(out=lam_t[:, 5:6], in0=lam_t[:, 4:5], scalar1=-float(lambda_init), scalar2=None, op0=ALU.add), reads=[c_b], writes=[c_b])
        neglam = lam_t[:, 5:6]
        cx.op("dve", lambda: V.tensor_scalar(out=hg2_s[:, 0:1], in0=hg_s[:, 0:1], scalar1=float(1.0 - lambda_init), scalar2=None, op0=ALU.mult), reads=[c_b], writes=[c_b])
        cx.op("dve", lambda: V.tensor_copy(out=hg2_s[:, 1:2], in_=hg_s[:, 1:2]), reads=[c_b], writes=[c_b])

        load_T(mem, 0, hT, hT_b, MEM // 128)
        norm_to(hT, hT_b, G(8), hnT, hnT_b)
        for h in range(XH):
            p = next_ps()
            lin_chunk(wkv, h, hnT, hnT_b, DC, ktpD, ps[p], ps_b[p])
            cx.op("act", lambda h=h, p=p: nc.scalar.copy(out=KmT[:, h, :], in_=ps[p][:, :MEM]), reads=[ps_b[p]], writes=[km_b])
        for h in range(XH):
            p = next_ps()
            lin_chunk(wkv, XH + h, hnT, hnT_b, DC, ktpD, ps[p], ps_b[p])
            t = rr("tmpb", NTMP)
            cx.op("act", lambda p=p, t=t: nc.scalar.copy(out=tmpb[t][:], in_=ps[p][:, :MEM]), reads=[ps_b[p]], writes=[tmpb_b[t]])
            for mt in range(MEM // 128):
                cx.op("pe", lambda t=t, mt=mt: nc.tensor.transpose(psb[:, mt * 128:(mt + 1) * 128], tmpb[t][:, mt * 128:(mt + 1) * 128], identb[:]),
                      reads=[tmpb_b[t]], writes=[psb_b])
                cx.op("act", lambda h=h, mt=mt: nc.scalar.copy(out=Vm[:, mt, h * 128:(h + 1) * 128], in_=psb[:, mt * 128:(mt + 1) * 128]),
                      reads=[psb_b], writes=[km_b])
        cx.end_iter()

        with nc.Fori(0, NTT) as i:
            tok0 = i * TT
            load_T(x, tok0, hT, hT_b, TS)
            ffn(wg1, wu1, wd1, G(0), G(1))
            norm_to(hT, hT_b, G(2), hnT, hnT_b)
            cx.dma(h1T[bass.ds(i * 128, 128), :], hT[:].rearrange("p c t -> p (c t)"), reads=hT_b)
            cx.dma(posi[:], posb[:, bass.ds(tok0, TT)], writes=[pos_b])
            cx.op("dve", lambda: V.tensor_copy(out=posf[:], in_=posi[:]), reads=[pos_b], writes=[pos_b])
            for which, dst in ((0, St), (1, Ct)):
                t = rr("tmp", NTMP)
                cx.op("dve", lambda t=t, which=which: V.tensor_scalar(
                    out=tmp[t][:], in0=posf[:], scalar1=rope_s[:, 0:1], scalar2=(0.5 if which == 0 else 0.75),
                    op0=ALU.mult, op1=ALU.add), reads=[pos_b, c_b], writes=[tmp_b[t]])
                cx.op("dve", lambda t=t: V.tensor_copy(out=ki[:], in_=tmp[t][:]), reads=[tmp_b[t]], writes=[ki_b])
                t2 = rr("tmp", NTMP)
                cx.op("dve", lambda t2=t2: V.tensor_copy(out=tmp[t2][:], in_=ki[:]), reads=[ki_b], writes=[tmp_b[t2]])
                cx.op("dve", lambda t=t, t2=t2: V.tensor_tensor(out=tmp[t][:], in0=tmp[t][:], in1=tmp[t2][:], op=ALU.subtract),
                      reads=[tmp_b[t], tmp_b[t2]], writes=[tmp_b[t]])
                cx.op("dve", lambda t=t, t2=t2: V.scalar_tensor_tensor(out=tmp[t2][:], in0=tmp[t][:], scalar=0.0, in1=tmp[t][:],
                                                                      op0=ALU.is_lt, op1=ALU.add),
                      reads=[tmp_b[t]], writes=[tmp_b[t2]])
                cx.op("act", lambda t2=t2, dst=dst: nc.scalar.activation(out=dst[:], in_=tmp[t2][:], func=AF.Sin,
                                                                        scale=2.0 * math.pi, bias=npi_c),
                      reads=[tmp_b[t2], c_b], writes=[cs_b])
            cx.op("dve", lambda: V.tensor_scalar(out=St[:], in0=St[:], scalar1=rope_s[:, 1:2], scalar2=None, op0=ALU.mult),
                  reads=[cs_b, c_b], writes=[cs_b])
            for grp in range(6):
                for hh in range(HD if grp < 3 else HS):
                    plan_lin(win, (grp * HD + hh) if grp < 3 else (3 * HD + (grp - 3) * HS + hh), DC, ktpD)
                    if grp in (0, 1):
                        plan_lin(wsw, grp * HD + hh, DC, ktpD)
            Vst = actT[:, 2 * NH:3 * NH, :].rearrange("p c t -> p (c t)").rearrange("p (s n d) -> p s n d", s=TS, n=NH, d=128)
            for grp in range(6):
                nh = HD if grp < 3 else HS
                for hh in range(nh):
                    n = (grp * HD + hh) if grp < 3 else (3 * HD + (grp - 3) * HS + hh)
                    head = hh if grp < 3 else HD + hh
                    p = next_ps()
                    lin_chunk(win, n, hnT, hnT_b, DC, ktpD, ps[p], ps_b[p])
                    if grp in (0, 1):
                        slot = head if grp == 0 else NH + head
                        p2 = next_ps()
                        lin_chunk(wsw, grp * HD + hh, hnT, hnT_b, DC, ktpD, ps[p2], ps_b[p2])
                        t = rr("tmp", NTMP); t2 = rr("tmp", NTMP)
                        cx.op("dve", lambda t=t, p=p: V.tensor_tensor(out=tmp[t][:], in0=ps[p][:, :TT], in1=Ct[:], op=ALU.mult),
                              reads=[ps_b[p], cs_b], writes=[tmp_b[t]])
                        cx.op("dve", lambda t2=t2, p2=p2: V.tensor_tensor(out=tmp[t2][:], in0=ps[p2][:, :TT], in1=St[:], op=ALU.mult),
                              reads=[ps_b[p2], cs_b], writes=[tmp_b[t2]])
                        cx.op("dve", lambda t=t, t2=t2, slot=slot: V.tensor_tensor(out=actT[:, slot, :], in0=tmp[t][:], in1=tmp[t2][:], op=ALU.add),
                              reads=[tmp_b[t], tmp_b[t2]], writes=[actT_b[slot]])
                    elif grp in (3, 4):
                        slot = head if grp == 3 else NH + head
                        cx.op("act", lambda p=p, slot=slot: nc.scalar.copy(out=actT[:, slot, :], in_=ps[p][:, :TT]),
                              reads=[ps_b[p]], writes=[actT_b[slot]])
                    else:
                        tb = rr("tmpb", NTMP)
                        cx.op("act", lambda p=p, tb=tb: nc.scalar.copy(out=tmpb[tb][:], in_=ps[p][:, :TT]),
                              reads=[ps_b[p]], writes=[tmpb_b[tb]])
                        for ts in range(TS):
                            cx.op("pe", lambda ts=ts, tb=tb: nc.tensor.transpose(psb[:, ts * 128:(ts + 1) * 128],
                                                                              tmpb[tb][:, ts * 128:(ts + 1) * 128], identb[:]),
                                  reads=[tmpb_b[tb]], writes=[psb_b])
                            cx.op("act", lambda ts=ts, head=head: nc.scalar.copy(out=actT[:, 2 * NH + head, ts * 128:(ts + 1) * 128],
                                                                               in_=psb[:, ts * 128:(ts + 1) * 128]),
                                  reads=[psb_b], writes=[actT_b[2 * NH + head]])
            HB = min(16, NH)
            for n0 in range(0, NH, HB):
                for which, (dst, qq) in enumerate(((QT, "sp"), (KT, "act"), (Vd, "act"))):
                    cx.dma(dst[n0 * 128:(n0 + HB) * 128, bass.ds(tok0, TT)].rearrange("(n p) t -> p n t", p=128),
                           actT[:, which * NH + n0:which * NH + n0 + HB, :],
                           reads=actT_b[which * NH + n0:which * NH + n0 + HB], q=qq)
            cx.end_iter()

        NQG = SEQ // TT
        assert NQG <= DC
        sc_d = 64 ** -0.5
        sc_s = 128 ** -0.5

        def load_head(hidx):
            cx.dma(KTs, KT[bass.ds(hidx * 128, 128), :], writes=[kqv_b[0]])
            cx.dma(QTs, QT[bass.ds(hidx * 128, 128), :], writes=[kqv_b[1]])
            cx.dma(actT[:, 2 * NQ:3 * NQ, :].rearrange("p c t -> p (c t)"), Vd[bass.ds(hidx * 128, 128), :], writes=[kqv_b[2]])

        def head_norm_store(hidx, g, gcol):
            a = 3
            t = rr("tmp", NTMP)
            cx.op("act", lambda t=t: nc.scalar.activation(out=tmp[t][:], in_=ofull[:], func=AF.Square),
                  reads=[ofull_b], writes=[tmp_b[t]])
            cx.op("pe", lambda t=t: nc.tensor.matmul(ps[a][:, :TT], ones32[:], tmp[t][:], start=True, stop=True),
                  reads=[tmp_b[t]], writes=[ps_b[a]])
            cx.op("act", lambda: nc.scalar.activation(out=lnv[:], in_=ps[a][:, :TT], func=AF.Ln, scale=1.0 / 128, bias=eps_c),
                  reads=[ps_b[a]], writes=[lnv_b])
            cx.op("act", lambda: nc.scalar.activation(out=rstd[:], in_=lnv[:], func=AF.Exp, scale=-0.5),
                  reads=[lnv_b], writes=[rstd_b])
            cx.op("dve", lambda: V.scalar_tensor_tensor(out=hnT[:, g, :], in0=ofull[:], scalar=hg2_s[:, gcol:gcol + 1],
                                                       in1=rstd[:], op0=ALU.mult, op1=ALU.mult),
                  reads=[ofull_b, rstd_b, c_b], writes=[hnT_b[g]])
            if g == NQG - 1:
                cx.dma(MT[bass.ds(hidx * 128, 128), :], hnT[:, 0:NQG, :].rearrange("p c t -> p (c t)"), reads=hnT_b[0:NQG])

        def skew(tasks, stageA, stageB, lag):
            n = len(tasks)
            st = {}
            for i in range(n + lag):
                if i < n:
                    st[i] = stageA(tasks[i])
                j = i - lag
                if j >= 0:
                    stageB(tasks[j], st.pop(j))

        with nc.Fori(0, HD) as hd:
            load_head(hd)
            tasks = [(g, kt, comp) for g in range(NQG) for kt in range((g + 1) * TS) for comp in range(2)]

            def dA(task):
                g, kt, comp = task
                qs = slice(g * TT, (g + 1) * TT); ks = slice(kt * 128, (kt + 1) * 128)
                dg = kt - g * TS
                p = next_ps()
                lo = comp * 64
                cx.op("pe", lambda: nc.tensor.matmul(ps[p][:, :TT], KTs[lo:lo + 64, ks], QTs[lo:lo + 64, qs], start=True, stop=True),
                      reads=[kqv_b[0], kqv_b[1]], writes=[ps_b[p]])
                tb = rr("tmpb", NTMP)
                cx.op("act", lambda: nc.scalar.activation(out=tmpb[tb][:], in_=ps[p][:, :TT], func=AF.Exp, scale=sc_d),
                      reads=[ps_b[p]], writes=[tmpb_b[tb]])
                if dg >= 0:
                    cx.op("dve", lambda: V.tensor_tensor(out=tmpb[tb][:], in0=tmpb[tb][:], in1=masks_bf[:, dg, :], op=ALU.mult),
                          reads=[tmpb_b[tb], c_b], writes=[tmpb_b[tb]])
                return tb

            def dB(task, tb):
                g, kt, comp = task
                nkt = (g + 1) * TS
                po, pl = (3, 4) if comp == 0 else (5, 6)
                cx.op("pe", lambda: nc.tensor.matmul(ps[po][:, :TT], Vs[:, kt, :], tmpb[tb][:], start=(kt == 0), stop=(kt == nkt - 1)),
                      reads=[tmpb_b[tb], kqv_b[2]], writes=[ps_b[po]])
                cx.op("pe", lambda: nc.tensor.matmul(ps[pl][:, :TT], onesb[:], tmpb[tb][:], start=(kt == 0), stop=(kt == nkt - 1)),
                      reads=[tmpb_b[tb]], writes=[ps_b[pl]])
                if kt == nkt - 1 and comp == 1:
                    t1 = rr("tmp", NTMP); t2 = rr("tmp", NTMP)
                    for (po_, pl_, t) in ((3, 4, t1), (5, 6, t2)):
                        cx.op("dve", lambda: V.reciprocal(out=tmp[t][:], in_=ps[pl_][:, :TT]), reads=[ps_b[pl_]], writes=[tmp_b[t]])
                        cx.op("dve", lambda: V.tensor_tensor(out=tmp[t][:], in0=ps[po_][:, :TT], in1=tmp[t][:], op=ALU.mult),
                              reads=[ps_b[po_], tmp_b[t]], writes=[tmp_b[t]])
                    cx.op("dve", lambda: V.scalar_tensor_tensor(out=ofull[:], in0=tmp[t2][:], scalar=neglam, in1=tmp[t1][:],
                                                               op0=ALU.mult, op1=ALU.add),
                          reads=[tmp_b[t1], tmp_b[t2], c_b], writes=[ofull_b])
                    head_norm_store(hd, g, 0)

            skew(tasks, dA, dB, 2)
            cx.end_iter()

        with nc.Fori(0, HS) as hs:
            hidx = hs + HD
            load_head(hidx)
            PO, PC = 4, 5
            tasks = [(g, kt) for g in range(NQG) for kt in range((g + 1) * TS - 1, -1, -1)]

            n_t = len(tasks)
            S = [dict() for _ in range(n_t)]

            def flags(i):
                g, kt = tasks[i]
                nkt = (g + 1) * TS
                return g, kt, kt - g * TS, (kt == nkt - 1), (kt == 0)

            def sa_pe(i):
                g, kt, dg, first, last = flags(i)
                p = next_ps(); S[i]["p"] = p
                cx.op("pe", lambda: nc.tensor.matmul(ps[p][:, :TT], KTs[:, kt * 128:(kt + 1) * 128], QTs[:, g * TT:(g + 1) * TT],
                                                     start=True, stop=True),
                      reads=[kqv_b[0], kqv_b[1]], writes=[ps_b[p]])

            def sa_act(i):
                p = S[i]["p"]
                te = rr("tmp", NTMP); tl = rr("tmp", NTMP)
                S[i]["te"], S[i]["tl"] = te, tl
                cx.op("act", lambda: nc.scalar.activation(out=tmp[te][:], in_=ps[p][:, :TT], func=AF.Exp, scale=-sc_s),
                      reads=[ps_b[p]], writes=[tmp_b[te]])
                cx.op("act", lambda: nc.scalar.activation(out=tmp[tl][:], in_=tmp[te][:], func=AF.Ln, scale=1.0, bias=one_c),
                      reads=[tmp_b[te], c_b], writes=[tmp_b[tl]])

            def sa_dve(i):
                g, kt, dg, first, last = flags(i)
                p, te, tl = S[i]["p"], S[i]["te"], S[i]["tl"]
                tb = rr("tmpb", NTMP); S[i]["tb"] = tb
                if dg >= 0:
                    cx.op("dve", lambda: V.scalar_tensor_tensor(out=tmp[te][:], in0=ps[p][:, :TT], scalar=-sc_s, in1=tmp[tl][:],
                                                               op0=ALU.mult, op1=ALU.subtract),
                          reads=[ps_b[p], tmp_b[tl]], writes=[tmp_b[te]])
                    cx.op("dve", lambda: V.tensor_tensor(out=tmpb[tb][:], in0=tmp[te][:], in1=masks_s[:, 2 + dg, :], op=ALU.mult),
                          reads=[tmp_b[te], c_b], writes=[tmpb_b[tb]])
                else:
                    cx.op("dve", lambda: V.scalar_tensor_tensor(out=tmpb[tb][:], in0=ps[p][:, :TT], scalar=-sc_s, in1=tmp[tl][:],
                                                               op0=ALU.mult, op1=ALU.subtract),
                          reads=[ps_b[p], tmp_b[tl]], writes=[tmpb_b[tb]])

            def sb_pe(i):
                tb = S[i]["tb"]
                p2 = next_ps(); S[i]["p2"] = p2
                cx.op("pe", lambda: nc.tensor.matmul(ps[p2][:, :TT], trib[:], tmpb[tb][:], start=True, stop=True),
                      reads=[tmpb_b[tb], c_b], writes=[ps_b[p2]])

            def sb_dve(i):
                p2, tl = S[i]["p2"], S[i]["tl"]
                cx.op("dve", lambda: V.tensor_tensor(out=tmp[tl][:], in0=ps[p2][:, :TT], in1=tmp[tl][:], op=ALU.subtract),
                      reads=[ps_b[p2], tmp_b[tl]], writes=[tmp_b[tl]])

            def sc(i):
                g, kt, dg, first, last = flags(i)
                tl, tb = S[i]["tl"], S[i]["tb"]
                if not first:
                    cx.op("dve", lambda: V.tensor_tensor(out=tmp[tl][:], in0=ps[PC][:, :TT], in1=tmp[tl][:], op=ALU.add),
                          reads=[ps_b[PC], tmp_b[tl]], writes=[tmp_b[tl]])
                if not last:
                    cx.op("pe", lambda: nc.tensor.matmul(ps[PC][:, :TT], onesb[:], tmpb[tb][:], start=first, stop=(kt == 1)),
                          reads=[tmpb_b[tb]], writes=[ps_b[PC]])
                ta = rr("tmpb", NTMP); S[i]["ta"] = ta
                cx.op("act", lambda: nc.scalar.activation(out=tmpb[ta][:], in_=tmp[tl][:], func=AF.Exp),
                      reads=[tmp_b[tl]], writes=[tmpb_b[ta]])
                if dg >= 0:
                    cx.op("dve", lambda: V.tensor_tensor(out=tmpb[ta][:], in0=tmpb[ta][:], in1=masks_bf[:, 2 + dg, :], op=ALU.mult),
                          reads=[tmpb_b[ta], c_b], writes=[tmpb_b[ta]])

            def sd(i):
                g, kt, dg, first, last = flags(i)
                ta = S[i]["ta"]
                cx.op("pe", lambda: nc.tensor.matmul(ps[PO][:, :TT], Vs[:, kt, :], tmpb[ta][:], start=first, stop=last),
                      reads=[tmpb_b[ta], kqv_b[2]], writes=[ps_b[PO]])
                if last:
                    cx.op("act", lambda: nc.scalar.copy(out=ofull[:], in_=ps[PO][:, :TT]), reads=[ps_b[PO]], writes=[ofull_b])
                    head_norm_store(hidx, g, 1)

            ok = lambda j: 0 <= j < n_t
            for i in range(-2, n_t + 1):
                if ok(i): sc(i)
                if ok(i - 1): sd(i - 1)
                if ok(i + 2): sa_pe(i + 2); sa_act(i + 2)
                if ok(i + 1): sb_pe(i + 1); sb_dve(i + 1)
                if ok(i + 2): sa_dve(i + 2)
            cx.end_iter()

        sc_x = 128 ** -0.5
        with nc.Fori(0, NTT) as i:
            tok0 = i * TT
            cx.dma(hT[:].rearrange("p c t -> p (c t)"), h1T[bass.ds(i * 128, 128), :], writes=hT_b)
            for n0 in range(0, NH, 16):
                n1 = min(NH, n0 + 16)
                cx.dma(actT[:, n0:n1, :], MT[n0 * 128:n1 * 128, bass.ds(tok0, TT)].rearrange("(n p) t -> p n t", p=128),
                       writes=actT_b[n0:n1])
            for c in range(DC):
                plan_lin(wout, c, NH, ktpH)
            for c in range(DC):
                p = next_ps()
                lin_chunk(wout, c, actT, actT_b, NH, ktpH, ps[p], ps_b[p])
                cx.op("act", lambda c=c, p=p: nc.scalar.copy(out=yT[:, c, :], in_=ps[p][:, :TT]), reads=[ps_b[p]], writes=[yT_b[c]])
            post_res(G(3), 1.0)
            norm_to(hT, hT_b, G(4), hnT, hnT_b)
            for h in range(XH):
                plan_lin(wq, h, DC, ktpD)
            for h in range(XH):
                p = next_ps()
                lin_chunk(wq, h, hnT, hnT_b, DC, ktpD, ps[p], ps_b[p])
                cx.op("act", lambda h=h, p=p: nc.scalar.copy(out=qx[:, h, :], in_=ps[p][:, :TT]), reads=[ps_b[p]], writes=[qx_b[h]])
            for h in range(XH):
                PO, PL = 4, 5
                nmt = MEM // 128
                for mt in range(nmt):
                    p = next_ps()
                    cx.op("pe", lambda p=p, h=h, mt=mt: nc.tensor.matmul(ps[p][:, :TT], KmT[:, h, mt * 128:(mt + 1) * 128], qx[:, h, :], start=True, stop=True),
                          reads=[km_b, qx_b[h]], writes=[ps_b[p]])
                    tb = rr("tmpb", NTMP)
                    cx.op("act", lambda p=p, tb=tb: nc.scalar.activation(out=tmpb[tb][:], in_=ps[p][:, :TT], func=AF.Exp, scale=sc_x),
                          reads=[ps_b[p]], writes=[tmpb_b[tb]])
                    cx.op("pe", lambda tb=tb, h=h, mt=mt: nc.tensor.matmul(ps[PO][:, :TT], Vm[:, mt, h * 128:(h + 1) * 128], tmpb[tb][:],
                                                                         start=(mt == 0), stop=(mt == nmt - 1)),
                          reads=[tmpb_b[tb], km_b], writes=[ps_b[PO]])
                    cx.op("pe", lambda tb=tb, mt=mt: nc.tensor.matmul(ps[PL][:, :TT], onesb[:], tmpb[tb][:], start=(mt == 0), stop=(mt == nmt - 1)),
                          reads=[tmpb_b[tb]], writes=[ps_b[PL]])
                t = rr("tmp", NTMP)
                cx.op("dve", lambda t=t: V.reciprocal(out=tmp[t][:], in_=ps[PL][:, :TT]), reads=[ps_b[PL]], writes=[tmp_b[t]])
                cx.op("dve", lambda t=t, h=h: V.tensor_tensor(out=ox[:, h, :], in0=ps[PO][:, :TT], in1=tmp[t][:], op=ALU.mult),
                      reads=[ps_b[PO], tmp_b[t]], writes=[ox_b[h]])
            for c in range(DC):
                plan_lin(wo, c, XH, XH)
            for c in range(DC):
                p = next_ps()
                lin_chunk(wo, c, ox, ox_b, XH, XH, ps[p], ps_b[p])
                cx.op("act", lambda c=c, p=p: nc.scalar.copy(out=yT[:, c, :], in_=ps[p][:, :TT]), reads=[ps_b[p]], writes=[yT_b[c]])
            post_res(G(5), 1.0)
            ffn(wg2, wu2, wd2, G(6), G(7))
            for ts in range(TS):
                for cg in range(max(1, DC // 4)):
                    ncg = min(4, DC)
                    p = next_ps()
                    cx.pe_group([lambda q=q, cg=cg, ts=ts, p=p: nc.tensor.transpose(
                        ps[p][:, q * 128:(q + 1) * 128], hT[:, cg * 4 + q, ts * 128:(ts + 1) * 128], ident[:]) for q in range(ncg)],
                        reads=[hT_b[cg * 4 + q] for q in range(ncg)], writes=[ps_b[p]])
                    cx.op("act", lambda cg=cg, p=p, ncg=ncg: nc.scalar.copy(out=xin[:, cg * 512:cg * 512 + ncg * 128], in_=ps[p][:, :ncg * 128]),
                          reads=[ps_b[p]], writes=yT_b)
                cx.dma(out[bass.ds(tok0 + ts * 128, 128), :], xin, reads=yT_b)
            cx.end_iter()
    return nc


def prep_inputs(cfg, inp, lambda_init):
    D, FF, SEQ, HD, HS, MEM, XH, TT = (cfg[k] for k in ["D", "FF", "SEQ", "HD", "HS", "MEM", "XH", "TT"])
    DC, FC, NH = D // 128, FF // 128, HD + HS
    ktpD, ktpF, ktpH = min(16, DC), min(8, FC // 2), min(16, NH)
    f = lambda a: np.ascontiguousarray(np.asarray(a, dtype=np.float32))
    m = {}
    m["x"] = f(inp["x"]).reshape(SEQ, D)
    m["mem"] = f(inp["mem"]).reshape(MEM, D)
    m["posb"] = np.ascontiguousarray(np.broadcast_to(np.asarray(inp["positions"]).reshape(1, SEQ).astype(np.int32), (128, SEQ)))
    gl = ["ffn1_norm_pre", "ffn1_norm_post", "mix_norm_pre", "mix_norm_post", "xattn_norm_pre", "xattn_norm_post",
          "ffn2_norm_pre", "ffn2_norm_post", "mem_norm"]
    m["gains"] = np.ascontiguousarray(np.concatenate([gain_t(f(inp[k])[0]) for k in gl], axis=1))
    m["hg"] = np.ascontiguousarray(np.stack([f(inp["diff_subln"])[0], f(inp["sb_norm"])[0]], axis=1))
    lv = np.concatenate([f(inp[k])[0] for k in ["lambda_q1", "lambda_k1", "lambda_q2", "lambda_k2"]])
    m["lamv"] = np.ascontiguousarray(np.broadcast_to(lv[None, :], (128, 256)))
    rt = np.zeros((128, 2), np.float32)
    inv = ROPE_THETA ** (-np.arange(0, 16, 2, dtype=np.float32) / 16)
    for base in (0, 64):
        rt[base:base + 8, 0] = inv / np.float32(2 * math.pi)
        rt[base + 8:base + 16, 0] = inv / np.float32(2 * math.pi)
        rt[base:base + 8, 1] = -1.0
        rt[base + 8:base + 16, 1] = 1.0
    m["ropet"] = rt
    kp = np.arange(128)[:, None]
    qf = np.arange(TT)[None, :]
    mk = [((kp + 128 * j) <= qf) for j in range(TT // 128)] + [((kp + 128 * j) < qf) for j in range(TT // 128)]
    m["masks"] = np.ascontiguousarray(np.concatenate(mk, axis=0).astype(np.float32))
    m["ident"] = np.eye(128, dtype=np.float32)
    m["tri"] = (np.arange(128)[:, None] > np.arange(128)[None, :]).astype(np.float32)
    m["wg1"] = tile_w(f(inp["ffn1_w_gate"])[0], ktpD); m["wu1"] = tile_w(f(inp["ffn1_w_up"])[0], ktpD)
    m["wd1"] = tile_w(f(inp["ffn1_w_down"])[0], ktpF)
    m["wg2"] = tile_w(f(inp["ffn2_w_gate"])[0], ktpD); m["wu2"] = tile_w(f(inp["ffn2_w_up"])[0], ktpD)
    m["wd2"] = tile_w(f(inp["ffn2_w_down"])[0], ktpF)
    w_in = f(inp["w_in"])[0]
    m["win"] = tile_w(w_in, ktpD)
    perm = np.arange(128)
    for base in (0, 64):
        perm[base:base + 8] = np.arange(base + 8, base + 16)
        perm[base + 8:base + 16] = np.arange(base, base + 8)
    cols = np.concatenate([h * 128 + perm for h in range(2 * HD)])
    m["wsw"] = tile_w(np.ascontiguousarray(w_in[:, cols]), ktpD)
    m["wout"] = tile_w(f(inp["w_out"])[0], ktpH)
    m["wq"] = tile_w(f(inp["xattn_w_q"])[0], ktpD)
    m["wkv"] = tile_w(f(inp["xattn_w_kv"])[0], ktpD)
    m["wo"] = tile_w(f(inp["xattn_w_o"])[0], XH)
    return m


def run(cfg, inp, lambda_init):
    nc = build(cfg, lambda_init)
    m = prep_inputs(cfg, inp, lambda_init)
    res = run_bass_kernel_spmd(nc, [m], core_ids=[0])
    return res.results[0]["out"]


def kernel(**inputs):
    lambda_init = 0.8 - 0.6 * math.exp(-0.3 * 0)
    o = run(FULL, inputs, lambda_init)
    return np.asarray(o, dtype=np.float32).reshape(1, FULL["SEQ"], FULL["D"])
```

```python
import math
import contextlib
import numpy as np
import concourse.bass as bass
import concourse.mybir as mybir
from concourse.bass_utils import run_bass_kernel_spmd

F32 = mybir.dt.float32
BF16 = mybir.dt.bfloat16
I32 = mybir.dt.int32
AF = mybir.ActivationFunctionType
ALU = mybir.AluOpType

FULL = dict(D=4096, FF=14336, SEQ=8192, HD=16, HS=16, MEM=256, XH=4, TT=256)
ROPE_THETA = 500000.0
EPS = 1e-6
NDMA = 16


class Buf:
    __slots__ = ("w", "r")

    def __init__(self):
        self.w = None
        self.r = []


class Ctx:
    def __init__(self, nc, es):
        self.nc = nc
        self.es = es
        self.eng = {"pe": nc.tensor, "act": nc.scalar, "dve": nc.vector, "pool": nc.gpsimd, "sp": nc.sync}
        self.sem = {k: es.enter_context(nc.semaphore("sem_" + k)) for k in ["pe", "act", "dve", "pool"]}
        self.dsem = [es.enter_context(nc.semaphore(f"dsem{i}")) for i in range(NDMA)]
        for i, s in enumerate(self.dsem):
            self.sem[f"d{i}"] = s
        self.bufs = []
        self.reset()

    def reset(self):
        self.cnt = {k: 0 for k in self.sem}
        self.waited = {e: {} for e in self.eng}
        self.dnext = 0
        for b in self.bufs:
            b.w = None
            b.r = []

    def buf(self):
        b = Buf()
        self.bufs.append(b)
        return b

    def bufs_n(self, n):
        return [self.buf() for _ in range(n)]

    def _deps(self, eng, reads, writes):
        deps = {}

        def add(tok, same_ok):
            if tok is None:
                return
            sm, v = tok
            if sm == eng and not same_ok:
                return
            if eng == "pe" and sm == "pe":
                return
            if deps.get(sm, 0) < v:
                deps[sm] = v

        for b in reads:
            add(b.w, True)
        for b in writes:
            add(b.w, False)
            for t in b.r:
                add(t, False)
        w = self.waited[eng]
        for sm, v in deps.items():
            if w.get(sm, 0) < v:
                self.eng[eng].wait_ge(self.sem[sm], v)
                w[sm] = v

    def _commit(self, tok, reads, writes):
        for b in reads:
            b.r.append(tok)
            if len(b.r) > 12:
                b.r = b.r[-12:] if False else self._compact(b.r)
        for b in writes:
            b.w = tok
            b.r = []

    @staticmethod
    def _compact(toks):
        best = {}
        for sm, v in toks:
            if best.get(sm, 0) < v:
                best[sm] = v
        return list(best.items())

    def op(self, eng, fn, reads=(), writes=()):
        self._deps(eng, reads, writes)
        ins = fn()
        self.cnt[eng] += 1
        ins.then_inc(self.sem[eng], 1)
        self._commit((eng, self.cnt[eng]), reads, writes)

    def pe_group(self, fns, reads=(), writes=()):
        self._deps("pe", reads, writes)
        ins = None
        for fn in fns:
            ins = fn()
        self.cnt["pe"] += 1
        ins.then_inc(self.sem["pe"], 1)
        self._commit(("pe", self.cnt["pe"]), reads, writes)

    def dma(self, out, in_, reads=(), writes=(), q="sp"):
        k = self.dnext
        self.dnext = (self.dnext + 1) % NDMA
        name = f"d{k}"
        prev = self.cnt[name]
        w = self.waited[q]
        if prev and w.get(name, 0) < prev:
            self.eng[q].wait_ge(self.sem[name], prev)
            w[name] = prev
        self._deps(q, reads, writes)
        self.eng[q].dma_start(out=out, in_=in_).then_inc(self.sem[name], 16)
        self.cnt[name] += 16
        self._commit((name, self.cnt[name]), reads, writes)

    def end_iter(self):
        w = self.waited["sp"]
        for i in range(NDMA):
            name = f"d{i}"
            if self.cnt[name] and w.get(name, 0) < self.cnt[name]:
                self.nc.sync.wait_ge(self.sem[name], self.cnt[name])
        for e in ["pe", "act", "dve", "pool"]:
            if self.cnt[e]:
                self.nc.sync.wait_ge(self.sem[e], self.cnt[e])
        self.nc.all_engine_barrier()
        for s in self.sem.values():
            self.nc.sync.sem_clear(s)
        self.nc.all_engine_barrier()
        self.reset()


def tile_w(W, ktp):
    K, N = W.shape
    kb = K // (128 * ktp)
    a = W.reshape(kb, ktp, 128, N // 128, 128)
    a = a.transpose(3, 0, 2, 1, 4)
    return np.ascontiguousarray(a).reshape(N // 128, kb, 128, ktp * 128)


def gain_t(g):
    g = np.asarray(g).reshape(-1)
    return np.ascontiguousarray(g.reshape(-1, 128).T)


def build(cfg, lambda_init):
    D, FF, SEQ, HD, HS, MEM, XH, TT = (cfg[k] for k in ["D", "FF", "SEQ", "HD", "HS", "MEM", "XH", "TT"])
    DC, FC, NTT, NH = D // 128, FF // 128, SEQ // TT, HD + HS
    TS = TT // 128
    assert MEM == TT
    ktpD = min(16, DC)
    ktpF = min(8, FC // 2)
    ktpH = min(16, NH)
    nc = bass.Bass("TRN2", target_bir_lowering=False)

    def din(name, shape, dt=F32):
        return nc.dram_tensor(name, list(shape), dt, kind="ExternalInput").ap()

    def dscr(name, shape, dt):
        return nc.dram_tensor(name, list(shape), dt, kind="Internal").ap()

    x = din("x", [SEQ, D])
    mem = din("mem", [MEM, D])
    posb = din("posb", [128, SEQ], I32)
    gains = din("gains", [128, 9 * DC])
    hg = din("hg", [128, 2])
    lamv = din("lamv", [128, 4 * 64])
    ropet = din("ropet", [128, 2])
    masks = din("masks", [4 * 128, TT])
    ident_d = din("ident", [128, 128])
    tri_d = din("tri", [128, 128])
    wg1 = din("wg1", [FC, DC // ktpD, 128, ktpD * 128])
    wu1 = din("wu1", [FC, DC // ktpD, 128, ktpD * 128])
    wd1 = din("wd1", [DC, FC // ktpF, 128, ktpF * 128])
    wg2 = din("wg2", [FC, DC // ktpD, 128, ktpD * 128])
    wu2 = din("wu2", [FC, DC // ktpD, 128, ktpD * 128])
    wd2 = din("wd2", [DC, FC // ktpF, 128, ktpF * 128])
    win = din("win", [6 * (NH // 2), DC // ktpD, 128, ktpD * 128])
    wsw = din("wsw", [2 * HD, DC // ktpD, 128, ktpD * 128])
    wout = din("wout", [DC, NH // ktpH, 128, ktpH * 128])
    wq = din("wq", [XH, DC // ktpD, 128, ktpD * 128])
    wkv = din("wkv", [2 * XH, DC // ktpD, 128, ktpD * 128])
    wo = din("wo", [DC, 1, 128, XH * 128])
    out = nc.dram_tensor("out", [SEQ, D], F32, kind="ExternalOutput").ap()

    h1T = dscr("h1T", [NTT * 128, DC * TT], F32)
    QT = dscr("QT", [NH * 128, SEQ], BF16)
    KT = dscr("KT", [NH * 128, SEQ], BF16)
    Vd = dscr("Vd", [NH * 128, SEQ], BF16)
    MT = dscr("MT", [NH * 128, SEQ], BF16)

    es = contextlib.ExitStack()
    with es:
        cx = Ctx(nc, es)

        def sb(name, shape, dt):
            return es.enter_context(nc.sbuf_tensor(name, list(shape), dt))

        def psum(name, dt=F32, n=512):
            return es.enter_context(nc.psum_tensor(name, [128, n], dt))

        hT = sb("hT", [128, DC, TT], F32); hT_b = cx.bufs_n(DC)
        yT = sb("yT", [128, DC, TT], F32); yT_b = cx.bufs_n(DC)
        hnT = sb("hnT", [128, DC, TT], BF16); hnT_b = cx.bufs_n(DC)
        NACT = max(FC // 2, NH, 3 * (SEQ // TT), 3 * NH, 2 * XH)
        actT = sb("actT", [128, NACT, TT], BF16); actT_b = cx.bufs_n(NACT)
        xin = yT[:].rearrange("p c t -> p (c t)")[:, 0:D]
        NSTG, NWB = 4, 4
        stg = [sb(f"stg{i}", [128, 16 * 128], F32) for i in range(NSTG)]; stg_b = cx.bufs_n(NSTG)
        wbf = [sb(f"wbf{i}", [128, 16 * 128], BF16) for i in range(NWB)]; wbf_b = cx.bufs_n(NWB)
        NTMP = 6
        tmp = [sb(f"tmp{i}", [128, TT], F32) for i in range(NTMP)]; tmp_b = cx.bufs_n(NTMP)
        tmpb = [sb(f"tmpb{i}", [128, TT], BF16) for i in range(NTMP)]; tmpb_b = cx.bufs_n(NTMP)
        rstd = sb("rstd", [128, TT], F32); rstd_b = cx.buf()
        lnv = sb("lnv", [128, TT], F32); lnv_b = cx.buf()
        gains_s = sb("gains_s", [128, 9 * DC], F32); c_b = cx.buf()
        hg_s = sb("hg_s", [128, 2], F32)
        hg2_s = sb("hg2_s", [128, 2], F32)
        lam_s = sb("lam_s", [128, 256], F32)
        lam_t = sb("lam_t", [128, 8], F32)
        rope_s = sb("rope_s", [128, 2], F32)
        pt = [sb(f"pt{i}", [128, 2 * TT], BF16) for i in range(4)]; pt_b = cx.bufs_n(4)
        masks_bf = sb("masks_bf", [128, 4, TT], BF16)
        ident = sb("ident_s", [128, 128], F32)
        identb = sb("identb", [128, 128], BF16)
        tri32 = sb("tri32", [128, 128], F32)
        trib = sb("trib", [128, 128], BF16)
        ones32 = sb("ones32", [128, 128], F32)
        onesb = sb("onesb", [128, 128], BF16)
        cst = sb("cst", [128, 4], F32)
        Ct = sb("Ct", [128, TT], F32); St = sb("St", [128, TT], F32); cs_b = cx.buf()
        posi = sb("posi", [128, TT], I32); posf = sb("posf", [128, TT], F32); pos_b = cx.buf()
        ki = sb("ki", [128, TT], I32); ki_b = cx.buf()
        KmT = sb("KmT", [128, XH, MEM], BF16); Vm = sb("Vm", [128, MEM // 128, XH * 128], BF16); km_b = cx.buf()
        qx = actT[:, 0:XH, :]; qx_b = actT_b[0:XH]
        ox = actT[:, XH:2 * XH, :]; ox_b = actT_b[XH:2 * XH]
        NQ = SEQ // TT
        KTs = actT[:, 0:NQ, :].rearrange("p c t -> p (c t)")
        QTs = actT[:, NQ:2 * NQ, :].rearrange("p c t -> p (c t)")
        Vs = actT[:, 2 * NQ:3 * NQ, :].rearrange("p c t -> p (c t)").rearrange("p (k d) -> p k d", d=128)
        kqv_b = cx.bufs_n(3)
        vst_b = cx.buf()
        ofull = sb("ofull", [128, TT], F32); ofull_b = cx.buf()
        NPS = 7
        ps = [psum(f"ps{i}") for i in range(NPS)]; ps_b = cx.bufs_n(NPS)
        psb = psum("psb", BF16, 1024); psb_b = cx.buf()
        psrr = [0]
        castrr = [0]
        CAST_ENG = ["dve", "act"]

        PSROT = [[0, 1, 2]]

        def next_ps():
            rot = PSROT[0]
            i = psrr[0] % len(rot)
            psrr[0] += 1
            return rot[i]

        ring = {"pt": 0, "tmp": 0, "tmpb": 0, "xin": 0, "ostg": 0, "vstg": 0}

        def rr(name, n):
            i = ring[name]
            ring[name] = (i + 1) % n
            return i

        G = lambda k: gains_s[:, k * DC:(k + 1) * DC]
        eps_c, one_c, npi_c = cst[:, 0:1], cst[:, 1:2], cst[:, 2:3]

        wf = {"q": [], "issued": 0, "consumed": 0, "n": 0}
        LOOK = 3

        def plan_lin(W, n, KT_, ktp, kb0=0):
            for kb in range(KT_ // ktp):
                wf["q"].append((W, n, kb0 + kb, ktp))

        def w_issue(upto):
            while wf["issued"] <= min(upto, len(wf["q"]) - 1):
                W, n, kb, ktp = wf["q"][wf["issued"]]
                i = wf["n"] + wf["issued"]
                s, b = i % NSTG, i % NWB
                cx.dma(stg[s][:, :ktp * 128], W[n, kb], writes=[stg_b[s]])
                if i % 2 == 0:
                    cx.op("dve", lambda s=s, b=b, ktp=ktp: nc.vector.tensor_copy(out=wbf[b][:, :ktp * 128], in_=stg[s][:, :ktp * 128]),
                          reads=[stg_b[s]], writes=[wbf_b[b]])
                else:
                    cx.op("act", lambda s=s, b=b, ktp=ktp: nc.scalar.copy(out=wbf[b][:, :ktp * 128], in_=stg[s][:, :ktp * 128]),
                          reads=[stg_b[s]], writes=[wbf_b[b]])
                wf["issued"] += 1

        def w_get(W, n, kb, ktp):
            if wf["consumed"] == len(wf["q"]):
                wf["q"].append((W, n, kb, ktp))
            e = wf["q"][wf["consumed"]]
            assert e[0] is W and e[1:] == (n, kb, ktp), ("weight plan mismatch", e[1:], (n, kb, ktp))
            w_issue(wf["consumed"] + LOOK)
            b = (wf["n"] + wf["consumed"]) % NWB
            wf["consumed"] += 1
            if wf["consumed"] == len(wf["q"]):
                wf["n"] += len(wf["q"]); wf["q"] = []; wf["issued"] = 0; wf["consumed"] = 0
            return b

        def lin_chunk(W, n, inT, in_b, KT_, ktp, pst, pst_b, kb0=0):
            nkb = KT_ // ktp
            for kb in range(nkb):
                b = w_get(W, n, kb0 + kb, ktp)
                fns = []
                for j in range(ktp):
                    k = kb * ktp + j
                    fns.append(lambda b=b, j=j, k=k: nc.tensor.matmul(
                        pst[:, :TT], wbf[b][:, j * 128:(j + 1) * 128], inT[:, k, :],
                        start=(k == 0), stop=(k == KT_ - 1)))
                cx.pe_group(fns, reads=[wbf_b[b]] + [in_b[kb * ktp + j] for j in range(ktp)], writes=[pst_b])

        def rms_stats(srcT, src_b, nch, div):
            a = 4
            for c in range(nch):
                t = rr("tmp", NTMP)
                cx.op("act", lambda c=c, t=t: nc.scalar.activation(out=tmp[t][:], in_=srcT[:, c, :], func=AF.Square),
                      reads=[src_b[c]], writes=[tmp_b[t]])
                cx.op("pe", lambda c=c, t=t: nc.tensor.matmul(ps[a][:, :TT], ones32[:], tmp[t][:],
                                                              start=(c == 0), stop=(c == nch - 1)),
                      reads=[tmp_b[t]], writes=[ps_b[a]])
            cx.op("act", lambda: nc.scalar.activation(out=lnv[:], in_=ps[a][:, :TT], func=AF.Ln, scale=1.0 / div, bias=eps_c),
                  reads=[ps_b[a]], writes=[lnv_b])
            cx.op("act", lambda: nc.scalar.activation(out=rstd[:], in_=lnv[:], func=AF.Exp, scale=-0.5),
                  reads=[lnv_b], writes=[rstd_b])

        def norm_to(srcT, src_b, gain, dstT, dst_b):
            rms_stats(srcT, src_b, DC, float(D))
            for c in range(DC):
                cx.op("dve", lambda c=c: nc.vector.scalar_tensor_tensor(
                    out=dstT[:, c, :], in0=srcT[:, c, :], scalar=gain[:, c:c + 1], in1=rstd[:],
                    op0=ALU.mult, op1=ALU.mult), reads=[src_b[c], rstd_b], writes=[dst_b[c]])

        def post_res(gain, factor):
            rms_stats(yT, yT_b, DC, float(D))
            for c in range(DC):
                t = rr("tmp", NTMP)
                cx.op("dve", lambda c=c, t=t: nc.vector.scalar_tensor_tensor(
                    out=tmp[t][:], in0=yT[:, c, :], scalar=gain[:, c:c + 1], in1=rstd[:],
                    op0=ALU.mult, op1=ALU.mult), reads=[yT_b[c], rstd_b], writes=[tmp_b[t]])
                cx.op("dve", lambda c=c, t=t: nc.vector.scalar_tensor_tensor(
                    out=hT[:, c, :], in0=tmp[t][:], scalar=float(factor), in1=hT[:, c, :],
                    op0=ALU.mult, op1=ALU.add), reads=[tmp_b[t], hT_b[c]], writes=[hT_b[c]])

        def load_T(src, row0, dstT, dst_b, ntok_tiles):
            for ts in range(ntok_tiles):
                cx.dma(xin, src[bass.ds(row0 + ts * 128, 128), :], writes=yT_b)
                for cg in range(max(1, DC // 4)):
                    ncg = min(4, DC)
                    p = next_ps()
                    cx.pe_group([lambda q=q, p=p, cg=cg: nc.tensor.transpose(
                        ps[p][:, q * 128:(q + 1) * 128], xin[:, (cg * 4 + q) * 128:(cg * 4 + q + 1) * 128], ident[:]) for q in range(ncg)],
                        reads=yT_b, writes=[ps_b[p]])
                    for q in range(ncg):
                        c = cg * 4 + q
                        cx.op("act", lambda q=q, c=c, p=p, ts=ts: nc.scalar.copy(
                            out=dstT[:, c, ts * 128:(ts + 1) * 128], in_=ps[p][:, q * 128:(q + 1) * 128]),
                            reads=[ps_b[p]], writes=[dst_b[c]])

        def ffn(Wg, Wu, Wd, gpre, gpost):
            FH = FC // 2
            for half in range(2):
                for n in range(FH):
                    plan_lin(Wg, half * FH + n, DC, ktpD)
                    plan_lin(Wu, half * FH + n, DC, ktpD)
                for c in range(DC):
                    plan_lin(Wd, c, FH, ktpF, kb0=half * (FH // ktpF))
            norm_to(hT, hT_b, gpre, hnT, hnT_b)
            for half in range(2):
                for n in range(FH):
                    pg, pu = next_ps(), next_ps()
                    lin_chunk(Wg, half * FH + n, hnT, hnT_b, DC, ktpD, ps[pg], ps_b[pg])
                    lin_chunk(Wu, half * FH + n, hnT, hnT_b, DC, ktpD, ps[pu], ps_b[pu])
                    t = rr("tmp", NTMP)
                    cx.op("act", lambda pg=pg, t=t: nc.scalar.activation(out=tmp[t][:], in_=ps[pg][:, :TT], func=AF.Silu),
                          reads=[ps_b[pg]], writes=[tmp_b[t]])
                    cx.op("dve", lambda pu=pu, t=t, n=n: nc.vector.tensor_tensor(
                        out=actT[:, n, :], in0=tmp[t][:], in1=ps[pu][:, :TT], op=ALU.mult),
                        reads=[tmp_b[t], ps_b[pu]], writes=[actT_b[n]])
                for c in range(DC):
                    p = next_ps()
                    lin_chunk(Wd, c, actT, actT_b, FH, ktpF, ps[p], ps_b[p], kb0=half * (FH // ktpF))
                    if half == 0:
                        cx.op("act", lambda c=c, p=p: nc.scalar.copy(out=yT[:, c, :], in_=ps[p][:, :TT]),
                              reads=[ps_b[p]], writes=[yT_b[c]])
                    else:
                        cx.op("dve", lambda c=c, p=p: nc.vector.tensor_tensor(out=yT[:, c, :], in0=ps[p][:, :TT], in1=yT[:, c, :], op=ALU.add),
                              reads=[ps_b[p], yT_b[c]], writes=[yT_b[c]])
            post_res(gpost, 0.5)

        def to_tokmajor_store(srcb_ap, src_buf, dram_rows_fn):
            for ts in range(TS):
                cx.op("pe", lambda ts=ts: nc.tensor.transpose(psb[:, ts * 128:(ts + 1) * 128],
                                                              srcb_ap[:, ts * 128:(ts + 1) * 128], identb[:]),
                      reads=[src_buf], writes=[psb_b])
                v = rr("vstg", 2)
                cx.op("act", lambda ts=ts, v=v: nc.scalar.copy(out=vstg[v][:], in_=psb[:, ts * 128:(ts + 1) * 128]),
                      reads=[psb_b], writes=[vstg_b[v]])
                cx.dma(dram_rows_fn(ts), vstg[v][:], reads=[vstg_b[v]])

        cx.dma(gains_s[:], gains[:, :], writes=[c_b])
        cx.dma(hg_s[:], hg[:, :], writes=[c_b])
        cx.dma(lam_s[:], lamv[:, :], writes=[c_b])
        cx.dma(rope_s[:], ropet[:, :], writes=[c_b])
        cx.dma(ident[:], ident_d[:, :], writes=[c_b])
        cx.dma(tri32[:], tri_d[:, :], writes=[c_b])
        V = nc.vector
        cx.op("dve", lambda: V.memset(ones32[:], 1.0), writes=[c_b])
        cx.op("dve", lambda: V.memset(onesb[:], 1.0), writes=[c_b])
        cx.op("dve", lambda: V.memset(cst[:, 0:1], EPS), writes=[c_b])
        cx.op("dve", lambda: V.memset(cst[:, 1:2], 1.0), writes=[c_b])
        cx.op("dve", lambda: V.memset(cst[:, 2:3], -math.pi), writes=[c_b])
        cx.op("dve", lambda: V.memset(cst[:, 3:4], 0.0), writes=[c_b])
        cx.op("dve", lambda: V.tensor_copy(out=identb[:], in_=ident[:]), reads=[c_b], writes=[c_b])
        cx.op("dve", lambda: V.tensor_copy(out=trib[:], in_=tri32[:]), reads=[c_b], writes=[c_b])
        for j in range(4):
            cx.dma(tmp[j][:], masks[j * 128:(j + 1) * 128, :], writes=[tmp_b[j]])
            cx.op("dve", lambda j=j: V.tensor_copy(out=masks_bf[:, j, :], in_=tmp[j][:]), reads=[tmp_b[j]], writes=[c_b])
        cx.op("dve", lambda: V.tensor_tensor(out=lam_s[:, 0:64], in0=lam_s[:, 0:64], in1=lam_s[:, 64:128], op=ALU.mult), reads=[c_b], writes=[c_b])
        cx.op("dve", lambda: V.tensor_tensor(out=lam_s[:, 128:192], in0=lam_s[:, 128:192], in1=lam_s[:, 192:256], op=ALU.mult), reads=[c_b], writes=[c_b])
        cx.op("dve", lambda: V.tensor_reduce(out=lam_t[:, 0:1], in_=lam_s[:, 0:64], op=ALU.add, axis=mybir.AxisListType.X), reads=[c_b], writes=[c_b])
        cx.op("dve", lambda: V.tensor_reduce(out=lam_t[:, 1:2], in_=lam_s[:, 128:192], op=ALU.add, axis=mybir.AxisListType.X), reads=[c_b], writes=[c_b])
        cx.op("act", lambda: nc.scalar.activation(out=lam_t[:, 2:4], in_=lam_t[:, 0:2], func=AF.Exp), reads=[c_b], writes=[c_b])
        cx.op("dve", lambda: V.tensor_tensor(out=lam_t[:, 4:5], in0=lam_t[:, 3:4], in1=lam_t[:, 2:3], op=ALU.subtract), reads=[c_b], writes=[c_b])
        cx.op("dve", lambda: V.tensor_scalar(out=lam_t[:, 5:6], in0=lam_t[:, 4:5], scalar1=-float(lambda_init), scalar2=None, op0=ALU.add), reads=[c_b], writes=[c_b])
        neglam = lam_t[:, 5:6]
        cx.op("dve", lambda: V.tensor_scalar(out=hg2_s[:, 0:1], in0=hg_s[:, 0:1], scalar1=float(1.0 - lambda_init), scalar2=None, op0=ALU.mult), reads=[c_b], writes=[c_b])
        cx.op("dve", lambda: V.tensor_copy(out=hg2_s[:, 1:2], in_=hg_s[:, 1:2]), reads=[c_b], writes=[c_b])

        load_T(mem, 0, hT, hT_b, MEM // 128)
        norm_to(hT, hT_b, G(8), hnT, hnT_b)
        for h in range(XH):
            p = next_ps()
            lin_chunk(wkv, h, hnT, hnT_b, DC, ktpD, ps[p], ps_b[p])
            cx.op("act", lambda h=h, p=p: nc.scalar.copy(out=KmT[:, h, :], in_=ps[p][:, :MEM]), reads=[ps_b[p]], writes=[km_b])
        for h in range(XH):
            p = next_ps()
            lin_chunk(wkv, XH + h, hnT, hnT_b, DC, ktpD, ps[p], ps_b[p])
            t = rr("tmpb", NTMP)
            cx.op("act", lambda p=p, t=t: nc.scalar.copy(out=tmpb[t][:], in_=ps[p][:, :MEM]), reads=[ps_b[p]], writes=[tmpb_b[t]])
            for mt in range(MEM // 128):
                cx.op("pe", lambda t=t, mt=mt: nc.tensor.transpose(psb[:, mt * 128:(mt + 1) * 128], tmpb[t][:, mt * 128:(mt + 1) * 128], identb[:]),
                      reads=[tmpb_b[t]], writes=[psb_b])
                cx.op("act", lambda h=h, mt=mt: nc.scalar.copy(out=Vm[:, mt, h * 128:(h + 1) * 128], in_=psb[:, mt * 128:(mt + 1) * 128]),
                      reads=[psb_b], writes=[km_b])
        cx.end_iter()

        with nc.Fori(0, NTT) as i:
            tok0 = i * TT
            load_T(x, tok0, hT, hT_b, TS)
            ffn(wg1, wu1, wd1, G(0), G(1))
            norm_to(hT, hT_b, G(2), hnT, hnT_b)
            cx.dma(h1T[bass.ds(i * 128, 128), :], hT[:].rearrange("p c t -> p (c t)"), reads=hT_b)
            cx.dma(posi[:], posb[:, bass.ds(tok0, TT)], writes=[pos_b])
            cx.op("dve", lambda: V.tensor_copy(out=posf[:], in_=posi[:]), reads=[pos_b], writes=[pos_b])
            for which, dst in ((0, St), (1, Ct)):
                t = rr("tmp", NTMP)
                cx.op("dve", lambda t=t, which=which: V.tensor_scalar(
                    out=tmp[t][:], in0=posf[:], scalar1=rope_s[:, 0:1], scalar2=(0.5 if which == 0 else 0.75),
                    op0=ALU.mult, op1=ALU.add), reads=[pos_b, c_b], writes=[tmp_b[t]])
                cx.op("dve", lambda t=t: V.tensor_copy(out=ki[:], in_=tmp[t][:]), reads=[tmp_b[t]], writes=[ki_b])
                t2 = rr("tmp", NTMP)
                cx.op("dve", lambda t2=t2: V.tensor_copy(out=tmp[t2][:], in_=ki[:]), reads=[ki_b], writes=[tmp_b[t2]])
                cx.op("dve", lambda t=t, t2=t2: V.tensor_tensor(out=tmp[t][:], in0=tmp[t][:], in1=tmp[t2][:], op=ALU.subtract),
                      reads=[tmp_b[t], tmp_b[t2]], writes=[tmp_b[t]])
                cx.op("dve", lambda t=t, t2=t2: V.scalar_tensor_tensor(out=tmp[t2][:], in0=tmp[t][:], scalar=0.0, in1=tmp[t][:],
                                                                      op0=ALU.is_lt, op1=ALU.add),
                      reads=[tmp_b[t]], writes=[tmp_b[t2]])
                cx.op("act", lambda t2=t2, dst=dst: nc.scalar.activation(out=dst[:], in_=tmp[t2][:], func=AF.Sin,
                                                                        scale=2.0 * math.pi, bias=npi_c),
                      reads=[tmp_b[t2], c_b], writes=[cs_b])
            cx.op("dve", lambda: V.tensor_scalar(out=St[:], in0=St[:], scalar1=rope_s[:, 1:2], scalar2=None, op0=ALU.mult),
                  reads=[cs_b, c_b], writes=[cs_b])
            for grp in range(6):
                for hh in range(HD if grp < 3 else HS):
                    plan_lin(win, (grp * HD + hh) if grp < 3 else (3 * HD + (grp - 3) * HS + hh), DC, ktpD)
                    if grp in (0, 1):
                        plan_lin(wsw, grp * HD + hh, DC, ktpD)
            Vst = actT[:, 2 * NH:3 * NH, :].rearrange("p c t -> p (c t)").rearrange("p (s n d) -> p s n d", s=TS, n=NH, d=128)
            for grp in range(6):
                nh = HD if grp < 3 else HS
                for hh in range(nh):
                    n = (grp * HD + hh) if grp < 3 else (3 * HD + (grp - 3) * HS + hh)
                    head = hh if grp < 3 else HD + hh
                    p = next_ps()
                    lin_chunk(win, n, hnT, hnT_b, DC, ktpD, ps[p], ps_b[p])
                    if grp in (0, 1):
                        slot = head if grp == 0 else NH + head
                        p2 = next_ps()
                        lin_chunk(wsw, grp * HD + hh, hnT, hnT_b, DC, ktpD, ps[p2], ps_b[p2])
                        t = rr("tmp", NTMP); t2 = rr("tmp", NTMP)
                        cx.op("dve", lambda t=t, p=p: V.tensor_tensor(out=tmp[t][:], in0=ps[p][:, :TT], in1=Ct[:], op=ALU.mult),
                              reads=[ps_b[p], cs_b], writes=[tmp_b[t]])
                        cx.op("dve", lambda t2=t2, p2=p2: V.tensor_tensor(out=tmp[t2][:], in0=ps[p2][:, :TT], in1=St[:], op=ALU.mult),
                              reads=[ps_b[p2], cs_b], writes=[tmp_b[t2]])
                        cx.op("dve", lambda t=t, t2=t2, slot=slot: V.tensor_tensor(out=actT[:, slot, :], in0=tmp[t][:], in1=tmp[t2][:], op=ALU.add),
                              reads=[tmp_b[t], tmp_b[t2]], writes=[actT_b[slot]])
                    elif grp in (3, 4):
                        slot = head if grp == 3 else NH + head
                        cx.op("act", lambda p=p, slot=slot: nc.scalar.copy(out=actT[:, slot, :], in_=ps[p][:, :TT]),
                              reads=[ps_b[p]], writes=[actT_b[slot]])
                    else:
                        tb = rr("tmpb", NTMP)
                        cx.op("act", lambda p=p, tb=tb: nc.scalar.copy(out=tmpb[tb][:], in_=ps[p][:, :TT]),
                              reads=[ps_b[p]], writes=[tmpb_b[tb]])
                        for ts in range(TS):
                            cx.op("pe", lambda ts=ts, tb=tb: nc.tensor.transpose(psb[:, ts * 128:(ts + 1) * 128],
                                                                              tmpb[tb][:, ts * 128:(ts + 1) * 128], identb[:]),
                                  reads=[tmpb_b[tb]], writes=[psb_b])
                            cx.op("act", lambda ts=ts, head=head: nc.scalar.copy(out=actT[:, 2 * NH + head, ts * 128:(ts + 1) * 128],
                                                                               in_=psb[:, ts * 128:(ts + 1) * 128]),
                                  reads=[psb_b], writes=[actT_b[2 * NH + head]])
            HB = min(16, NH)
            for n0 in range(0, NH, HB):
                for which, (dst, qq) in enumerate(((QT, "sp"), (KT, "act"), (Vd, "act"))):
                    cx.dma(dst[n0 * 128:(n0 + HB) * 128, bass.ds(tok0, TT)].rearrange("(n p) t -> p n t", p=128),
                           actT[:, which * NH + n0:which * NH + n0 + HB, :],
                           reads=actT_b[which * NH + n0:which * NH + n0 + HB], q=qq)
            cx.end_iter()

        NQG = SEQ // TT
        assert NQG <= DC
        sc_d = 64 ** -0.5
        sc_s = 128 ** -0.5

        def load_head(hidx):
            cx.dma(KTs, KT[bass.ds(hidx * 128, 128), :], writes=[kqv_b[0]])
            cx.dma(QTs, QT[bass.ds(hidx * 128, 128), :], writes=[kqv_b[1]])
            cx.dma(actT[:, 2 * NQ:3 * NQ, :].rearrange("p c t -> p (c t)"), Vd[bass.ds(hidx * 128, 128), :], writes=[kqv_b[2]])

        def head_norm_store(hidx, g, gcol, a=3):
            t = rr("tmp", NTMP)
            cx.op("act", lambda t=t: nc.scalar.activation(out=tmp[t][:], in_=ofull[:], func=AF.Square),
                  reads=[ofull_b], writes=[tmp_b[t]])
            cx.op("pe", lambda t=t: nc.tensor.matmul(ps[a][:, :TT], ones32[:], tmp[t][:], start=True, stop=True),
                  reads=[tmp_b[t]], writes=[ps_b[a]])
            cx.op("act", lambda: nc.scalar.activation(out=lnv[:], in_=ps[a][:, :TT], func=AF.Ln, scale=1.0 / 128, bias=eps_c),
                  reads=[ps_b[a]], writes=[lnv_b])
            cx.op("act", lambda: nc.scalar.activation(out=rstd[:], in_=lnv[:], func=AF.Exp, scale=-0.5),
                  reads=[lnv_b], writes=[rstd_b])
            cx.op("dve", lambda: V.scalar_tensor_tensor(out=hnT[:, g, :], in0=ofull[:], scalar=hg2_s[:, gcol:gcol + 1],
                                                       in1=rstd[:], op0=ALU.mult, op1=ALU.mult),
                  reads=[ofull_b, rstd_b, c_b], writes=[hnT_b[g]])
            if g == NQG - 1:
                cx.dma(MT[bass.ds(hidx * 128, 128), :], hnT[:, 0:NQG, :].rearrange("p c t -> p (c t)"), reads=hnT_b[0:NQG])

        def skew(tasks, stageA, stageB, lag):
            n = len(tasks)
            st = {}
            for i in range(n + lag):
                if i < n:
                    st[i] = stageA(tasks[i])
                j = i - lag
                if j >= 0:
                    stageB(tasks[j], st.pop(j))

        with nc.Fori(0, HD) as hd:
            load_head(hd)
            tasks = [(g, kt) for g in range(NQG) for kt in range((g + 1) * TS)]
            PSROT[0] = [0, 1, 2, 5, 6]

            def dA(task):
                g, kt = task
                qs = slice(g * TT, (g + 1) * TT); ks = slice(kt * 128, (kt + 1) * 128)
                dg = kt - g * TS
                k = rr("pt", 4)
                for comp in range(2):
                    p = next_ps()
                    lo = comp * 64
                    cx.op("pe", lambda: nc.tensor.matmul(ps[p][:, :TT], KTs[lo:lo + 64, ks], QTs[lo:lo + 64, qs], start=True, stop=True),
                          reads=[kqv_b[0], kqv_b[1]], writes=[ps_b[p]])
                    cx.op("act", lambda: nc.scalar.activation(out=pt[k][:, comp * TT:(comp + 1) * TT], in_=ps[p][:, :TT], func=AF.Exp, scale=sc_d),
                          reads=[ps_b[p]], writes=[pt_b[k]])
                    if dg >= 0:
                        cx.op("dve", lambda: V.tensor_tensor(out=pt[k][:, comp * TT:(comp + 1) * TT], in0=pt[k][:, comp * TT:(comp + 1) * TT],
                                                             in1=masks_bf[:, dg, :], op=ALU.mult),
                              reads=[pt_b[k], c_b], writes=[pt_b[k]])
                return k

            def dB(task, k):
                g, kt = task
                nkt = (g + 1) * TS
                cx.op("pe", lambda: nc.tensor.matmul(ps[3][:, :2 * TT], Vs[:, kt, :], pt[k][:], start=(kt == 0), stop=(kt == nkt - 1)),
                      reads=[pt_b[k], kqv_b[2]], writes=[ps_b[3]])
                cx.op("pe", lambda: nc.tensor.matmul(ps[4][:, :2 * TT], onesb[:], pt[k][:], start=(kt == 0), stop=(kt == nkt - 1)),
                      reads=[pt_b[k]], writes=[ps_b[4]])
                if kt == nkt - 1:
                    t1 = rr("tmp", NTMP); t2 = rr("tmp", NTMP)
                    for (c, t) in ((0, t1), (1, t2)):
                        cx.op("dve", lambda: V.reciprocal(out=tmp[t][:], in_=ps[4][:, c * TT:(c + 1) * TT]), reads=[ps_b[4]], writes=[tmp_b[t]])
                        cx.op("dve", lambda: V.tensor_tensor(out=tmp[t][:], in0=ps[3][:, c * TT:(c + 1) * TT], in1=tmp[t][:], op=ALU.mult),
                              reads=[ps_b[3], tmp_b[t]], writes=[tmp_b[t]])
                    cx.op("dve", lambda: V.scalar_tensor_tensor(out=ofull[:], in0=tmp[t2][:], scalar=neglam, in1=tmp[t1][:],
                                                               op0=ALU.mult, op1=ALU.add),
                          reads=[tmp_b[t1], tmp_b[t2], c_b], writes=[ofull_b])
                    head_norm_store(hd, g, 0, a=0)

            skew(tasks, dA, dB, 2)
            PSROT[0] = [0, 1, 2]
            cx.end_iter()

        with nc.Fori(0, HS) as hs:
            hidx = hs + HD
            load_head(hidx)
            PO, PC = 4, 5
            PSROT[0] = [0, 1, 2, 6]
            tasks = [(g, kt) for g in range(NQG) for kt in range((g + 1) * TS - 1, -1, -1)]

            n_t = len(tasks)
            S = [dict() for _ in range(n_t)]

            def flags(i):
                g, kt = tasks[i]
                nkt = (g + 1) * TS
                return g, kt, kt - g * TS, (kt == nkt - 1), (kt == 0)

            def sa_pe(i):
                g, kt, dg, first, last = flags(i)
                p = next_ps(); S[i]["p"] = p
                cx.op("pe", lambda: nc.tensor.matmul(ps[p][:, :TT], KTs[:, kt * 128:(kt + 1) * 128], QTs[:, g * TT:(g + 1) * TT],
                                                     start=True, stop=True),
                      reads=[kqv_b[0], kqv_b[1]], writes=[ps_b[p]])

            def sa_act(i):
                p = S[i]["p"]
                te = rr("tmp", NTMP); tl = rr("tmp", NTMP)
                S[i]["te"], S[i]["tl"] = te, tl
                cx.op("act", lambda: nc.scalar.activation(out=tmp[te][:], in_=ps[p][:, :TT], func=AF.Exp, scale=-sc_s),
                      reads=[ps_b[p]], writes=[tmp_b[te]])
                cx.op("act", lambda: nc.scalar.activation(out=tmp[tl][:], in_=tmp[te][:], func=AF.Ln, scale=1.0, bias=one_c),
                      reads=[tmp_b[te], c_b], writes=[tmp_b[tl]])

            def sa_dve(i):
                g, kt, dg, first, last = flags(i)
                p, te, tl = S[i]["p"], S[i]["te"], S[i]["tl"]
                tb = rr("tmpb", NTMP); S[i]["tb"] = tb
                if dg >= 0:
                    cx.op("dve", lambda: V.scalar_tensor_tensor(out=tmp[te][:], in0=ps[p][:, :TT], scalar=-sc_s, in1=tmp[tl][:],
                                                               op0=ALU.mult, op1=ALU.subtract),
                          reads=[ps_b[p], tmp_b[tl]], writes=[tmp_b[te]])
                    cx.op("dve", lambda: V.tensor_tensor(out=tmpb[tb][:], in0=tmp[te][:], in1=masks_bf[:, 2 + dg, :], op=ALU.mult),
                          reads=[tmp_b[te], c_b], writes=[tmpb_b[tb]])
                else:
                    cx.op("dve", lambda: V.scalar_tensor_tensor(out=tmpb[tb][:], in0=ps[p][:, :TT], scalar=-sc_s, in1=tmp[tl][:],
                                                               op0=ALU.mult, op1=ALU.subtract),
                          reads=[ps_b[p], tmp_b[tl]], writes=[tmpb_b[tb]])

            def sb_pe(i):
                tb = S[i]["tb"]
                p2 = next_ps(); S[i]["p2"] = p2
                cx.op("pe", lambda: nc.tensor.matmul(ps[p2][:, :TT], trib[:], tmpb[tb][:], start=True, stop=True),
                      reads=[tmpb_b[tb], c_b], writes=[ps_b[p2]])

            def sb_dve(i):
                p2, tl = S[i]["p2"], S[i]["tl"]
                cx.op("dve", lambda: V.tensor_tensor(out=tmp[tl][:], in0=ps[p2][:, :TT], in1=tmp[tl][:], op=ALU.subtract),
                      reads=[ps_b[p2], tmp_b[tl]], writes=[tmp_b[tl]])

            def sc(i):
                g, kt, dg, first, last = flags(i)
                tl, tb = S[i]["tl"], S[i]["tb"]
                if not first:
                    cx.op("dve", lambda: V.tensor_tensor(out=tmp[tl][:], in0=ps[PC][:, :TT], in1=tmp[tl][:], op=ALU.add),
                          reads=[ps_b[PC], tmp_b[tl]], writes=[tmp_b[tl]])
                if not last:
                    cx.op("pe", lambda: nc.tensor.matmul(ps[PC][:, :TT], onesb[:], tmpb[tb][:], start=first, stop=(kt == 1)),
                          reads=[tmpb_b[tb]], writes=[ps_b[PC]])
                ta = rr("tmpb", NTMP); S[i]["ta"] = ta
                cx.op("act", lambda: nc.scalar.activation(out=tmpb[ta][:], in_=tmp[tl][:], func=AF.Exp),
                      reads=[tmp_b[tl]], writes=[tmpb_b[ta]])
                if dg >= 0:
                    cx.op("dve", lambda: V.tensor_tensor(out=tmpb[ta][:], in0=tmpb[ta][:], in1=masks_bf[:, 2 + dg, :], op=ALU.mult),
                          reads=[tmpb_b[ta], c_b], writes=[tmpb_b[ta]])

            def sd(i):
                g, kt, dg, first, last = flags(i)
                ta = S[i]["ta"]
                cx.op("pe", lambda: nc.tensor.matmul(ps[PO][:, :TT], Vs[:, kt, :], tmpb[ta][:], start=first, stop=last),
                      reads=[tmpb_b[ta], kqv_b[2]], writes=[ps_b[PO]])
                if last:
                    cx.op("act", lambda: nc.scalar.copy(out=ofull[:], in_=ps[PO][:, :TT]), reads=[ps_b[PO]], writes=[ofull_b])
                    head_norm_store(hidx, g, 1)

            ok = lambda j: 0 <= j < n_t

            def sc_cadd(i):
                g, kt, dg, first, last = flags(i)
                tl, tb = S[i]["tl"], S[i]["tb"]
                if not first:
                    cx.op("dve", lambda: V.tensor_tensor(out=tmp[tl][:], in0=ps[PC][:, :TT], in1=tmp[tl][:], op=ALU.add),
                          reads=[ps_b[PC], tmp_b[tl]], writes=[tmp_b[tl]])
                if not last:
                    cx.op("pe", lambda: nc.tensor.matmul(ps[PC][:, :TT], onesb[:], tmpb[tb][:], start=first, stop=(kt == 1)),
                          reads=[tmpb_b[tb]], writes=[ps_b[PC]])

            def sc_exp(i):
                g, kt, dg, first, last = flags(i)
                tl = S[i]["tl"]
                ta = rr("tmpb", NTMP); S[i]["ta"] = ta
                cx.op("act", lambda: nc.scalar.activation(out=tmpb[ta][:], in_=tmp[tl][:], func=AF.Exp),
                      reads=[tmp_b[tl]], writes=[tmpb_b[ta]])
                if dg >= 0:
                    cx.op("dve", lambda: V.tensor_tensor(out=tmpb[ta][:], in0=tmpb[ta][:], in1=masks_bf[:, 2 + dg, :], op=ALU.mult),
                          reads=[tmpb_b[ta], c_b], writes=[tmpb_b[ta]])

            for i in range(-3, n_t + 1):
                if ok(i): sc_cadd(i)
                if ok(i + 2): sa_act(i + 2)
                if ok(i - 1): sd(i - 1)
                if ok(i + 3): sa_pe(i + 3)
                if ok(i + 1): sb_pe(i + 1); sb_dve(i + 1)
                if ok(i + 2): sa_dve(i + 2)
                if ok(i): sc_exp(i)
            PSROT[0] = [0, 1, 2]
            cx.end_iter()

        sc_x = 128 ** -0.5
        with nc.Fori(0, NTT) as i:
            tok0 = i * TT
            cx.dma(hT[:].rearrange("p c t -> p (c t)"), h1T[bass.ds(i * 128, 128), :], writes=hT_b)
            for n0 in range(0, NH, 16):
                n1 = min(NH, n0 + 16)
                cx.dma(actT[:, n0:n1, :], MT[n0 * 128:n1 * 128, bass.ds(tok0, TT)].rearrange("(n p) t -> p n t", p=128),
                       writes=actT_b[n0:n1])
            for c in range(DC):
                plan_lin(wout, c, NH, ktpH)
            for c in range(DC):
                p = next_ps()
                lin_chunk(wout, c, actT, actT_b, NH, ktpH, ps[p], ps_b[p])
                cx.op("act", lambda c=c, p=p: nc.scalar.copy(out=yT[:, c, :], in_=ps[p][:, :TT]), reads=[ps_b[p]], writes=[yT_b[c]])
            post_res(G(3), 1.0)
            norm_to(hT, hT_b, G(4), hnT, hnT_b)
            for h in range(XH):
                plan_lin(wq, h, DC, ktpD)
            for h in range(XH):
                p = next_ps()
                lin_chunk(wq, h, hnT, hnT_b, DC, ktpD, ps[p], ps_b[p])
                cx.op("act", lambda h=h, p=p: nc.scalar.copy(out=qx[:, h, :], in_=ps[p][:, :TT]), reads=[ps_b[p]], writes=[qx_b[h]])
            for h in range(XH):
                PO, PL = 4, 5
                nmt = MEM // 128
                for mt in range(nmt):
                    p = next_ps()
                    cx.op("pe", lambda p=p, h=h, mt=mt: nc.tensor.matmul(ps[p][:, :TT], KmT[:, h, mt * 128:(mt + 1) * 128], qx[:, h, :], start=True, stop=True),
                          reads=[km_b, qx_b[h]], writes=[ps_b[p]])
                    tb = rr("tmpb", NTMP)
                    cx.op("act", lambda p=p, tb=tb: nc.scalar.activation(out=tmpb[tb][:], in_=ps[p][:, :TT], func=AF.Exp, scale=sc_x),
                          reads=[ps_b[p]], writes=[tmpb_b[tb]])
                    cx.op("pe", lambda tb=tb, h=h, mt=mt: nc.tensor.matmul(ps[PO][:, :TT], Vm[:, mt, h * 128:(h + 1) * 128], tmpb[tb][:],
                                                                         start=(mt == 0), stop=(mt == nmt - 1)),
                          reads=[tmpb_b[tb], km_b], writes=[ps_b[PO]])
                    cx.op("pe", lambda tb=tb, mt=mt: nc.tensor.matmul(ps[PL][:, :TT], onesb[:], tmpb[tb][:], start=(mt == 0), stop=(mt == nmt - 1)),
                          reads=[tmpb_b[tb]], writes=[ps_b[PL]])
                t = rr("tmp", NTMP)
                cx.op("dve", lambda t=t: V.reciprocal(out=tmp[t][:], in_=ps[PL][:, :TT]), reads=[ps_b[PL]], writes=[tmp_b[t]])
                cx.op("dve", lambda t=t, h=h: V.tensor_tensor(out=ox[:, h, :], in0=ps[PO][:, :TT], in1=tmp[t][:], op=ALU.mult),
                      reads=[ps_b[PO], tmp_b[t]], writes=[ox_b[h]])
            for c in range(DC):
                plan_lin(wo, c, XH, XH)
            for c in range(DC):
                p = next_ps()
                lin_chunk(wo, c, ox, ox_b, XH, XH, ps[p], ps_b[p])
                cx.op("act", lambda c=c, p=p: nc.scalar.copy(out=yT[:, c, :], in_=ps[p][:, :TT]), reads=[ps_b[p]], writes=[yT_b[c]])
            post_res(G(5), 1.0)
            ffn(wg2, wu2, wd2, G(6), G(7))
            for ts in range(TS):
                for cg in range(max(1, DC // 4)):
                    ncg = min(4, DC)
                    p = next_ps()
                    cx.pe_group([lambda q=q, cg=cg, ts=ts, p=p: nc.tensor.transpose(
                        ps[p][:, q * 128:(q + 1) * 128], hT[:, cg * 4 + q, ts * 128:(ts + 1) * 128], ident[:]) for q in range(ncg)],
                        reads=[hT_b[cg * 4 + q] for q in range(ncg)], writes=[ps_b[p]])
                    cx.op("act", lambda cg=cg, p=p, ncg=ncg: nc.scalar.copy(out=xin[:, cg * 512:cg * 512 + ncg * 128], in_=ps[p][:, :ncg * 128]),
                          reads=[ps_b[p]], writes=yT_b)
                cx.dma(out[bass.ds(tok0 + ts * 128, 128), :], xin, reads=yT_b)
            cx.end_iter()
    return nc


def prep_inputs(cfg, inp, lambda_init):
    D, FF, SEQ, HD, HS, MEM, XH, TT = (cfg[k] for k in ["D", "FF", "SEQ", "HD", "HS", "MEM", "XH", "TT"])
    DC, FC, NH = D // 128, FF // 128, HD + HS
    ktpD, ktpF, ktpH = min(16, DC), min(8, FC // 2), min(16, NH)
    f = lambda a: np.ascontiguousarray(np.asarray(a, dtype=np.float32))
    m = {}
    m["x"] = f(inp["x"]).reshape(SEQ, D)
    m["mem"] = f(inp["mem"]).reshape(MEM, D)
    m["posb"] = np.ascontiguousarray(np.broadcast_to(np.asarray(inp["positions"]).reshape(1, SEQ).astype(np.int32), (128, SEQ)))
    gl = ["ffn1_norm_pre", "ffn1_norm_post", "mix_norm_pre", "mix_norm_post", "xattn_norm_pre", "xattn_norm_post",
          "ffn2_norm_pre", "ffn2_norm_post", "mem_norm"]
    m["gains"] = np.ascontiguousarray(np.concatenate([gain_t(f(inp[k])[0]) for k in gl], axis=1))
    m["hg"] = np.ascontiguousarray(np.stack([f(inp["diff_subln"])[0], f(inp["sb_norm"])[0]], axis=1))
    lv = np.concatenate([f(inp[k])[0] for k in ["lambda_q1", "lambda_k1", "lambda_q2", "lambda_k2"]])
    m["lamv"] = np.ascontiguousarray(np.broadcast_to(lv[None, :], (128, 256)))
    rt = np.zeros((128, 2), np.float32)
    inv = ROPE_THETA ** (-np.arange(0, 16, 2, dtype=np.float32) / 16)
    for base in (0, 64):
        rt[base:base + 8, 0] = inv / np.float32(2 * math.pi)
        rt[base + 8:base + 16, 0] = inv / np.float32(2 * math.pi)
        rt[base:base + 8, 1] = -1.0
        rt[base + 8:base + 16, 1] = 1.0
    m["ropet"] = rt
    kp = np.arange(128)[:, None]
    qf = np.arange(TT)[None, :]
    mk = [((kp + 128 * j) <= qf) for j in range(TT // 128)] + [((kp + 128 * j) < qf) for j in range(TT // 128)]
    m["masks"] = np.ascontiguousarray(np.concatenate(mk, axis=0).astype(np.float32))
    m["ident"] = np.eye(128, dtype=np.float32)
    m["tri"] = (np.arange(128)[:, None] > np.arange(128)[None, :]).astype(np.float32)
    m["wg1"] = tile_w(f(inp["ffn1_w_gate"])[0], ktpD); m["wu1"] = tile_w(f(inp["ffn1_w_up"])[0], ktpD)
    m["wd1"] = tile_w(f(inp["ffn1_w_down"])[0], ktpF)
    m["wg2"] = tile_w(f(inp["ffn2_w_gate"])[0], ktpD); m["wu2"] = tile_w(f(inp["ffn2_w_up"])[0], ktpD)
    m["wd2"] = tile_w(f(inp["ffn2_w_down"])[0], ktpF)
    w_in = f(inp["w_in"])[0]
    m["win"] = tile_w(w_in, ktpD)
    perm = np.arange(128)
    for base in (0, 64):
        perm[base:base + 8] = np.arange(base + 8, base + 16)
        perm[base + 8:base + 16] = np.arange(base, base + 8)
    cols = np.concatenate([h * 128 + perm for h in range(2 * HD)])
    m["wsw"] = tile_w(np.ascontiguousarray(w_in[:, cols]), ktpD)
    m["wout"] = tile_w(f(inp["w_out"])[0], ktpH)
    m["wq"] = tile_w(f(inp["xattn_w_q"])[0], ktpD)
    m["wkv"] = tile_w(f(inp["xattn_w_kv"])[0], ktpD)
    m["wo"] = tile_w(f(inp["xattn_w_o"])[0], XH)
    return m


def run(cfg, inp, lambda_init):
    nc = build(cfg, lambda_init)
    m = prep_inputs(cfg, inp, lambda_init)
    res = run_bass_kernel_spmd(nc, [m], core_ids=[0])
    return res.results[0]["out"]


def kernel(**inputs):
    lambda_init = 0.8 - 0.6 * math.exp(-0.3 * 0)
    o = run(FULL, inputs, lambda_init)
    return np.asarray(o, dtype=np.float32).reshape(1, FULL["SEQ"], FULL["D"])
```

```python
import math
import contextlib
import numpy as np
import concourse.bass as bass
import concourse.mybir as mybir
from concourse.bass_utils import run_bass_kernel_spmd

F32 = mybir.dt.float32
BF16 = mybir.dt.bfloat16
I32 = mybir.dt.int32
AF = mybir.ActivationFunctionType
ALU = mybir.AluOpType

FULL = dict(D=4096, FF=14336, SEQ=8192, HD=16, HS=16, MEM=256, XH=4, TT=256)
ROPE_THETA = 500000.0
EPS = 1e-6
NDMA = 16


class Buf:
    __slots__ = ("w", "r")

    def __init__(self):
        self.w = None
        self.r = []


class Ctx:
    def __init__(self, nc, es):
        self.nc = nc
        self.es = es
        self.eng = {"pe": nc.tensor, "act": nc.scalar, "dve": nc.vector, "pool": nc.gpsimd, "sp": nc.sync}
        self.sem = {k: es.enter_context(nc.semaphore("sem_" + k)) for k in ["pe", "act", "dve", "pool"]}
        self.dsem = [es.enter_context(nc.semaphore(f"dsem{i}")) for i in range(NDMA)]
        for i, s in enumerate(self.dsem):
            self.sem[f"d{i}"] = s
        self.bufs = []
        self.reset()

    def reset(self):
        self.cnt = {k: 0 for k in self.sem}
        self.waited = {e: {} for e in self.eng}
        self.dnext = 0
        for b in self.bufs:
            b.w = None
            b.r = []

    def buf(self):
        b = Buf()
        self.bufs.append(b)
        return b

    def bufs_n(self, n):
        return [self.buf() for _ in range(n)]

    def _deps(self, eng, reads, writes):
        deps = {}

        def add(tok, same_ok):
            if tok is None:
                return
            sm, v = tok
            if sm == eng and not same_ok:
                return
            if eng == "pe" and sm == "pe":
                return
            if deps.get(sm, 0) < v:
                deps[sm] = v

        for b in reads:
            add(b.w, True)
        for b in writes:
            add(b.w, False)
            for t in b.r:
                add(t, False)
        w = self.waited[eng]
        for sm, v in deps.items():
            if w.get(sm, 0) < v:
                self.eng[eng].wait_ge(self.sem[sm], v)
                w[sm] = v

    def _commit(self, tok, reads, writes):
        for b in reads:
            b.r.append(tok)
            if len(b.r) > 12:
                b.r = b.r[-12:] if False else self._compact(b.r)
        for b in writes:
            b.w = tok
            b.r = []

    @staticmethod
    def _compact(toks):
        best = {}
        for sm, v in toks:
            if best.get(sm, 0) < v:
                best[sm] = v
        return list(best.items())

    def op(self, eng, fn, reads=(), writes=()):
        self._deps(eng, reads, writes)
        ins = fn()
        self.cnt[eng] += 1
        ins.then_inc(self.sem[eng], 1)
        self._commit((eng, self.cnt[eng]), reads, writes)

    def pe_group(self, fns, reads=(), writes=()):
        self._deps("pe", reads, writes)
        ins = None
        for fn in fns:
            ins = fn()
        self.cnt["pe"] += 1
        ins.then_inc(self.sem["pe"], 1)
        self._commit(("pe", self.cnt["pe"]), reads, writes)

    def dma(self, out, in_, reads=(), writes=(), q="sp"):
        k = self.dnext
        self.dnext = (self.dnext + 1) % NDMA
        name = f"d{k}"
        prev = self.cnt[name]
        w = self.waited[q]
        if prev and w.get(name, 0) < prev:
            self.eng[q].wait_ge(self.sem[name], prev)
            w[name] = prev
        self._deps(q, reads, writes)
        self.eng[q].dma_start(out=out, in_=in_).then_inc(self.sem[name], 16)
        self.cnt[name] += 16
        self._commit((name, self.cnt[name]), reads, writes)

    def end_iter(self):
        w = self.waited["sp"]
        for i in range(NDMA):
            name = f"d{i}"
            if self.cnt[name] and w.get(name, 0) < self.cnt[name]:
                self.nc.sync.wait_ge(self.sem[name], self.cnt[name])
        for e in ["pe", "act", "dve", "pool"]:
            if self.cnt[e]:
                self.nc.sync.wait_ge(self.sem[e], self.cnt[e])
        self.nc.all_engine_barrier()
        for s in self.sem.values():
            self.nc.sync.sem_clear(s)
        self.nc.all_engine_barrier()
        self.reset()


def tile_w(W, ktp):
    K, N = W.shape
    kb = K // (128 * ktp)
    a = W.reshape(kb, ktp, 128, N // 128, 128)
    a = a.transpose(3, 0, 2, 1, 4)
    return np.ascontiguousarray(a).reshape(N // 128, kb, 128, ktp * 128)


def gain_t(g):
    g = np.asarray(g).reshape(-1)
    return np.ascontiguousarray(g.reshape(-1, 128).T)


def build(cfg, lambda_init):
    D, FF, SEQ, HD, HS, MEM, XH, TT = (cfg[k] for k in ["D", "FF", "SEQ", "HD", "HS", "MEM", "XH", "TT"])
    DC, FC, NTT, NH = D // 128, FF // 128, SEQ // TT, HD + HS
    TS = TT // 128
    assert MEM == TT
    ktpD = min(16, DC)
    ktpF = min(8, FC // 2)
    ktpH = min(16, NH)
    nc = bass.Bass("TRN2", target_bir_lowering=False)

    def din(name, shape, dt=F32):
        return nc.dram_tensor(name, list(shape), dt, kind="ExternalInput").ap()

    def dscr(name, shape, dt):
        return nc.dram_tensor(name, list(shape), dt, kind="Internal").ap()

    x = din("x", [SEQ, D])
    mem = din("mem", [MEM, D])
    posb = din("posb", [128, SEQ], I32)
    gains = din("gains", [128, 9 * DC])
    hg = din("hg", [128, 2])
    lamv = din("lamv", [128, 4 * 64])
    ropet = din("ropet", [128, 2])
    masks = din("masks", [4 * 128, TT])
    ident_d = din("ident", [128, 128])
    tri_d = din("tri", [128, 128])
    wg1 = din("wg1", [FC, DC // ktpD, 128, ktpD * 128])
    wu1 = din("wu1", [FC, DC // ktpD, 128, ktpD * 128])
    wd1 = din("wd1", [DC, FC // ktpF, 128, ktpF * 128])
    wg2 = din("wg2", [FC, DC // ktpD, 128, ktpD * 128])
    wu2 = din("wu2", [FC, DC // ktpD, 128, ktpD * 128])
    wd2 = din("wd2", [DC, FC // ktpF, 128, ktpF * 128])
    win = din("win", [6 * (NH // 2), DC // ktpD, 128, ktpD * 128])
    wsw = din("wsw", [2 * HD, DC // ktpD, 128, ktpD * 128])
    wout = din("wout", [DC, NH // ktpH, 128, ktpH * 128])
    wq = din("wq", [XH, DC // ktpD, 128, ktpD * 128])
    wkv = din("wkv", [2 * XH, DC // ktpD, 128, ktpD * 128])
    wo = din("wo", [DC, 1, 128, XH * 128])
    out = nc.dram_tensor("out", [SEQ, D], F32, kind="ExternalOutput").ap()

    h1T = dscr("h1T", [NTT * 128, DC * TT], F32)
    QT = dscr("QT", [NH * 128, SEQ], BF16)
    KT = dscr("KT", [NH * 128, SEQ], BF16)
    Vd = dscr("Vd", [NH * 128, SEQ], BF16)
    MT = dscr("MT", [NH * 128, SEQ], BF16)

    es = contextlib.ExitStack()
    with es:
        cx = Ctx(nc, es)

        def sb(name, shape, dt):
            return es.enter_context(nc.sbuf_tensor(name, list(shape), dt))

        def psum(name, dt=F32, n=512):
            return es.enter_context(nc.psum_tensor(name, [128, n], dt))

        hT = sb("hT", [128, DC, TT], F32); hT_b = cx.bufs_n(DC)
        yT = sb("yT", [128, DC, TT], F32); yT_b = cx.bufs_n(DC)
        hnT = sb("hnT", [128, DC, TT], BF16); hnT_b = cx.bufs_n(DC)
        NACT = max(FC // 2, NH, 3 * (SEQ // TT), 3 * NH, 2 * XH)
        actT = sb("actT", [128, NACT, TT], BF16); actT_b = cx.bufs_n(NACT)
        xin = yT[:].rearrange("p c t -> p (c t)")[:, 0:D]
        NSTG, NWB = 4, 4
        stg = [sb(f"stg{i}", [128, 16 * 128], F32) for i in range(NSTG)]; stg_b = cx.bufs_n(NSTG)
        wbf = [sb(f"wbf{i}", [128, 16 * 128], BF16) for i in range(NWB)]; wbf_b = cx.bufs_n(NWB)
        NTMP = 6
        tmp = [sb(f"tmp{i}", [128, TT], F32) for i in range(NTMP)]; tmp_b = cx.bufs_n(NTMP)
        tmpb = [sb(f"tmpb{i}", [128, TT], BF16) for i in range(NTMP)]; tmpb_b = cx.bufs_n(NTMP)
        rstd = sb("rstd", [128, TT], F32); rstd_b = cx.buf()
        lnv = sb("lnv", [128, TT], F32); lnv_b = cx.buf()
        gains_s = sb("gains_s", [128, 9 * DC], F32); c_b = cx.buf()
        hg_s = sb("hg_s", [128, 2], F32)
        hg2_s = sb("hg2_s", [128, 2], F32)
        lam_s = sb("lam_s", [128, 256], F32)
        lam_t = sb("lam_t", [128, 8], F32)
        rope_s = sb("rope_s", [128, 2], F32)
        pt = [sb(f"pt{i}", [128, 2 * TT], BF16) for i in range(4)]; pt_b = cx.bufs_n(4)
        masks_bf = sb("masks_bf", [128, 4, TT], BF16)
        ident = sb("ident_s", [128, 128], F32)
        identb = sb("identb", [128, 128], BF16)
        tri32 = sb("tri32", [128, 128], F32)
        trib = sb("trib", [128, 128], BF16)
        ones32 = sb("ones32", [128, 128], F32)
        onesb = sb("onesb", [128, 128], BF16)
        cst = sb("cst", [128, 4], F32)
        Ct = sb("Ct", [128, TT], F32); St = sb("St", [128, TT], F32); cs_b = cx.buf()
        posi = sb("posi", [128, TT], I32); posf = sb("posf", [128, TT], F32); pos_b = cx.buf()
        ki = sb("ki", [128, TT], I32); ki_b = cx.buf()
        KmT = sb("KmT", [128, XH, MEM], BF16); Vm = sb("Vm", [128, MEM // 128, XH * 128], BF16); km_b = cx.buf()
        qx = actT[:, 0:XH, :]; qx_b = actT_b[0:XH]
        ox = actT[:, XH:2 * XH, :]; ox_b = actT_b[XH:2 * XH]
        NQ = SEQ // TT
        KTs = actT[:, 0:NQ, :].rearrange("p c t -> p (c t)")
        QTs = actT[:, NQ:2 * NQ, :].rearrange("p c t -> p (c t)")
        Vs = actT[:, 2 * NQ:3 * NQ, :].rearrange("p c t -> p (c t)").rearrange("p (k d) -> p k d", d=128)
        kqv_b = cx.bufs_n(3)
        vst_b = cx.buf()
        ofull = sb("ofull", [128, TT], F32); ofull_b = cx.buf()
        NPS = 7
        ps = [psum(f"ps{i}") for i in range(NPS)]; ps_b = cx.bufs_n(NPS)
        psb = psum("psb", BF16, 1024); psb_b = cx.buf()
        psrr = [0]
        castrr = [0]
        CAST_ENG = ["dve", "act"]

        PSROT = [[0, 1, 2]]

        def next_ps():
            rot = PSROT[0]
            i = psrr[0] % len(rot)
            psrr[0] += 1
            return rot[i]

        ring = {"pt": 0, "tmp": 0, "tmpb": 0, "xin": 0, "ostg": 0, "vstg": 0}

        def rr(name, n):
            i = ring[name]
            ring[name] = (i + 1) % n
            return i

        G = lambda k: gains_s[:, k * DC:(k + 1) * DC]
        eps_c, one_c, npi_c = cst[:, 0:1], cst[:, 1:2], cst[:, 2:3]

        wf = {"q": [], "issued": 0, "consumed": 0, "n": 0}
        LOOK = 3

        def plan_lin(W, n, KT_, ktp, kb0=0):
            for kb in range(KT_ // ktp):
                wf["q"].append((W, n, kb0 + kb, ktp))

        def w_issue(upto):
            while wf["issued"] <= min(upto, len(wf["q"]) - 1):
                W, n, kb, ktp = wf["q"][wf["issued"]]
                i = wf["n"] + wf["issued"]
                s, b = i % NSTG, i % NWB
                cx.dma(stg[s][:, :ktp * 128], W[n, kb], writes=[stg_b[s]])
                if i % 2 == 0:
                    cx.op("dve", lambda s=s, b=b, ktp=ktp: nc.vector.tensor_copy(out=wbf[b][:, :ktp * 128], in_=stg[s][:, :ktp * 128]),
                          reads=[stg_b[s]], writes=[wbf_b[b]])
                else:
                    cx.op("act", lambda s=s, b=b, ktp=ktp: nc.scalar.copy(out=wbf[b][:, :ktp * 128], in_=stg[s][:, :ktp * 128]),
                          reads=[stg_b[s]], writes=[wbf_b[b]])
                wf["issued"] += 1

        def w_get(W, n, kb, ktp):
            if wf["consumed"] == len(wf["q"]):
                wf["q"].append((W, n, kb, ktp))
            e = wf["q"][wf["consumed"]]
            assert e[0] is W and e[1:] == (n, kb, ktp), ("weight plan mismatch", e[1:], (n, kb, ktp))
            w_issue(wf["consumed"] + LOOK)
            b = (wf["n"] + wf["consumed"]) % NWB
            wf["consumed"] += 1
            if wf["consumed"] == len(wf["q"]):
                wf["n"] += len(wf["q"]); wf["q"] = []; wf["issued"] = 0; wf["consumed"] = 0
            return b

        def lin_chunk(W, n, inT, in_b, KT_, ktp, pst, pst_b, kb0=0):
            nkb = KT_ // ktp
            for kb in range(nkb):
                b = w_get(W, n, kb0 + kb, ktp)
                fns = []
                for j in range(ktp):
                    k = kb * ktp + j
                    fns.append(lambda b=b, j=j, k=k: nc.tensor.matmul(
                        pst[:, :TT], wbf[b][:, j * 128:(j + 1) * 128], inT[:, k, :],
                        start=(k == 0), stop=(k == KT_ - 1)))
                cx.pe_group(fns, reads=[wbf_b[b]] + [in_b[kb * ktp + j] for j in range(ktp)], writes=[pst_b])

        def rms_stats(srcT, src_b, nch, div):
            a = 4
            for c in range(nch):
                t = rr("tmp", NTMP)
                cx.op("act", lambda c=c, t=t: nc.scalar.activation(out=tmp[t][:], in_=srcT[:, c, :], func=AF.Square),
                      reads=[src_b[c]], writes=[tmp_b[t]])
                cx.op("pe", lambda c=c, t=t: nc.tensor.matmul(ps[a][:, :TT], ones32[:], tmp[t][:],
                                                              start=(c == 0), stop=(c == nch - 1)),
                      reads=[tmp_b[t]], writes=[ps_b[a]])
            cx.op("act", lambda: nc.scalar.activation(out=lnv[:], in_=ps[a][:, :TT], func=AF.Ln, scale=1.0 / div, bias=eps_c),
                  reads=[ps_b[a]], writes=[lnv_b])
            cx.op("act", lambda: nc.scalar.activation(out=rstd[:], in_=lnv[:], func=AF.Exp, scale=-0.5),
                  reads=[lnv_b], writes=[rstd_b])

        def norm_to(srcT, src_b, gain, dstT, dst_b):
            rms_stats(srcT, src_b, DC, float(D))
            for c in range(DC):
                cx.op("dve", lambda c=c: nc.vector.scalar_tensor_tensor(
                    out=dstT[:, c, :], in0=srcT[:, c, :], scalar=gain[:, c:c + 1], in1=rstd[:],
                    op0=ALU.mult, op1=ALU.mult), reads=[src_b[c], rstd_b], writes=[dst_b[c]])

        def post_res(gain, factor):
            rms_stats(yT, yT_b, DC, float(D))
            for c in range(DC):
                t = rr("tmp", NTMP)
                cx.op("dve", lambda c=c, t=t: nc.vector.scalar_tensor_tensor(
                    out=tmp[t][:], in0=yT[:, c, :], scalar=gain[:, c:c + 1], in1=rstd[:],
                    op0=ALU.mult, op1=ALU.mult), reads=[yT_b[c], rstd_b], writes=[tmp_b[t]])
                cx.op("dve", lambda c=c, t=t: nc.vector.scalar_tensor_tensor(
                    out=hT[:, c, :], in0=tmp[t][:], scalar=float(factor), in1=hT[:, c, :],
                    op0=ALU.mult, op1=ALU.add), reads=[tmp_b[t], hT_b[c]], writes=[hT_b[c]])

        def load_T(src, row0, dstT, dst_b, ntok_tiles):
            for ts in range(ntok_tiles):
                cx.dma(xin, src[bass.ds(row0 + ts * 128, 128), :], writes=yT_b)
                for cg in range(max(1, DC // 4)):
                    ncg = min(4, DC)
                    p = next_ps()
                    cx.pe_group([lambda q=q, p=p, cg=cg: nc.tensor.transpose(
                        ps[p][:, q * 128:(q + 1) * 128], xin[:, (cg * 4 + q) * 128:(cg * 4 + q + 1) * 128], ident[:]) for q in range(ncg)],
                        reads=yT_b, writes=[ps_b[p]])
                    for q in range(ncg):
                        c = cg * 4 + q
                        cx.op("act", lambda q=q, c=c, p=p, ts=ts: nc.scalar.copy(
                            out=dstT[:, c, ts * 128:(ts + 1) * 128], in_=ps[p][:, q * 128:(q + 1) * 128]),
                            reads=[ps_b[p]], writes=[dst_b[c]])

        def ffn(Wg, Wu, Wd, gpre, gpost):
            FH = FC // 2
            for half in range(2):
                for n in range(FH):
                    plan_lin(Wg, half * FH + n, DC, ktpD)
                    plan_lin(Wu, half * FH + n, DC, ktpD)
                for c in range(DC):
                    plan_lin(Wd, c, FH, ktpF, kb0=half * (FH // ktpF))
            norm_to(hT, hT_b, gpre, hnT, hnT_b)
            for half in range(2):
                for n in range(FH):
                    pg, pu = next_ps(), next_ps()
                    lin_chunk(Wg, half * FH + n, hnT, hnT_b, DC, ktpD, ps[pg], ps_b[pg])
                    lin_chunk(Wu, half * FH + n, hnT, hnT_b, DC, ktpD, ps[pu], ps_b[pu])
                    t = rr("tmp", NTMP)
                    cx.op("act", lambda pg=pg, t=t: nc.scalar.activation(out=tmp[t][:], in_=ps[pg][:, :TT], func=AF.Silu),
                          reads=[ps_b[pg]], writes=[tmp_b[t]])
                    cx.op("dve", lambda pu=pu, t=t, n=n: nc.vector.tensor_tensor(
                        out=actT[:, n, :], in0=tmp[t][:], in1=ps[pu][:, :TT], op=ALU.mult),
                        reads=[tmp_b[t], ps_b[pu]], writes=[actT_b[n]])
                for c in range(DC):
                    p = next_ps()
                    lin_chunk(Wd, c, actT, actT_b, FH, ktpF, ps[p], ps_b[p], kb0=half * (FH // ktpF))
                    if half == 0:
                        cx.op("act", lambda c=c, p=p: nc.scalar.copy(out=yT[:, c, :], in_=ps[p][:, :TT]),
                              reads=[ps_b[p]], writes=[yT_b[c]])
                    else:
                        cx.op("dve", lambda c=c, p=p: nc.vector.tensor_tensor(out=yT[:, c, :], in0=ps[p][:, :TT], in1=yT[:, c, :], op=ALU.add),
                              reads=[ps_b[p], yT_b[c]], writes=[yT_b[c]])
            post_res(gpost, 0.5)

        def to_tokmajor_store(srcb_ap, src_buf, dram_rows_fn):
            for ts in range(TS):
                cx.op("pe", lambda ts=ts: nc.tensor.transpose(psb[:, ts * 128:(ts + 1) * 128],
                                                              srcb_ap[:, ts * 128:(ts + 1) * 128], identb[:]),
                      reads=[src_buf], writes=[psb_b])
                v = rr("vstg", 2)
                cx.op("act", lambda ts=ts, v=v: nc.scalar.copy(out=vstg[v][:], in_=psb[:, ts * 128:(ts + 1) * 128]),
                      reads=[psb_b], writes=[vstg_b[v]])
                cx.dma(dram_rows_fn(ts), vstg[v][:], reads=[vstg_b[v]])

        cx.dma(gains_s[:], gains[:, :], writes=[c_b])
        cx.dma(hg_s[:], hg[:, :], writes=[c_b])
        cx.dma(lam_s[:], lamv[:, :], writes=[c_b])
        cx.dma(rope_s[:], ropet[:, :], writes=[c_b])
        cx.dma(ident[:], ident_d[:, :], writes=[c_b])
        cx.dma(tri32[:], tri_d[:, :], writes=[c_b])
        V = nc.vector
        cx.op("dve", lambda: V.memset(ones32[:], 1.0), writes=[c_b])
        cx.op("dve", lambda: V.memset(onesb[:], 1.0), writes=[c_b])
        cx.op("dve", lambda: V.memset(cst[:, 0:1], EPS), writes=[c_b])
        cx.op("dve", lambda: V.memset(cst[:, 1:2], 1.0), writes=[c_b])
        cx.op("dve", lambda: V.memset(cst[:, 2:3], -math.pi), writes=[c_b])
        cx.op("dve", lambda: V.memset(cst[:, 3:4], 0.0), writes=[c_b])
        cx.op("dve", lambda: V.tensor_copy(out=identb[:], in_=ident[:]), reads=[c_b], writes=[c_b])
        cx.op("dve", lambda: V.tensor_copy(out=trib[:], in_=tri32[:]), reads=[c_b], writes=[c_b])
        for j in range(4):
            cx.dma(tmp[j][:], masks[j * 128:(j + 1) * 128, :], writes=[tmp_b[j]])
            cx.op("dve", lambda j=j: V.tensor_copy(out=masks_bf[:, j, :], in_=tmp[j][:]), reads=[tmp_b[j]], writes=[c_b])
        cx.op("dve", lambda: V.tensor_tensor(out=lam_s[:, 0:64], in0=lam_s[:, 0:64], in1=lam_s[:, 64:128], op=ALU.mult), reads=[c_b], writes=[c_b])
        cx.op("dve", lambda: V.tensor_tensor(out=lam_s[:, 128:192], in0=lam_s[:, 128:192], in1=lam_s[:, 192:256], op=ALU.mult), reads=[c_b], writes=[c_b])
        cx.op("dve", lambda: V.tensor_reduce(out=lam_t[:, 0:1], in_=lam_s[:, 0:64], op=ALU.add, axis=mybir.AxisListType.X), reads=[c_b], writes=[c_b])
        cx.op("dve", lambda: V.tensor_reduce(out=lam_t[:, 1:2], in_=lam_s[:, 128:192], op=ALU.add, axis=mybir.AxisListType.X), reads=[c_b], writes=[c_b])
        cx.op("act", lambda: nc.scalar.activation(out=lam_t[:, 2:4], in_=lam_t[:, 0:2], func=AF.Exp), reads=[c_b], writes=[c_b])
        cx.op("dve", lambda: V.tensor_tensor(out=lam_t[:, 4:5], in0=lam_t[:, 3:4], in1=lam_t[:, 2:3], op=ALU.subtract), reads=[c_b], writes=[c_b])
        cx.op("dve", lambda: V.tensor_scalar(out=lam_t[:, 5:6], in0=lam_t[:, 4:5], scalar1=-float(lambda_init), scalar2=None, op0=ALU.add), reads=[c_b], writes=[c_b])
        neglam = lam_t[:, 5:6]
        cx.op("dve", lambda: V.tensor_scalar(out=hg2_s[:, 0:1], in0=hg_s[:, 0:1], scalar1=float(1.0 - lambda_init), scalar2=None, op0=ALU.mult), reads=[c_b], writes=[c_b])
        cx.op("dve", lambda: V.tensor_copy(out=hg2_s[:, 1:2], in_=hg_s[:, 1:2]), reads=[c_b], writes=[c_b])

        load_T(mem, 0, hT, hT_b, MEM // 128)
        norm_to(hT, hT_b, G(8), hnT, hnT_b)
        for h in range(XH):
            p = next_ps()
            lin_chunk(wkv, h, hnT, hnT_b, DC, ktpD, ps[p], ps_b[p])
            cx.op("act", lambda h=h, p=p: nc.scalar.copy(out=KmT[:, h, :], in_=ps[p][:, :MEM]), reads=[ps_b[p]], writes=[km_b])
        for h in range(XH):
            p = next_ps()
            lin_chunk(wkv, XH + h, hnT, hnT_b, DC, ktpD, ps[p], ps_b[p])
            t = rr("tmpb", NTMP)
            cx.op("act", lambda p=p, t=t: nc.scalar.copy(out=tmpb[t][:], in_=ps[p][:, :MEM]), reads=[ps_b[p]], writes=[tmpb_b[t]])
            for mt in range(MEM // 128):
                cx.op("pe", lambda t=t, mt=mt: nc.tensor.transpose(psb[:, mt * 128:(mt + 1) * 128], tmpb[t][:, mt * 128:(mt + 1) * 128], identb[:]),
                      reads=[tmpb_b[t]], writes=[psb_b])
                cx.op("act", lambda h=h, mt=mt: nc.scalar.copy(out=Vm[:, mt, h * 128:(h + 1) * 128], in_=psb[:, mt * 128:(mt + 1) * 128]),
                      reads=[psb_b], writes=[km_b])
        cx.end_iter()

        with nc.Fori(0, NTT) as i:
            tok0 = i * TT
            load_T(x, tok0, hT, hT_b, TS)
            ffn(wg1, wu1, wd1, G(0), G(1))
            norm_to(hT, hT_b, G(2), hnT, hnT_b)
            cx.dma(h1T[bass.ds(i * 128, 128), :], hT[:].rearrange("p c t -> p (c t)"), reads=hT_b)
            cx.dma(posi[:], posb[:, bass.ds(tok0, TT)], writes=[pos_b])
            cx.op("dve", lambda: V.tensor_copy(out=posf[:], in_=posi[:]), reads=[pos_b], writes=[pos_b])
            for which, dst in ((0, St), (1, Ct)):
                t = rr("tmp", NTMP)
                cx.op("dve", lambda t=t, which=which: V.tensor_scalar(
                    out=tmp[t][:], in0=posf[:], scalar1=rope_s[:, 0:1], scalar2=(0.5 if which == 0 else 0.75),
                    op0=ALU.mult, op1=ALU.add), reads=[pos_b, c_b], writes=[tmp_b[t]])
                cx.op("dve", lambda t=t: V.tensor_copy(out=ki[:], in_=tmp[t][:]), reads=[tmp_b[t]], writes=[ki_b])
                t2 = rr("tmp", NTMP)
                cx.op("dve", lambda t2=t2: V.tensor_copy(out=tmp[t2][:], in_=ki[:]), reads=[ki_b], writes=[tmp_b[t2]])
                cx.op("dve", lambda t=t, t2=t2: V.tensor_tensor(out=tmp[t][:], in0=tmp[t][:], in1=tmp[t2][:], op=ALU.subtract),
                      reads=[tmp_b[t], tmp_b[t2]], writes=[tmp_b[t]])
                cx.op("dve", lambda t=t, t2=t2: V.scalar_tensor_tensor(out=tmp[t2][:], in0=tmp[t][:], scalar=0.0, in1=tmp[t][:],
                                                                      op0=ALU.is_lt, op1=ALU.add),
                      reads=[tmp_b[t]], writes=[tmp_b[t2]])
                cx.op("act", lambda t2=t2, dst=dst: nc.scalar.activation(out=dst[:], in_=tmp[t2][:], func=AF.Sin,
                                                                        scale=2.0 * math.pi, bias=npi_c),
                      reads=[tmp_b[t2], c_b], writes=[cs_b])
            cx.op("dve", lambda: V.tensor_scalar(out=St[:], in0=St[:], scalar1=rope_s[:, 1:2], scalar2=None, op0=ALU.mult),
                  reads=[cs_b, c_b], writes=[cs_b])
            for grp in range(6):
                for hh in range(HD if grp < 3 else HS):
                    plan_lin(win, (grp * HD + hh) if grp < 3 else (3 * HD + (grp - 3) * HS + hh), DC, ktpD)
                    if grp in (0, 1):
                        plan_lin(wsw, grp * HD + hh, DC, ktpD)
            Vst = actT[:, 2 * NH:3 * NH, :].rearrange("p c t -> p (c t)").rearrange("p (s n d) -> p s n d", s=TS, n=NH, d=128)
            for grp in range(6):
                nh = HD if grp < 3 else HS
                for hh in range(nh):
                    n = (grp * HD + hh) if grp < 3 else (3 * HD + (grp - 3) * HS + hh)
                    head = hh if grp < 3 else HD + hh
                    p = next_ps()
                    lin_chunk(win, n, hnT, hnT_b, DC, ktpD, ps[p], ps_b[p])
                    if grp in (0, 1):
                        slot = head if grp == 0 else NH + head
                        p2 = next_ps()
                        lin_chunk(wsw, grp * HD + hh, hnT, hnT_b, DC, ktpD, ps[p2], ps_b[p2])
                        t = rr("tmp", NTMP); t2 = rr("tmp", NTMP)
                        cx.op("dve", lambda t=t, p=p: V.tensor_tensor(out=tmp[t][:], in0=ps[p][:, :TT], in1=Ct[:], op=ALU.mult),
                              reads=[ps_b[p], cs_b], writes=[tmp_b[t]])
                        cx.op("dve", lambda t2=t2, p2=p2: V.tensor_tensor(out=tmp[t2][:], in0=ps[p2][:, :TT], in1=St[:], op=ALU.mult),
                              reads=[ps_b[p2], cs_b], writes=[tmp_b[t2]])
                        cx.op("dve", lambda t=t, t2=t2, slot=slot: V.tensor_tensor(out=actT[:, slot, :], in0=tmp[t][:], in1=tmp[t2][:], op=ALU.add),
                              reads=[tmp_b[t], tmp_b[t2]], writes=[actT_b[slot]])
                    elif grp in (3, 4):
                        slot = head if grp == 3 else NH + head
                        cx.op("act", lambda p=p, slot=slot: nc.scalar.copy(out=actT[:, slot, :], in_=ps[p][:, :TT]),
                              reads=[ps_b[p]], writes=[actT_b[slot]])
                    else:
                        tb = rr("tmpb", NTMP)
                        cx.op("act", lambda p=p, tb=tb: nc.scalar.copy(out=tmpb[tb][:], in_=ps[p][:, :TT]),
                              reads=[ps_b[p]], writes=[tmpb_b[tb]])
                        for ts in range(TS):
                            cx.op("pe", lambda ts=ts, tb=tb: nc.tensor.transpose(psb[:, ts * 128:(ts + 1) * 128],
                                                                              tmpb[tb][:, ts * 128:(ts + 1) * 128], identb[:]),
                                  reads=[tmpb_b[tb]], writes=[psb_b])
                            cx.op("act", lambda ts=ts, head=head: nc.scalar.copy(out=actT[:, 2 * NH + head, ts * 128:(ts + 1) * 128],
                                                                               in_=psb[:, ts * 128:(ts + 1) * 128]),
                                  reads=[psb_b], writes=[actT_b[2 * NH + head]])
            HB = min(16, NH)
            for n0 in range(0, NH, HB):
                for which, (dst, qq) in enumerate(((QT, "sp"), (KT, "act"), (Vd, "act"))):
                    cx.dma(dst[n0 * 128:(n0 + HB) * 128, bass.ds(tok0, TT)].rearrange("(n p) t -> p n t", p=128),
                           actT[:, which * NH + n0:which * NH + n0 + HB, :],
                           reads=actT_b[which * NH + n0:which * NH + n0 + HB], q=qq)
            cx.end_iter()

        NQG = SEQ // TT
        assert NQG <= DC
        sc_d = 64 ** -0.5
        sc_s = 128 ** -0.5

        def load_head(hidx):
            cx.dma(KTs, KT[bass.ds(hidx * 128, 128), :], writes=[kqv_b[0]])
            cx.dma(QTs, QT[bass.ds(hidx * 128, 128), :], writes=[kqv_b[1]])
            cx.dma(actT[:, 2 * NQ:3 * NQ, :].rearrange("p c t -> p (c t)"), Vd[bass.ds(hidx * 128, 128), :], writes=[kqv_b[2]])

        def head_norm_store(hidx, g, gcol, a=3):
            t = rr("tmp", NTMP)
            cx.op("act", lambda t=t: nc.scalar.activation(out=tmp[t][:], in_=ofull[:], func=AF.Square),
                  reads=[ofull_b], writes=[tmp_b[t]])
            cx.op("pe", lambda t=t: nc.tensor.matmul(ps[a][:, :TT], ones32[:], tmp[t][:], start=True, stop=True),
                  reads=[tmp_b[t]], writes=[ps_b[a]])
            cx.op("act", lambda: nc.scalar.activation(out=lnv[:], in_=ps[a][:, :TT], func=AF.Ln, scale=1.0 / 128, bias=eps_c),
                  reads=[ps_b[a]], writes=[lnv_b])
            cx.op("act", lambda: nc.scalar.activation(out=rstd[:], in_=lnv[:], func=AF.Exp, scale=-0.5),
                  reads=[lnv_b], writes=[rstd_b])
            cx.op("dve", lambda: V.scalar_tensor_tensor(out=hnT[:, g, :], in0=ofull[:], scalar=hg2_s[:, gcol:gcol + 1],
                                                       in1=rstd[:], op0=ALU.mult, op1=ALU.mult),
                  reads=[ofull_b, rstd_b, c_b], writes=[hnT_b[g]])
            if g == NQG - 1:
                cx.dma(MT[bass.ds(hidx * 128, 128), :], hnT[:, 0:NQG, :].rearrange("p c t -> p (c t)"), reads=hnT_b[0:NQG])

        def skew(tasks, stageA, stageB, lag):
            n = len(tasks)
            st = {}
            for i in range(n + lag):
                if i < n:
                    st[i] = stageA(tasks[i])
                j = i - lag
                if j >= 0:
                    stageB(tasks[j], st.pop(j))

        with nc.Fori(0, HD) as hd:
            load_head(hd)
            tasks = [(g, kt) for g in range(NQG) for kt in range((g + 1) * TS)]
            PSROT[0] = [0, 1, 2, 5, 6]

            def dA(task):
                g, kt = task
                qs = slice(g * TT, (g + 1) * TT); ks = slice(kt * 128, (kt + 1) * 128)
                dg = kt - g * TS
                k = rr("pt", 4)
                for comp in range(2):
                    p = next_ps()
                    lo = comp * 64
                    cx.op("pe", lambda: nc.tensor.matmul(ps[p][:, :TT], KTs[lo:lo + 64, ks], QTs[lo:lo + 64, qs], start=True, stop=True),
                          reads=[kqv_b[0], kqv_b[1]], writes=[ps_b[p]])
                    cx.op("act", lambda: nc.scalar.activation(out=pt[k][:, comp * TT:(comp + 1) * TT], in_=ps[p][:, :TT], func=AF.Exp, scale=sc_d),
                          reads=[ps_b[p]], writes=[pt_b[k]])
                    if dg >= 0:
                        cx.op("dve", lambda: V.tensor_tensor(out=pt[k][:, comp * TT:(comp + 1) * TT], in0=pt[k][:, comp * TT:(comp + 1) * TT],
                                                             in1=masks_bf[:, dg, :], op=ALU.mult),
                              reads=[pt_b[k], c_b], writes=[pt_b[k]])
                return k

            def dB(task, k):
                g, kt = task
                nkt = (g + 1) * TS
                cx.op("pe", lambda: nc.tensor.matmul(ps[3][:, :2 * TT], Vs[:, kt, :], pt[k][:], start=(kt == 0), stop=(kt == nkt - 1)),
                      reads=[pt_b[k], kqv_b[2]], writes=[ps_b[3]])
                cx.op("pe", lambda: nc.tensor.matmul(ps[4][:, :2 * TT], onesb[:], pt[k][:], start=(kt == 0), stop=(kt == nkt - 1)),
                      reads=[pt_b[k]], writes=[ps_b[4]])
                if kt == nkt - 1:
                    t1 = rr("tmp", NTMP); t2 = rr("tmp", NTMP)
                    for (c, t) in ((0, t1), (1, t2)):
                        cx.op("dve", lambda: V.reciprocal(out=tmp[t][:], in_=ps[4][:, c * TT:(c + 1) * TT]), reads=[ps_b[4]], writes=[tmp_b[t]])
                        cx.op("dve", lambda: V.tensor_tensor(out=tmp[t][:], in0=ps[3][:, c * TT:(c + 1) * TT], in1=tmp[t][:], op=ALU.mult),
                              reads=[ps_b[3], tmp_b[t]], writes=[tmp_b[t]])
                    cx.op("dve", lambda: V.scalar_tensor_tensor(out=ofull[:], in0=tmp[t2][:], scalar=neglam, in1=tmp[t1][:],
                                                               op0=ALU.mult, op1=ALU.add),
                          reads=[tmp_b[t1], tmp_b[t2], c_b], writes=[ofull_b])
                    head_norm_store(hd, g, 0, a=0)

            skew(tasks, dA, dB, 2)
            PSROT[0] = [0, 1, 2]
            cx.end_iter()

        with nc.Fori(0, HS) as hs:
            hidx = hs + HD
            load_head(hidx)
            PO, PC = 4, 5
            PSROT[0] = [0, 1, 2, 6]
            tasks = [(g, kt) for g in range(NQG) for kt in range((g + 1) * TS - 1, -1, -1)]

            n_t = len(tasks)
            S = [dict() for _ in range(n_t)]

            def flags(i):
                g, kt = tasks[i]
                nkt = (g + 1) * TS
                return g, kt, kt - g * TS, (kt == nkt - 1), (kt == 0)

            def sa_pe(i):
                g, kt, dg, first, last = flags(i)
                p = next_ps(); S[i]["p"] = p
                cx.op("pe", lambda: nc.tensor.matmul(ps[p][:, :TT], KTs[:, kt * 128:(kt + 1) * 128], QTs[:, g * TT:(g + 1) * TT],
                                                     start=True, stop=True),
                      reads=[kqv_b[0], kqv_b[1]], writes=[ps_b[p]])

            def sa_act(i):
                p = S[i]["p"]
                te = rr("tmp", NTMP); tl = rr("tmp", NTMP)
                S[i]["te"], S[i]["tl"] = te, tl
                cx.op("act", lambda: nc.scalar.activation(out=tmp[te][:], in_=ps[p][:, :TT], func=AF.Exp, scale=-sc_s),
                      reads=[ps_b[p]], writes=[tmp_b[te]])
                cx.op("act", lambda: nc.scalar.activation(out=tmp[tl][:], in_=tmp[te][:], func=AF.Ln, scale=1.0, bias=one_c),
                      reads=[tmp_b[te], c_b], writes=[tmp_b[tl]])

            def sa_dve(i):
                g, kt, dg, first, last = flags(i)
                p, te, tl = S[i]["p"], S[i]["te"], S[i]["tl"]
                tb = rr("tmpb", NTMP); S[i]["tb"] = tb
                if dg >= 0:
                    cx.op("dve", lambda: V.scalar_tensor_tensor(out=tmp[te][:], in0=ps[p][:, :TT], scalar=-sc_s, in1=tmp[tl][:],
                                                               op0=ALU.mult, op1=ALU.subtract),
                          reads=[ps_b[p], tmp_b[tl]], writes=[tmp_b[te]])
                    cx.op("dve", lambda: V.tensor_tensor(out=tmpb[tb][:], in0=tmp[te][:], in1=masks_bf[:, 2 + dg, :], op=ALU.mult),
                          reads=[tmp_b[te], c_b], writes=[tmpb_b[tb]])
                else:
                    cx.op("dve", lambda: V.scalar_tensor_tensor(out=tmpb[tb][:], in0=ps[p][:, :TT], scalar=-sc_s, in1=tmp[tl][:],
                                                               op0=ALU.mult, op1=ALU.subtract),
                          reads=[ps_b[p], tmp_b[tl]], writes=[tmpb_b[tb]])

            def sb_pe(i):
                tb = S[i]["tb"]
                p2 = next_ps(); S[i]["p2"] = p2
                cx.op("pe", lambda: nc.tensor.matmul(ps[p2][:, :TT], trib[:], tmpb[tb][:], start=True, stop=True),
                      reads=[tmpb_b[tb], c_b], writes=[ps_b[p2]])

            def sb_dve(i):
                p2, tl = S[i]["p2"], S[i]["tl"]
                cx.op("dve", lambda: V.tensor_tensor(out=tmp[tl][:], in0=ps[p2][:, :TT], in1=tmp[tl][:], op=ALU.subtract),
                      reads=[ps_b[p2], tmp_b[tl]], writes=[tmp_b[tl]])

            def sc(i):
                g, kt, dg, first, last = flags(i)
                tl, tb = S[i]["tl"], S[i]["tb"]
                if not first:
                    cx.op("dve", lambda: V.tensor_tensor(out=tmp[tl][:], in0=ps[PC][:, :TT], in1=tmp[tl][:], op=ALU.add),
                          reads=[ps_b[PC], tmp_b[tl]], writes=[tmp_b[tl]])
                if not last:
                    cx.op("pe", lambda: nc.tensor.matmul(ps[PC][:, :TT], onesb[:], tmpb[tb][:], start=first, stop=(kt == 1)),
                          reads=[tmpb_b[tb]], writes=[ps_b[PC]])
                ta = rr("tmpb", NTMP); S[i]["ta"] = ta
                cx.op("act", lambda: nc.scalar.activation(out=tmpb[ta][:], in_=tmp[tl][:], func=AF.Exp),
                      reads=[tmp_b[tl]], writes=[tmpb_b[ta]])
                if dg >= 0:
                    cx.op("dve", lambda: V.tensor_tensor(out=tmpb[ta][:], in0=tmpb[ta][:], in1=masks_bf[:, 2 + dg, :], op=ALU.mult),
                          reads=[tmpb_b[ta], c_b], writes=[tmpb_b[ta]])

            def sd(i):
                g, kt, dg, first, last = flags(i)
                ta = S[i]["ta"]
                cx.op("pe", lambda: nc.tensor.matmul(ps[PO][:, :TT], Vs[:, kt, :], tmpb[ta][:], start=first, stop=last),
                      reads=[tmpb_b[ta], kqv_b[2]], writes=[ps_b[PO]])
                if last:
                    cx.op("act", lambda: nc.scalar.copy(out=ofull[:], in_=ps[PO][:, :TT]), reads=[ps_b[PO]], writes=[ofull_b])
                    head_norm_store(hidx, g, 1)

            ok = lambda j: 0 <= j < n_t

            def sc_cadd(i):
                g, kt, dg, first, last = flags(i)
                tl, tb = S[i]["tl"], S[i]["tb"]
                if not first:
                    cx.op("dve", lambda: V.tensor_tensor(out=tmp[tl][:], in0=ps[PC][:, :TT], in1=tmp[tl][:], op=ALU.add),
                          reads=[ps_b[PC], tmp_b[tl]], writes=[tmp_b[tl]])
                if not last:
                    cx.op("pe", lambda: nc.tensor.matmul(ps[PC][:, :TT], onesb[:], tmpb[tb][:], start=first, stop=(kt == 1)),
                          reads=[tmpb_b[tb]], writes=[ps_b[PC]])

            def sc_exp(i):
                g, kt, dg, first, last = flags(i)
                tl = S[i]["tl"]
                ta = rr("tmpb", NTMP); S[i]["ta"] = ta
                cx.op("act", lambda: nc.scalar.activation(out=tmpb[ta][:], in_=tmp[tl][:], func=AF.Exp),
                      reads=[tmp_b[tl]], writes=[tmpb_b[ta]])
                if dg >= 0:
                    cx.op("dve", lambda: V.tensor_tensor(out=tmpb[ta][:], in0=tmpb[ta][:], in1=masks_bf[:, 2 + dg, :], op=ALU.mult),
                          reads=[tmpb_b[ta], c_b], writes=[tmpb_b[ta]])

            for i in range(-3, n_t + 1):
                if ok(i + 1): sb_pe(i + 1)
                if ok(i + 3): sa_pe(i + 3)
                if ok(i): sc_cadd(i)
                if ok(i - 1): sd(i - 1)
                if ok(i + 2): sa_act(i + 2)
                if ok(i + 1): sb_dve(i + 1)
                if ok(i + 2): sa_dve(i + 2)
                if ok(i): sc_exp(i)
            PSROT[0] = [0, 1, 2]
            cx.end_iter()

        sc_x = 128 ** -0.5
        with nc.Fori(0, NTT) as i:
            tok0 = i * TT
            cx.dma(hT[:].rearrange("p c t -> p (c t)"), h1T[bass.ds(i * 128, 128), :], writes=hT_b)
            for n0 in range(0, NH, 16):
                n1 = min(NH, n0 + 16)
                cx.dma(actT[:, n0:n1, :], MT[n0 * 128:n1 * 128, bass.ds(tok0, TT)].rearrange("(n p) t -> p n t", p=128),
                       writes=actT_b[n0:n1])
            for c in range(DC):
                plan_lin(wout, c, NH, ktpH)
            for c in range(DC):
                p = next_ps()
                lin_chunk(wout, c, actT, actT_b, NH, ktpH, ps[p], ps_b[p])
                cx.op("act", lambda c=c, p=p: nc.scalar.copy(out=yT[:, c, :], in_=ps[p][:, :TT]), reads=[ps_b[p]], writes=[yT_b[c]])
            post_res(G(3), 1.0)
            norm_to(hT, hT_b, G(4), hnT, hnT_b)
            for h in range(XH):
                plan_lin(wq, h, DC, ktpD)
            for h in range(XH):
                p = next_ps()
                lin_chunk(wq, h, hnT, hnT_b, DC, ktpD, ps[p], ps_b[p])
                cx.op("act", lambda h=h, p=p: nc.scalar.copy(out=qx[:, h, :], in_=ps[p][:, :TT]), reads=[ps_b[p]], writes=[qx_b[h]])
            for h in range(XH):
                PO, PL = 4, 5
                nmt = MEM // 128
                for mt in range(nmt):
                    p = next_ps()
                    cx.op("pe", lambda p=p, h=h, mt=mt: nc.tensor.matmul(ps[p][:, :TT], KmT[:, h, mt * 128:(mt + 1) * 128], qx[:, h, :], start=True, stop=True),
                          reads=[km_b, qx_b[h]], writes=[ps_b[p]])
                    tb = rr("tmpb", NTMP)
                    cx.op("act", lambda p=p, tb=tb: nc.scalar.activation(out=tmpb[tb][:], in_=ps[p][:, :TT], func=AF.Exp, scale=sc_x),
                          reads=[ps_b[p]], writes=[tmpb_b[tb]])
                    cx.op("pe", lambda tb=tb, h=h, mt=mt: nc.tensor.matmul(ps[PO][:, :TT], Vm[:, mt, h * 128:(h + 1) * 128], tmpb[tb][:],
                                                                         start=(mt == 0), stop=(mt == nmt - 1)),
                          reads=[tmpb_b[tb], km_b], writes=[ps_b[PO]])
                    cx.op("pe", lambda tb=tb, mt=mt: nc.tensor.matmul(ps[PL][:, :TT], onesb[:], tmpb[tb][:], start=(mt == 0), stop=(mt == nmt - 1)),
                          reads=[tmpb_b[tb]], writes=[ps_b[PL]])
                t = rr("tmp", NTMP)
                cx.op("dve", lambda t=t: V.reciprocal(out=tmp[t][:], in_=ps[PL][:, :TT]), reads=[ps_b[PL]], writes=[tmp_b[t]])
                cx.op("dve", lambda t=t, h=h: V.tensor_tensor(out=ox[:, h, :], in0=ps[PO][:, :TT], in1=tmp[t][:], op=ALU.mult),
                      reads=[ps_b[PO], tmp_b[t]], writes=[ox_b[h]])
            for c in range(DC):
                plan_lin(wo, c, XH, XH)
            for c in range(DC):
                p = next_ps()
                lin_chunk(wo, c, ox, ox_b, XH, XH, ps[p], ps_b[p])
                cx.op("act", lambda c=c, p=p: nc.scalar.copy(out=yT[:, c, :], in_=ps[p][:, :TT]), reads=[ps_b[p]], writes=[yT_b[c]])
            post_res(G(5), 1.0)
            ffn(wg2, wu2, wd2, G(6), G(7))
            for ts in range(TS):
                for cg in range(max(1, DC // 4)):
                    ncg = min(4, DC)
                    p = next_ps()
                    cx.pe_group([lambda q=q, cg=cg, ts=ts, p=p: nc.tensor.transpose(
                        ps[p][:, q * 128:(q + 1) * 128], hT[:, cg * 4 + q, ts * 128:(ts + 1) * 128], ident[:]) for q in range(ncg)],
                        reads=[hT_b[cg * 4 + q] for q in range(ncg)], writes=[ps_b[p]])
                    cx.op("act", lambda cg=cg, p=p, ncg=ncg: nc.scalar.copy(out=xin[:, cg * 512:cg * 512 + ncg * 128], in_=ps[p][:, :ncg * 128]),
                          reads=[ps_b[p]], writes=yT_b)
                cx.dma(out[bass.ds(tok0 + ts * 128, 128), :], xin, reads=yT_b)
            cx.end_iter()
    return nc


def prep_inputs(cfg, inp, lambda_init):
    D, FF, SEQ, HD, HS, MEM, XH, TT = (cfg[k] for k in ["D", "FF", "SEQ", "HD", "HS", "MEM", "XH", "TT"])
    DC, FC, NH = D // 128, FF // 128, HD + HS
    ktpD, ktpF, ktpH = min(16, DC), min(8, FC // 2), min(16, NH)
    f = lambda a: np.ascontiguousarray(np.asarray(a, dtype=np.float32))
    m = {}
    m["x"] = f(inp["x"]).reshape(SEQ, D)
    m["mem"] = f(inp["mem"]).reshape(MEM, D)
    m["posb"] = np.ascontiguousarray(np.broadcast_to(np.asarray(inp["positions"]).reshape(1, SEQ).astype(np.int32), (128, SEQ)))
    gl = ["ffn1_norm_pre", "ffn1_norm_post", "mix_norm_pre", "mix_norm_post", "xattn_norm_pre", "xattn_norm_post",
          "ffn2_norm_pre", "ffn2_norm_post", "mem_norm"]
    m["gains"] = np.ascontiguousarray(np.concatenate([gain_t(f(inp[k])[0]) for k in gl], axis=1))
    m["hg"] = np.ascontiguousarray(np.stack([f(inp["diff_subln"])[0], f(inp["sb_norm"])[0]], axis=1))
    lv = np.concatenate([f(inp[k])[0] for k in ["lambda_q1", "lambda_k1", "lambda_q2", "lambda_k2"]])
    m["lamv"] = np.ascontiguousarray(np.broadcast_to(lv[None, :], (128, 256)))
    rt = np.zeros((128, 2), np.float32)
    inv = ROPE_THETA ** (-np.arange(0, 16, 2, dtype=np.float32) / 16)
    for base in (0, 64):
        rt[base:base + 8, 0] = inv / np.float32(2 * math.pi)
        rt[base + 8:base + 16, 0] = inv / np.float32(2 * math.pi)
        rt[base:base + 8, 1] = -1.0
        rt[base + 8:base + 16, 1] = 1.0
    m["ropet"] = rt
    kp = np.arange(128)[:, None]
    qf = np.arange(TT)[None, :]
    mk = [((kp + 128 * j) <= qf) for j in range(TT // 128)] + [((kp + 128 * j) < qf) for j in range(TT // 128)]
    m["masks"] = np.ascontiguousarray(np.concatenate(mk, axis=0).astype(np.float32))
    m["ident"] = np.eye(128, dtype=np.float32)
    m["tri"] = (np.arange(128)[:, None] > np.arange(128)[None, :]).astype(np.float32)
    m["wg1"] = tile_w(f(inp["ffn1_w_gate"])[0], ktpD); m["wu1"] = tile_w(f(inp["ffn1_w_up"])[0], ktpD)
    m["wd1"] = tile_w(f(inp["ffn1_w_down"])[0], ktpF)
    m["wg2"] = tile_w(f(inp["ffn2_w_gate"])[0], ktpD); m["wu2"] = tile_w(f(inp["ffn2_w_up"])[0], ktpD)
    m["wd2"] = tile_w(f(inp["ffn2_w_down"])[0], ktpF)
    w_in = f(inp["w_in"])[0]
    m["win"] = tile_w(w_in, ktpD)
    perm = np.arange(128)
    for base in (0, 64):
        perm[base:base + 8] = np.arange(base + 8, base + 16)
        perm[base + 8:base + 16] = np.arange(base, base + 8)
    cols = np.concatenate([h * 128 + perm for h in range(2 * HD)])
    m["wsw"] = tile_w(np.ascontiguousarray(w_in[:, cols]), ktpD)
    m["wout"] = tile_w(f(inp["w_out"])[0], ktpH)
    m["wq"] = tile_w(f(inp["xattn_w_q"])[0], ktpD)
    m["wkv"] = tile_w(f(inp["xattn_w_kv"])[0], ktpD)
    m["wo"] = tile_w(f(inp["xattn_w_o"])[0], XH)
    return m


def run(cfg, inp, lambda_init):
    nc = build(cfg, lambda_init)
    m = prep_inputs(cfg, inp, lambda_init)
    res = run_bass_kernel_spmd(nc, [m], core_ids=[0])
    return res.results[0]["out"]


def kernel(**inputs):
    lambda_init = 0.8 - 0.6 * math.exp(-0.3 * 0)
    o = run(FULL, inputs, lambda_init)
    return np.asarray(o, dtype=np.float32).reshape(1, FULL["SEQ"], FULL["D"])
```

```python
import math
import contextlib
import numpy as np
import concourse.bass as bass
import concourse.mybir as mybir
from concourse.bass_utils import run_bass_kernel_spmd

F32 = mybir.dt.float32
BF16 = mybir.dt.bfloat16
I32 = mybir.dt.int32
AF = mybir.ActivationFunctionType
ALU = mybir.AluOpType

FULL = dict(D=4096, FF=14336, SEQ=8192, HD=16, HS=16, MEM=256, XH=4, TT=256)
ROPE_THETA = 500000.0
EPS = 1e-6
NDMA = 16


class Buf:
    __slots__ = ("w", "r")

    def __init__(self):
        self.w = None
        self.r = []


class Ctx:
    def __init__(self, nc, es):
        self.nc = nc
        self.es = es
        self.eng = {"pe": nc.tensor, "act": nc.scalar, "dve": nc.vector, "pool": nc.gpsimd, "sp": nc.sync}
        self.sem = {k: es.enter_context(nc.semaphore("sem_" + k)) for k in ["pe", "act", "dve", "pool"]}
        self.dsem = [es.enter_context(nc.semaphore(f"dsem{i}")) for i in range(NDMA)]
        for i, s in enumerate(self.dsem):
            self.sem[f"d{i}"] = s
        self.bufs = []
        self.reset()

    def reset(self):
        self.cnt = {k: 0 for k in self.sem}
        self.waited = {e: {} for e in self.eng}
        self.dnext = 0
        for b in self.bufs:
            b.w = None
            b.r = []

    def buf(self):
        b = Buf()
        self.bufs.append(b)
        return b

    def bufs_n(self, n):
        return [self.buf() for _ in range(n)]

    def _deps(self, eng, reads, writes):
        deps = {}

        def add(tok, same_ok):
            if tok is None:
                return
            sm, v = tok
            if sm == eng and not same_ok:
                return
            if eng == "pe" and sm == "pe":
                return
            if deps.get(sm, 0) < v:
                deps[sm] = v

        for b in reads:
            add(b.w, True)
        for b in writes:
            add(b.w, False)
            for t in b.r:
                add(t, False)
        w = self.waited[eng]
        for sm, v in deps.items():
            if w.get(sm, 0) < v:
                self.eng[eng].wait_ge(self.sem[sm], v)
                w[sm] = v

    def _commit(self, tok, reads, writes):
        for b in reads:
            b.r.append(tok)
            if len(b.r) > 12:
                b.r = b.r[-12:] if False else self._compact(b.r)
        for b in writes:
            b.w = tok
            b.r = []

    @staticmethod
    def _compact(toks):
        best = {}
        for sm, v in toks:
            if best.get(sm, 0) < v:
                best[sm] = v
        return list(best.items())

    def op(self, eng, fn, reads=(), writes=()):
        self._deps(eng, reads, writes)
        ins = fn()
        self.cnt[eng] += 1
        ins.then_inc(self.sem[eng], 1)
        self._commit((eng, self.cnt[eng]), reads, writes)

    def pe_group(self, fns, reads=(), writes=()):
        self._deps("pe", reads, writes)
        ins = None
        for fn in fns:
            ins = fn()
        self.cnt["pe"] += 1
        ins.then_inc(self.sem["pe"], 1)
        self._commit(("pe", self.cnt["pe"]), reads, writes)

    def dma(self, out, in_, reads=(), writes=(), q="sp"):
        k = self.dnext
        self.dnext = (self.dnext + 1) % NDMA
        name = f"d{k}"
        prev = self.cnt[name]
        w = self.waited[q]
        if prev and w.get(name, 0) < prev:
            self.eng[q].wait_ge(self.sem[name], prev)
            w[name] = prev
        self._deps(q, reads, writes)
        self.eng[q].dma_start(out=out, in_=in_).then_inc(self.sem[name], 16)
        self.cnt[name] += 16
        self._commit((name, self.cnt[name]), reads, writes)

    def end_iter(self):
        w = self.waited["sp"]
        for i in range(NDMA):
            name = f"d{i}"
            if self.cnt[name] and w.get(name, 0) < self.cnt[name]:
                self.nc.sync.wait_ge(self.sem[name], self.cnt[name])
        for e in ["pe", "act", "dve", "pool"]:
            if self.cnt[e]:
                self.nc.sync.wait_ge(self.sem[e], self.cnt[e])
        self.nc.all_engine_barrier()
        for s in self.sem.values():
            self.nc.sync.sem_clear(s)
        self.nc.all_engine_barrier()
        self.reset()


def tile_w(W, ktp):
    K, N = W.shape
    kb = K // (128 * ktp)
    a = W.reshape(kb, ktp, 128, N // 128, 128)
    a = a.transpose(3, 0, 2, 1, 4)
    return np.ascontiguousarray(a).reshape(N // 128, kb, 128, ktp * 128)


def gain_t(g):
    g = np.asarray(g).reshape(-1)
    return np.ascontiguousarray(g.reshape(-1, 128).T)


def build(cfg, lambda_init):
    D, FF, SEQ, HD, HS, MEM, XH, TT = (cfg[k] for k in ["D", "FF", "SEQ", "HD", "HS", "MEM", "XH", "TT"])
    DC, FC, NTT, NH = D // 128, FF // 128, SEQ // TT, HD + HS
    TS = TT // 128
    assert MEM == TT
    ktpD = min(16, DC)
    ktpF = min(8, FC // 2)
    ktpH = min(16, NH)
    nc = bass.Bass("TRN2", target_bir_lowering=False)

    def din(name, shape, dt=F32):
        return nc.dram_tensor(name, list(shape), dt, kind="ExternalInput").ap()

    def dscr(name, shape, dt):
        return nc.dram_tensor(name, list(shape), dt, kind="Internal").ap()

    x = din("x", [SEQ, D])
    mem = din("mem", [MEM, D])
    posb = din("posb", [128, SEQ], I32)
    gains = din("gains", [128, 9 * DC])
    hg = din("hg", [128, 2])
    lamv = din("lamv", [128, 4 * 64])
    ropet = din("ropet", [128, 2])
    masks = din("masks", [4 * 128, TT])
    ident_d = din("ident", [128, 128])
    tri_d = din("tri", [128, 128])
    wg1 = din("wg1", [FC, DC // ktpD, 128, ktpD * 128])
    wu1 = din("wu1", [FC, DC // ktpD, 128, ktpD * 128])
    wd1 = din("wd1", [DC, FC // ktpF, 128, ktpF * 128])
    wg2 = din("wg2", [FC, DC // ktpD, 128, ktpD * 128])
    wu2 = din("wu2", [FC, DC // ktpD, 128, ktpD * 128])
    wd2 = din("wd2", [DC, FC // ktpF, 128, ktpF * 128])
    win = din("win", [6 * (NH // 2), DC // ktpD, 128, ktpD * 128])
    wsw = din("wsw", [2 * HD, DC // ktpD, 128, ktpD * 128])
    wout = din("wout", [DC, NH // ktpH, 128, ktpH * 128])
    wq = din("wq", [XH, DC // ktpD, 128, ktpD * 128])
    wkv = din("wkv", [2 * XH, DC // ktpD, 128, ktpD * 128])
    wo = din("wo", [DC, 1, 128, XH * 128])
    out = nc.dram_tensor("out", [SEQ, D], F32, kind="ExternalOutput").ap()

    h1T = dscr("h1T", [NTT * 128, DC * TT], F32)
    QT = dscr("QT", [NH * 128, SEQ], BF16)
    KT = dscr("KT", [NH * 128, SEQ], BF16)
    Vd = dscr("Vd", [NH * 128, SEQ], BF16)
    MT = dscr("MT", [NH * 128, SEQ], BF16)

    es = contextlib.ExitStack()
    with es:
        cx = Ctx(nc, es)

        def sb(name, shape, dt):
            return es.enter_context(nc.sbuf_tensor(name, list(shape), dt))

        def psum(name, dt=F32, n=512):
            return es.enter_context(nc.psum_tensor(name, [128, n], dt))

        hT = sb("hT", [128, DC, TT], F32); hT_b = cx.bufs_n(DC)
        yT = sb("yT", [128, DC, TT], F32); yT_b = cx.bufs_n(DC)
        hnT = sb("hnT", [128, DC, TT], BF16); hnT_b = cx.bufs_n(DC)
        NACT = max(FC // 2, NH, 3 * (SEQ // TT), 3 * NH, 2 * XH)
        actT = sb("actT", [128, NACT, TT], BF16); actT_b = cx.bufs_n(NACT)
        xin = yT[:].rearrange("p c t -> p (c t)")[:, 0:D]
        NSTG, NWB = 4, 4
        stg = [sb(f"stg{i}", [128, 16 * 128], F32) for i in range(NSTG)]; stg_b = cx.bufs_n(NSTG)
        wbf = [sb(f"wbf{i}", [128, 16 * 128], BF16) for i in range(NWB)]; wbf_b = cx.bufs_n(NWB)
        NTMP = 6
        tmp = [sb(f"tmp{i}", [128, TT], F32) for i in range(NTMP)]; tmp_b = cx.bufs_n(NTMP)
        tmpb = [sb(f"tmpb{i}", [128, TT], BF16) for i in range(NTMP)]; tmpb_b = cx.bufs_n(NTMP)
        rstd = sb("rstd", [128, TT], F32); rstd_b = cx.buf()
        lnv = sb("lnv", [128, TT], F32); lnv_b = cx.buf()
        gains_s = sb("gains_s", [128, 9 * DC], F32); c_b = cx.buf()
        hg_s = sb("hg_s", [128, 2], F32)
        hg2_s = sb("hg2_s", [128, 2], F32)
        lam_s = sb("lam_s", [128, 256], F32)
        lam_t = sb("lam_t", [128, 8], F32)
        rope_s = sb("rope_s", [128, 2], F32)
        pt = [sb(f"pt{i}", [128, 2 * TT], BF16) for i in range(4)]; pt_b = cx.bufs_n(4)
        masks_bf = sb("masks_bf", [128, 4, TT], BF16)
        ident = sb("ident_s", [128, 128], F32)
        identb = sb("identb", [128, 128], BF16)
        tri32 = sb("tri32", [128, 128], F32)
        trib = sb("trib", [128, 128], BF16)
        ones32 = sb("ones32", [128, 128], F32)
        onesb = sb("onesb", [128, 128], BF16)
        cst = sb("cst", [128, 4], F32)
        Ct = sb("Ct", [128, TT], F32); St = sb("St", [128, TT], F32); cs_b = cx.buf()
        posi = sb("posi", [128, TT], I32); posf = sb("posf", [128, TT], F32); pos_b = cx.buf()
        ki = sb("ki", [128, TT], I32); ki_b = cx.buf()
        KmT = sb("KmT", [128, XH, MEM], BF16); Vm = sb("Vm", [128, MEM // 128, XH * 128], BF16); km_b = cx.buf()
        qx = actT[:, 0:XH, :]; qx_b = actT_b[0:XH]
        ox = actT[:, XH:2 * XH, :]; ox_b = actT_b[XH:2 * XH]
        NQ = SEQ // TT
        KTs = actT[:, 0:NQ, :].rearrange("p c t -> p (c t)")
        QTs = actT[:, NQ:2 * NQ, :].rearrange("p c t -> p (c t)")
        Vs = actT[:, 2 * NQ:3 * NQ, :].rearrange("p c t -> p (c t)").rearrange("p (k d) -> p k d", d=128)
        kqv_b = cx.bufs_n(3)
        vst_b = cx.buf()
        ofull = sb("ofull", [128, TT], F32); ofull_b = cx.buf()
        NPS = 7
        ps = [psum(f"ps{i}") for i in range(NPS)]; ps_b = cx.bufs_n(NPS)
        psb = psum("psb", BF16, 1024); psb_b = cx.buf()
        psrr = [0]
        castrr = [0]
        CAST_ENG = ["dve", "act"]

        PSROT = [[0, 1, 2]]

        def next_ps():
            rot = PSROT[0]
            i = psrr[0] % len(rot)
            psrr[0] += 1
            return rot[i]

        ring = {"pt": 0, "tmp": 0, "tmpb": 0, "xin": 0, "ostg": 0, "vstg": 0}

        def rr(name, n):
            i = ring[name]
            ring[name] = (i + 1) % n
            return i

        G = lambda k: gains_s[:, k * DC:(k + 1) * DC]
        eps_c, one_c, npi_c = cst[:, 0:1], cst[:, 1:2], cst[:, 2:3]

        wf = {"q": [], "issued": 0, "consumed": 0, "n": 0}
        LOOK = 3

        def plan_lin(W, n, KT_, ktp, kb0=0):
            for kb in range(KT_ // ktp):
                wf["q"].append((W, n, kb0 + kb, ktp))

        def w_issue(upto):
            while wf["issued"] <= min(upto, len(wf["q"]) - 1):
                W, n, kb, ktp = wf["q"][wf["issued"]]
                i = wf["n"] + wf["issued"]
                s, b = i % NSTG, i % NWB
                cx.dma(stg[s][:, :ktp * 128], W[n, kb], writes=[stg_b[s]])
                if i % 2 == 0:
                    cx.op("dve", lambda s=s, b=b, ktp=ktp: nc.vector.tensor_copy(out=wbf[b][:, :ktp * 128], in_=stg[s][:, :ktp * 128]),
                          reads=[stg_b[s]], writes=[wbf_b[b]])
                else:
                    cx.op("act", lambda s=s, b=b, ktp=ktp: nc.scalar.copy(out=wbf[b][:, :ktp * 128], in_=stg[s][:, :ktp * 128]),
                          reads=[stg_b[s]], writes=[wbf_b[b]])
                wf["issued"] += 1

        def w_kick():
            if wf["q"]:
                w_issue(wf["consumed"] + LOOK)

        def w_get(W, n, kb, ktp):
            if wf["consumed"] == len(wf["q"]):
                wf["q"].append((W, n, kb, ktp))
            e = wf["q"][wf["consumed"]]
            assert e[0] is W and e[1:] == (n, kb, ktp), ("weight plan mismatch", e[1:], (n, kb, ktp))
            w_issue(wf["consumed"] + LOOK)
            b = (wf["n"] + wf["consumed"]) % NWB
            wf["consumed"] += 1
            if wf["consumed"] == len(wf["q"]):
                wf["n"] += len(wf["q"]); wf["q"] = []; wf["issued"] = 0; wf["consumed"] = 0
            return b

        def lin_chunk(W, n, inT, in_b, KT_, ktp, pst, pst_b, kb0=0):
            nkb = KT_ // ktp
            for kb in range(nkb):
                b = w_get(W, n, kb0 + kb, ktp)
                fns = []
                for j in range(ktp):
                    k = kb * ktp + j
                    fns.append(lambda b=b, j=j, k=k: nc.tensor.matmul(
                        pst[:, :TT], wbf[b][:, j * 128:(j + 1) * 128], inT[:, k, :],
                        start=(k == 0), stop=(k == KT_ - 1)))
                cx.pe_group(fns, reads=[wbf_b[b]] + [in_b[kb * ktp + j] for j in range(ktp)], writes=[pst_b])

        def rms_stats(srcT, src_b, nch, div):
            a = 4
            for c in range(nch):
                t = rr("tmp", NTMP)
                cx.op("act", lambda c=c, t=t: nc.scalar.activation(out=tmp[t][:], in_=srcT[:, c, :], func=AF.Square),
                      reads=[src_b[c]], writes=[tmp_b[t]])
                cx.op("pe", lambda c=c, t=t: nc.tensor.matmul(ps[a][:, :TT], ones32[:], tmp[t][:],
                                                              start=(c == 0), stop=(c == nch - 1)),
                      reads=[tmp_b[t]], writes=[ps_b[a]])
            cx.op("act", lambda: nc.scalar.activation(out=lnv[:], in_=ps[a][:, :TT], func=AF.Ln, scale=1.0 / div, bias=eps_c),
                  reads=[ps_b[a]], writes=[lnv_b])
            cx.op("act", lambda: nc.scalar.activation(out=rstd[:], in_=lnv[:], func=AF.Exp, scale=-0.5),
                  reads=[lnv_b], writes=[rstd_b])

        def norm_to(srcT, src_b, gain, dstT, dst_b):
            rms_stats(srcT, src_b, DC, float(D))
            for c in range(DC):
                cx.op("dve", lambda c=c: nc.vector.scalar_tensor_tensor(
                    out=dstT[:, c, :], in0=srcT[:, c, :], scalar=gain[:, c:c + 1], in1=rstd[:],
                    op0=ALU.mult, op1=ALU.mult), reads=[src_b[c], rstd_b], writes=[dst_b[c]])

        def post_res(gain, factor):
            rms_stats(yT, yT_b, DC, float(D))
            for c in range(DC):
                t = rr("tmp", NTMP)
                cx.op("dve", lambda c=c, t=t: nc.vector.scalar_tensor_tensor(
                    out=tmp[t][:], in0=yT[:, c, :], scalar=gain[:, c:c + 1], in1=rstd[:],
                    op0=ALU.mult, op1=ALU.mult), reads=[yT_b[c], rstd_b], writes=[tmp_b[t]])
                cx.op("dve", lambda c=c, t=t: nc.vector.scalar_tensor_tensor(
                    out=hT[:, c, :], in0=tmp[t][:], scalar=float(factor), in1=hT[:, c, :],
                    op0=ALU.mult, op1=ALU.add), reads=[tmp_b[t], hT_b[c]], writes=[hT_b[c]])

        def load_T(src, row0, dstT, dst_b, ntok_tiles):
            for ts in range(ntok_tiles):
                cx.dma(xin, src[bass.ds(row0 + ts * 128, 128), :], writes=yT_b)
                for cg in range(max(1, DC // 4)):
                    ncg = min(4, DC)
                    p = next_ps()
                    cx.pe_group([lambda q=q, p=p, cg=cg: nc.tensor.transpose(
                        ps[p][:, q * 128:(q + 1) * 128], xin[:, (cg * 4 + q) * 128:(cg * 4 + q + 1) * 128], ident[:]) for q in range(ncg)],
                        reads=yT_b, writes=[ps_b[p]])
                    for q in range(ncg):
                        c = cg * 4 + q
                        cx.op("act", lambda q=q, c=c, p=p, ts=ts: nc.scalar.copy(
                            out=dstT[:, c, ts * 128:(ts + 1) * 128], in_=ps[p][:, q * 128:(q + 1) * 128]),
                            reads=[ps_b[p]], writes=[dst_b[c]])

        def ffn(Wg, Wu, Wd, gpre, gpost):
            FH = FC // 2
            for half in range(2):
                for n in range(FH):
                    plan_lin(Wg, half * FH + n, DC, ktpD)
                    plan_lin(Wu, half * FH + n, DC, ktpD)
                for c in range(DC):
                    plan_lin(Wd, c, FH, ktpF, kb0=half * (FH // ktpF))
            w_kick()
            norm_to(hT, hT_b, gpre, hnT, hnT_b)
            for half in range(2):
                for n in range(FH):
                    pg, pu = next_ps(), next_ps()
                    lin_chunk(Wg, half * FH + n, hnT, hnT_b, DC, ktpD, ps[pg], ps_b[pg])
                    lin_chunk(Wu, half * FH + n, hnT, hnT_b, DC, ktpD, ps[pu], ps_b[pu])
                    t = rr("tmp", NTMP)
                    cx.op("act", lambda pg=pg, t=t: nc.scalar.activation(out=tmp[t][:], in_=ps[pg][:, :TT], func=AF.Silu),
                          reads=[ps_b[pg]], writes=[tmp_b[t]])
                    cx.op("dve", lambda pu=pu, t=t, n=n: nc.vector.tensor_tensor(
                        out=actT[:, n, :], in0=tmp[t][:], in1=ps[pu][:, :TT], op=ALU.mult),
                        reads=[tmp_b[t], ps_b[pu]], writes=[actT_b[n]])
                for c in range(DC):
                    p = next_ps()
                    lin_chunk(Wd, c, actT, actT_b, FH, ktpF, ps[p], ps_b[p], kb0=half * (FH // ktpF))
                    if half == 0:
                        cx.op("act", lambda c=c, p=p: nc.scalar.copy(out=yT[:, c, :], in_=ps[p][:, :TT]),
                              reads=[ps_b[p]], writes=[yT_b[c]])
                    else:
                        cx.op("dve", lambda c=c, p=p: nc.vector.tensor_tensor(out=yT[:, c, :], in0=ps[p][:, :TT], in1=yT[:, c, :], op=ALU.add),
                              reads=[ps_b[p], yT_b[c]], writes=[yT_b[c]])
            post_res(gpost, 0.5)

        def to_tokmajor_store(srcb_ap, src_buf, dram_rows_fn):
            for ts in range(TS):
                cx.op("pe", lambda ts=ts: nc.tensor.transpose(psb[:, ts * 128:(ts + 1) * 128],
                                                              srcb_ap[:, ts * 128:(ts + 1) * 128], identb[:]),
                      reads=[src_buf], writes=[psb_b])
                v = rr("vstg", 2)
                cx.op("act", lambda ts=ts, v=v: nc.scalar.copy(out=vstg[v][:], in_=psb[:, ts * 128:(ts + 1) * 128]),
                      reads=[psb_b], writes=[vstg_b[v]])
                cx.dma(dram_rows_fn(ts), vstg[v][:], reads=[vstg_b[v]])

        cx.dma(gains_s[:], gains[:, :], writes=[c_b])
        cx.dma(hg_s[:], hg[:, :], writes=[c_b])
        cx.dma(lam_s[:], lamv[:, :], writes=[c_b])
        cx.dma(rope_s[:], ropet[:, :], writes=[c_b])
        cx.dma(ident[:], ident_d[:, :], writes=[c_b])
        cx.dma(tri32[:], tri_d[:, :], writes=[c_b])
        V = nc.vector
        cx.op("dve", lambda: V.memset(ones32[:], 1.0), writes=[c_b])
        cx.op("dve", lambda: V.memset(onesb[:], 1.0), writes=[c_b])
        cx.op("dve", lambda: V.memset(cst[:, 0:1], EPS), writes=[c_b])
        cx.op("dve", lambda: V.memset(cst[:, 1:2], 1.0), writes=[c_b])
        cx.op("dve", lambda: V.memset(cst[:, 2:3], -math.pi), writes=[c_b])
        cx.op("dve", lambda: V.memset(cst[:, 3:4], 0.0), writes=[c_b])
        cx.op("dve", lambda: V.tensor_copy(out=identb[:], in_=ident[:]), reads=[c_b], writes=[c_b])
        cx.op("dve", lambda: V.tensor_copy(out=trib[:], in_=tri32[:]), reads=[c_b], writes=[c_b])
        for j in range(4):
            cx.dma(tmp[j][:], masks[j * 128:(j + 1) * 128, :], writes=[tmp_b[j]])
            cx.op("dve", lambda j=j: V.tensor_copy(out=masks_bf[:, j, :], in_=tmp[j][:]), reads=[tmp_b[j]], writes=[c_b])
        cx.op("dve", lambda: V.tensor_tensor(out=lam_s[:, 0:64], in0=lam_s[:, 0:64], in1=lam_s[:, 64:128], op=ALU.mult), reads=[c_b], writes=[c_b])
        cx.op("dve", lambda: V.tensor_tensor(out=lam_s[:, 128:192], in0=lam_s[:, 128:192], in1=lam_s[:, 192:256], op=ALU.mult), reads=[c_b], writes=[c_b])
        cx.op("dve", lambda: V.tensor_reduce(out=lam_t[:, 0:1], in_=lam_s[:, 0:64], op=ALU.add, axis=mybir.AxisListType.X), reads=[c_b], writes=[c_b])
        cx.op("dve", lambda: V.tensor_reduce(out=lam_t[:, 1:2], in_=lam_s[:, 128:192], op=ALU.add, axis=mybir.AxisListType.X), reads=[c_b], writes=[c_b])
        cx.op("act", lambda: nc.scalar.activation(out=lam_t[:, 2:4], in_=lam_t[:, 0:2], func=AF.Exp), reads=[c_b], writes=[c_b])
        cx.op("dve", lambda: V.tensor_tensor(out=lam_t[:, 4:5], in0=lam_t[:, 3:4], in1=lam_t[:, 2:3], op=ALU.subtract), reads=[c_b], writes=[c_b])
        cx.op("dve", lambda: V.tensor_scalar(out=lam_t[:, 5:6], in0=lam_t[:, 4:5], scalar1=-float(lambda_init), scalar2=None, op0=ALU.add), reads=[c_b], writes=[c_b])
        neglam = lam_t[:, 5:6]
        cx.op("dve", lambda: V.tensor_scalar(out=hg2_s[:, 0:1], in0=hg_s[:, 0:1], scalar1=float(1.0 - lambda_init), scalar2=None, op0=ALU.mult), reads=[c_b], writes=[c_b])
        cx.op("dve", lambda: V.tensor_copy(out=hg2_s[:, 1:2], in_=hg_s[:, 1:2]), reads=[c_b], writes=[c_b])

        load_T(mem, 0, hT, hT_b, MEM // 128)
        norm_to(hT, hT_b, G(8), hnT, hnT_b)
        for h in range(XH):
            p = next_ps()
            lin_chunk(wkv, h, hnT, hnT_b, DC, ktpD, ps[p], ps_b[p])
            cx.op("act", lambda h=h, p=p: nc.scalar.copy(out=KmT[:, h, :], in_=ps[p][:, :MEM]), reads=[ps_b[p]], writes=[km_b])
        for h in range(XH):
            p = next_ps()
            lin_chunk(wkv, XH + h, hnT, hnT_b, DC, ktpD, ps[p], ps_b[p])
            t = rr("tmpb", NTMP)
            cx.op("act", lambda p=p, t=t: nc.scalar.copy(out=tmpb[t][:], in_=ps[p][:, :MEM]), reads=[ps_b[p]], writes=[tmpb_b[t]])
            for mt in range(MEM // 128):
                cx.op("pe", lambda t=t, mt=mt: nc.tensor.transpose(psb[:, mt * 128:(mt + 1) * 128], tmpb[t][:, mt * 128:(mt + 1) * 128], identb[:]),
                      reads=[tmpb_b[t]], writes=[psb_b])
                cx.op("act", lambda h=h, mt=mt: nc.scalar.copy(out=Vm[:, mt, h * 128:(h + 1) * 128], in_=psb[:, mt * 128:(mt + 1) * 128]),
                      reads=[psb_b], writes=[km_b])
        cx.end_iter()

        with nc.Fori(0, NTT) as i:
            tok0 = i * TT
            load_T(x, tok0, hT, hT_b, TS)
            ffn(wg1, wu1, wd1, G(0), G(1))
            for grp in range(6):
                for hh in range(HD if grp < 3 else HS):
                    plan_lin(win, (grp * HD + hh) if grp < 3 else (3 * HD + (grp - 3) * HS + hh), DC, ktpD)
                    if grp in (0, 1):
                        plan_lin(wsw, grp * HD + hh, DC, ktpD)
            w_kick()
            norm_to(hT, hT_b, G(2), hnT, hnT_b)
            cx.dma(h1T[bass.ds(i * 128, 128), :], hT[:].rearrange("p c t -> p (c t)"), reads=hT_b)
            cx.dma(posi[:], posb[:, bass.ds(tok0, TT)], writes=[pos_b])
            cx.op("dve", lambda: V.tensor_copy(out=posf[:], in_=posi[:]), reads=[pos_b], writes=[pos_b])
            for which, dst in ((0, St), (1, Ct)):
                t = rr("tmp", NTMP)
                cx.op("dve", lambda t=t, which=which: V.tensor_scalar(
                    out=tmp[t][:], in0=posf[:], scalar1=rope_s[:, 0:1], scalar2=(0.5 if which == 0 else 0.75),
                    op0=ALU.mult, op1=ALU.add), reads=[pos_b, c_b], writes=[tmp_b[t]])
                cx.op("dve", lambda t=t: V.tensor_copy(out=ki[:], in_=tmp[t][:]), reads=[tmp_b[t]], writes=[ki_b])
                t2 = rr("tmp", NTMP)
                cx.op("dve", lambda t2=t2: V.tensor_copy(out=tmp[t2][:], in_=ki[:]), reads=[ki_b], writes=[tmp_b[t2]])
                cx.op("dve", lambda t=t, t2=t2: V.tensor_tensor(out=tmp[t][:], in0=tmp[t][:], in1=tmp[t2][:], op=ALU.subtract),
                      reads=[tmp_b[t], tmp_b[t2]], writes=[tmp_b[t]])
                cx.op("dve", lambda t=t, t2=t2: V.scalar_tensor_tensor(out=tmp[t2][:], in0=tmp[t][:], scalar=0.0, in1=tmp[t][:],
                                                                      op0=ALU.is_lt, op1=ALU.add),
                      reads=[tmp_b[t]], writes=[tmp_b[t2]])
                cx.op("act", lambda t2=t2, dst=dst: nc.scalar.activation(out=dst[:], in_=tmp[t2][:], func=AF.Sin,
                                                                        scale=2.0 * math.pi, bias=npi_c),
                      reads=[tmp_b[t2], c_b], writes=[cs_b])
            cx.op("dve", lambda: V.tensor_scalar(out=St[:], in0=St[:], scalar1=rope_s[:, 1:2], scalar2=None, op0=ALU.mult),
                  reads=[cs_b, c_b], writes=[cs_b])
            Vst = actT[:, 2 * NH:3 * NH, :].rearrange("p c t -> p (c t)").rearrange("p (s n d) -> p s n d", s=TS, n=NH, d=128)
            for grp in range(6):
                nh = HD if grp < 3 else HS
                for hh in range(nh):
                    n = (grp * HD + hh) if grp < 3 else (3 * HD + (grp - 3) * HS + hh)
                    head = hh if grp < 3 else HD + hh
                    p = next_ps()
                    lin_chunk(win, n, hnT, hnT_b, DC, ktpD, ps[p], ps_b[p])
                    if grp in (0, 1):
                        slot = head if grp == 0 else NH + head
                        p2 = next_ps()
                        lin_chunk(wsw, grp * HD + hh, hnT, hnT_b, DC, ktpD, ps[p2], ps_b[p2])
                        t = rr("tmp", NTMP); t2 = rr("tmp", NTMP)
                        cx.op("dve", lambda t=t, p=p: V.tensor_tensor(out=tmp[t][:], in0=ps[p][:, :TT], in1=Ct[:], op=ALU.mult),
                              reads=[ps_b[p], cs_b], writes=[tmp_b[t]])
                        cx.op("dve", lambda t2=t2, p2=p2: V.tensor_tensor(out=tmp[t2][:], in0=ps[p2][:, :TT], in1=St[:], op=ALU.mult),
                              reads=[ps_b[p2], cs_b], writes=[tmp_b[t2]])
                        cx.op("dve", lambda t=t, t2=t2, slot=slot: V.tensor_tensor(out=actT[:, slot, :], in0=tmp[t][:], in1=tmp[t2][:], op=ALU.add),
                              reads=[tmp_b[t], tmp_b[t2]], writes=[actT_b[slot]])
                    elif grp in (3, 4):
                        slot = head if grp == 3 else NH + head
                        cx.op("act", lambda p=p, slot=slot: nc.scalar.copy(out=actT[:, slot, :], in_=ps[p][:, :TT]),
                              reads=[ps_b[p]], writes=[actT_b[slot]])
                    else:
                        tb = rr("tmpb", NTMP)
                        cx.op("act", lambda p=p, tb=tb: nc.scalar.copy(out=tmpb[tb][:], in_=ps[p][:, :TT]),
                              reads=[ps_b[p]], writes=[tmpb_b[tb]])
                        for ts in range(TS):
                            cx.op("pe", lambda ts=ts, tb=tb: nc.tensor.transpose(psb[:, ts * 128:(ts + 1) * 128],
                                                                              tmpb[tb][:, ts * 128:(ts + 1) * 128], identb[:]),
                                  reads=[tmpb_b[tb]], writes=[psb_b])
                            cx.op("act", lambda ts=ts, head=head: nc.scalar.copy(out=actT[:, 2 * NH + head, ts * 128:(ts + 1) * 128],
                                                                               in_=psb[:, ts * 128:(ts + 1) * 128]),
                                  reads=[psb_b], writes=[actT_b[2 * NH + head]])
            HB = min(16, NH)
            for n0 in range(0, NH, HB):
                for which, (dst, qq) in enumerate(((QT, "sp"), (KT, "act"), (Vd, "act"))):
                    cx.dma(dst[n0 * 128:(n0 + HB) * 128, bass.ds(tok0, TT)].rearrange("(n p) t -> p n t", p=128),
                           actT[:, which * NH + n0:which * NH + n0 + HB, :],
                           reads=actT_b[which * NH + n0:which * NH + n0 + HB], q=qq)
            cx.end_iter()

        NQG = SEQ // TT
        assert NQG <= DC
        sc_d = 64 ** -0.5
        sc_s = 128 ** -0.5

        def load_head(hidx):
            cx.dma(KTs, KT[bass.ds(hidx * 128, 128), :], writes=[kqv_b[0]])
            cx.dma(QTs, QT[bass.ds(hidx * 128, 128), :], writes=[kqv_b[1]])
            cx.dma(actT[:, 2 * NQ:3 * NQ, :].rearrange("p c t -> p (c t)"), Vd[bass.ds(hidx * 128, 128), :], writes=[kqv_b[2]])

        def head_norm_store(hidx, g, gcol, a=3):
            t = rr("tmp", NTMP)
            cx.op("act", lambda t=t: nc.scalar.activation(out=tmp[t][:], in_=ofull[:], func=AF.Square),
                  reads=[ofull_b], writes=[tmp_b[t]])
            cx.op("pe", lambda t=t: nc.tensor.matmul(ps[a][:, :TT], ones32[:], tmp[t][:], start=True, stop=True),
                  reads=[tmp_b[t]], writes=[ps_b[a]])
            cx.op("act", lambda: nc.scalar.activation(out=lnv[:], in_=ps[a][:, :TT], func=AF.Ln, scale=1.0 / 128, bias=eps_c),
                  reads=[ps_b[a]], writes=[lnv_b])
            cx.op("act", lambda: nc.scalar.activation(out=rstd[:], in_=lnv[:], func=AF.Exp, scale=-0.5),
                  reads=[lnv_b], writes=[rstd_b])
            cx.op("dve", lambda: V.scalar_tensor_tensor(out=hnT[:, g, :], in0=ofull[:], scalar=hg2_s[:, gcol:gcol + 1],
                                                       in1=rstd[:], op0=ALU.mult, op1=ALU.mult),
                  reads=[ofull_b, rstd_b, c_b], writes=[hnT_b[g]])
            if g == NQG - 1:
                cx.dma(MT[bass.ds(hidx * 128, 128), :], hnT[:, 0:NQG, :].rearrange("p c t -> p (c t)"), reads=hnT_b[0:NQG])

        def skew(tasks, stageA, stageB, lag):
            n = len(tasks)
            st = {}
            for i in range(n + lag):
                if i < n:
                    st[i] = stageA(tasks[i])
                j = i - lag
                if j >= 0:
                    stageB(tasks[j], st.pop(j))

        with nc.Fori(0, HD) as hd:
            load_head(hd)
            tasks = [(g, kt) for g in range(NQG) for kt in range((g + 1) * TS)]
            PSROT[0] = [0, 1, 2, 5, 6]

            def dA(task):
                g, kt = task
                qs = slice(g * TT, (g + 1) * TT); ks = slice(kt * 128, (kt + 1) * 128)
                dg = kt - g * TS
                k = rr("pt", 4)
                for comp in range(2):
                    p = next_ps()
                    lo = comp * 64
                    cx.op("pe", lambda: nc.tensor.matmul(ps[p][:, :TT], KTs[lo:lo + 64, ks], QTs[lo:lo + 64, qs], start=True, stop=True),
                          reads=[kqv_b[0], kqv_b[1]], writes=[ps_b[p]])
                    cx.op("act", lambda: nc.scalar.activation(out=pt[k][:, comp * TT:(comp + 1) * TT], in_=ps[p][:, :TT], func=AF.Exp, scale=sc_d),
                          reads=[ps_b[p]], writes=[pt_b[k]])
                    if dg >= 0:
                        cx.op("dve", lambda: V.tensor_tensor(out=pt[k][:, comp * TT:(comp + 1) * TT], in0=pt[k][:, comp * TT:(comp + 1) * TT],
                                                             in1=masks_bf[:, dg, :], op=ALU.mult),
                              reads=[pt_b[k], c_b], writes=[pt_b[k]])
                return k

            def dB(task, k):
                g, kt = task
                nkt = (g + 1) * TS
                cx.op("pe", lambda: nc.tensor.matmul(ps[3][:, :2 * TT], Vs[:, kt, :], pt[k][:], start=(kt == 0), stop=(kt == nkt - 1)),
                      reads=[pt_b[k], kqv_b[2]], writes=[ps_b[3]])
                cx.op("pe", lambda: nc.tensor.matmul(ps[4][:, :2 * TT], onesb[:], pt[k][:], start=(kt == 0), stop=(kt == nkt - 1)),
                      reads=[pt_b[k]], writes=[ps_b[4]])
                if kt == nkt - 1:
                    t1 = rr("tmp", NTMP); t2 = rr("tmp", NTMP)
                    for (c, t) in ((0, t1), (1, t2)):
                        cx.op("dve", lambda: V.reciprocal(out=tmp[t][:], in_=ps[4][:, c * TT:(c + 1) * TT]), reads=[ps_b[4]], writes=[tmp_b[t]])
                        cx.op("dve", lambda: V.tensor_tensor(out=tmp[t][:], in0=ps[3][:, c * TT:(c + 1) * TT], in1=tmp[t][:], op=ALU.mult),
                              reads=[ps_b[3], tmp_b[t]], writes=[tmp_b[t]])
                    cx.op("dve", lambda: V.scalar_tensor_tensor(out=ofull[:], in0=tmp[t2][:], scalar=neglam, in1=tmp[t1][:],
                                                               op0=ALU.mult, op1=ALU.add),
                          reads=[tmp_b[t1], tmp_b[t2], c_b], writes=[ofull_b])
                    head_norm_store(hd, g, 0, a=0)

            skew(tasks, dA, dB, 2)
            PSROT[0] = [0, 1, 2]
            cx.end_iter()

        with nc.Fori(0, HS) as hs:
            hidx = hs + HD
            load_head(hidx)
            PO, PC = 4, 5
            PSROT[0] = [0, 1, 2, 6]
            tasks = [(g, kt) for g in range(NQG) for kt in range((g + 1) * TS - 1, -1, -1)]

            n_t = len(tasks)
            S = [dict() for _ in range(n_t)]

            def flags(i):
                g, kt = tasks[i]
                nkt = (g + 1) * TS
                return g, kt, kt - g * TS, (kt == nkt - 1), (kt == 0)

            def sa_pe(i):
                g, kt, dg, first, last = flags(i)
                p = next_ps(); S[i]["p"] = p
                cx.op("pe", lambda: nc.tensor.matmul(ps[p][:, :TT], KTs[:, kt * 128:(kt + 1) * 128], QTs[:, g * TT:(g + 1) * TT],
                                                     start=True, stop=True),
                      reads=[kqv_b[0], kqv_b[1]], writes=[ps_b[p]])

            def sa_act(i):
                p = S[i]["p"]
                te = rr("tmp", NTMP); tl = rr("tmp", NTMP)
                S[i]["te"], S[i]["tl"] = te, tl
                cx.op("act", lambda: nc.scalar.activation(out=tmp[te][:], in_=ps[p][:, :TT], func=AF.Exp, scale=-sc_s),
                      reads=[ps_b[p]], writes=[tmp_b[te]])
                cx.op("act", lambda: nc.scalar.activation(out=tmp[tl][:], in_=tmp[te][:], func=AF.Ln, scale=1.0, bias=one_c),
                      reads=[tmp_b[te], c_b], writes=[tmp_b[tl]])

            def sa_dve(i):
                g, kt, dg, first, last = flags(i)
                p, te, tl = S[i]["p"], S[i]["te"], S[i]["tl"]
                tb = rr("tmpb", NTMP); S[i]["tb"] = tb
                if dg >= 0:
                    cx.op("dve", lambda: V.scalar_tensor_tensor(out=tmp[te][:], in0=ps[p][:, :TT], scalar=-sc_s, in1=tmp[tl][:],
                                                               op0=ALU.mult, op1=ALU.subtract),
                          reads=[ps_b[p], tmp_b[tl]], writes=[tmp_b[te]])
                    cx.op("dve", lambda: V.tensor_tensor(out=tmpb[tb][:], in0=tmp[te][:], in1=masks_bf[:, 2 + dg, :], op=ALU.mult),
                          reads=[tmp_b[te], c_b], writes=[tmpb_b[tb]])
                else:
                    cx.op("dve", lambda: V.scalar_tensor_tensor(out=tmpb[tb][:], in0=ps[p][:, :TT], scalar=-sc_s, in1=tmp[tl][:],
                                                               op0=ALU.mult, op1=ALU.subtract),
                          reads=[ps_b[p], tmp_b[tl]], writes=[tmpb_b[tb]])

            def sb_pe(i):
                tb = S[i]["tb"]
                p2 = next_ps(); S[i]["p2"] = p2
                cx.op("pe", lambda: nc.tensor.matmul(ps[p2][:, :TT], trib[:], tmpb[tb][:], start=True, stop=True),
                      reads=[tmpb_b[tb], c_b], writes=[ps_b[p2]])

            def sb_dve(i):
                p2, tl = S[i]["p2"], S[i]["tl"]
                cx.op("dve", lambda: V.tensor_tensor(out=tmp[tl][:], in0=ps[p2][:, :TT], in1=tmp[tl][:], op=ALU.subtract),
                      reads=[ps_b[p2], tmp_b[tl]], writes=[tmp_b[tl]])

            def sc(i):
                g, kt, dg, first, last = flags(i)
                tl, tb = S[i]["tl"], S[i]["tb"]
                if not first:
                    cx.op("dve", lambda: V.tensor_tensor(out=tmp[tl][:], in0=ps[PC][:, :TT], in1=tmp[tl][:], op=ALU.add),
                          reads=[ps_b[PC], tmp_b[tl]], writes=[tmp_b[tl]])
                if not last:
                    cx.op("pe", lambda: nc.tensor.matmul(ps[PC][:, :TT], onesb[:], tmpb[tb][:], start=first, stop=(kt == 1)),
                          reads=[tmpb_b[tb]], writes=[ps_b[PC]])
                ta = rr("tmpb", NTMP); S[i]["ta"] = ta
                cx.op("act", lambda: nc.scalar.activation(out=tmpb[ta][:], in_=tmp[tl][:], func=AF.Exp),
                      reads=[tmp_b[tl]], writes=[tmpb_b[ta]])
                if dg >= 0:
                    cx.op("dve", lambda: V.tensor_tensor(out=tmpb[ta][:], in0=tmpb[ta][:], in1=masks_bf[:, 2 + dg, :], op=ALU.mult),
                          reads=[tmpb_b[ta], c_b], writes=[tmpb_b[ta]])

            def sd(i):
                g, kt, dg, first, last = flags(i)
                ta = S[i]["ta"]
                cx.op("pe", lambda: nc.tensor.matmul(ps[PO][:, :TT], Vs[:, kt, :], tmpb[ta][:], start=first, stop=last),
                      reads=[tmpb_b[ta], kqv_b[2]], writes=[ps_b[PO]])
                if last:
                    cx.op("act", lambda: nc.scalar.copy(out=ofull[:], in_=ps[PO][:, :TT]), reads=[ps_b[PO]], writes=[ofull_b])
                    head_norm_store(hidx, g, 1)

            ok = lambda j: 0 <= j < n_t

            def sc_cadd(i):
                g, kt, dg, first, last = flags(i)
                tl, tb = S[i]["tl"], S[i]["tb"]
                if not first:
                    cx.op("dve", lambda: V.tensor_tensor(out=tmp[tl][:], in0=ps[PC][:, :TT], in1=tmp[tl][:], op=ALU.add),
                          reads=[ps_b[PC], tmp_b[tl]], writes=[tmp_b[tl]])
                if not last:
                    cx.op("pe", lambda: nc.tensor.matmul(ps[PC][:, :TT], onesb[:], tmpb[tb][:], start=first, stop=(kt == 1)),
                          reads=[tmpb_b[tb]], writes=[ps_b[PC]])

            def sc_exp(i):
                g, kt, dg, first, last = flags(i)
                tl = S[i]["tl"]
                ta = rr("tmpb", NTMP); S[i]["ta"] = ta
                cx.op("act", lambda: nc.scalar.activation(out=tmpb[ta][:], in_=tmp[tl][:], func=AF.Exp),
                      reads=[tmp_b[tl]], writes=[tmpb_b[ta]])
                if dg >= 0:
                    cx.op("dve", lambda: V.tensor_tensor(out=tmpb[ta][:], in0=tmpb[ta][:], in1=masks_bf[:, 2 + dg, :], op=ALU.mult),
                          reads=[tmpb_b[ta], c_b], writes=[tmpb_b[ta]])

            for i in range(-3, n_t + 1):
                if ok(i + 1): sb_pe(i + 1)
                if ok(i + 3): sa_pe(i + 3)
                if ok(i): sc_cadd(i)
                if ok(i - 1): sd(i - 1)
                if ok(i + 2): sa_act(i + 2)
                if ok(i + 1): sb_dve(i + 1)
                if ok(i + 2): sa_dve(i + 2)
                if ok(i): sc_exp(i)
            PSROT[0] = [0, 1, 2]
            cx.end_iter()

        sc_x = 128 ** -0.5
        with nc.Fori(0, NTT) as i:
            tok0 = i * TT
            for c in range(DC):
                plan_lin(wout, c, NH, ktpH)
            w_kick()
            cx.dma(hT[:].rearrange("p c t -> p (c t)"), h1T[bass.ds(i * 128, 128), :], writes=hT_b)
            for n0 in range(0, NH, 16):
                n1 = min(NH, n0 + 16)
                cx.dma(actT[:, n0:n1, :], MT[n0 * 128:n1 * 128, bass.ds(tok0, TT)].rearrange("(n p) t -> p n t", p=128),
                       writes=actT_b[n0:n1])
            for c in range(DC):
                p = next_ps()
                lin_chunk(wout, c, actT, actT_b, NH, ktpH, ps[p], ps_b[p])
                cx.op("act", lambda c=c, p=p: nc.scalar.copy(out=yT[:, c, :], in_=ps[p][:, :TT]), reads=[ps_b[p]], writes=[yT_b[c]])
            for h in range(XH):
                plan_lin(wq, h, DC, ktpD)
            w_kick()
            post_res(G(3), 1.0)
            norm_to(hT, hT_b, G(4), hnT, hnT_b)
            for h in range(XH):
                p = next_ps()
                lin_chunk(wq, h, hnT, hnT_b, DC, ktpD, ps[p], ps_b[p])
                cx.op("act", lambda h=h, p=p: nc.scalar.copy(out=qx[:, h, :], in_=ps[p][:, :TT]), reads=[ps_b[p]], writes=[qx_b[h]])
            for c in range(DC):
                plan_lin(wo, c, XH, XH)
            w_kick()
            for h in range(XH):
                PO, PL = 4, 5
                nmt = MEM // 128
                for mt in range(nmt):
                    p = next_ps()
                    cx.op("pe", lambda p=p, h=h, mt=mt: nc.tensor.matmul(ps[p][:, :TT], KmT[:, h, mt * 128:(mt + 1) * 128], qx[:, h, :], start=True, stop=True),
                          reads=[km_b, qx_b[h]], writes=[ps_b[p]])
                    tb = rr("tmpb", NTMP)
                    cx.op("act", lambda p=p, tb=tb: nc.scalar.activation(out=tmpb[tb][:], in_=ps[p][:, :TT], func=AF.Exp, scale=sc_x),
                          reads=[ps_b[p]], writes=[tmpb_b[tb]])
                    cx.op("pe", lambda tb=tb, h=h, mt=mt: nc.tensor.matmul(ps[PO][:, :TT], Vm[:, mt, h * 128:(h + 1) * 128], tmpb[tb][:],
                                                                         start=(mt == 0), stop=(mt == nmt - 1)),
                          reads=[tmpb_b[tb], km_b], writes=[ps_b[PO]])
                    cx.op("pe", lambda tb=tb, mt=mt: nc.tensor.matmul(ps[PL][:, :TT], onesb[:], tmpb[tb][:], start=(mt == 0), stop=(mt == nmt - 1)),
                          reads=[tmpb_b[tb]], writes=[ps_b[PL]])
                t = rr("tmp", NTMP)
                cx.op("dve", lambda t=t: V.reciprocal(out=tmp[t][:], in_=ps[PL][:, :TT]), reads=[ps_b[PL]], writes=[tmp_b[t]])
                cx.op("dve", lambda t=t, h=h: V.tensor_tensor(out=ox[:, h, :], in0=ps[PO][:, :TT], in1=tmp[t][:], op=ALU.mult),
                      reads=[ps_b[PO], tmp_b[t]], writes=[ox_b[h]])
            for c in range(DC):
                p = next_ps()
                lin_chunk(wo, c, ox, ox_b, XH, XH, ps[p], ps_b[p])
                cx.op("act", lambda c=c, p=p: nc.scalar.copy(out=yT[:, c, :], in_=ps[p][:, :TT]), reads=[ps_b[p]], writes=[yT_b[c]])
            post_res(G(5), 1.0)
            ffn(wg2, wu2, wd2, G(6), G(7))
            for ts in range(TS):
                for cg in range(max(1, DC // 4)):
                    ncg = min(4, DC)
                    p = next_ps()
                    cx.pe_group([lambda q=q, cg=cg, ts=ts, p=p: nc.tensor.transpose(
                        ps[p][:, q * 128:(q + 1) * 128], hT[:, cg * 4 + q, ts * 128:(ts + 1) * 128], ident[:]) for q in range(ncg)],
                        reads=[hT_b[cg * 4 + q] for q in range(ncg)], writes=[ps_b[p]])
                    cx.op("act", lambda cg=cg, p=p, ncg=ncg: nc.scalar.copy(out=xin[:, cg * 512:cg * 512 + ncg * 128], in_=ps[p][:, :ncg * 128]),
                          reads=[ps_b[p]], writes=yT_b)
                cx.dma(out[bass.ds(tok0 + ts * 128, 128), :], xin, reads=yT_b)
            cx.end_iter()
    return nc


def prep_inputs(cfg, inp, lambda_init):
    D, FF, SEQ, HD, HS, MEM, XH, TT = (cfg[k] for k in ["D", "FF", "SEQ", "HD", "HS", "MEM", "XH", "TT"])
    DC, FC, NH = D // 128, FF // 128, HD + HS
    ktpD, ktpF, ktpH = min(16, DC), min(8, FC // 2), min(16, NH)
    f = lambda a: np.ascontiguousarray(np.asarray(a, dtype=np.float32))
    m = {}
    m["x"] = f(inp["x"]).reshape(SEQ, D)
    m["mem"] = f(inp["mem"]).reshape(MEM, D)
    m["posb"] = np.ascontiguousarray(np.broadcast_to(np.asarray(inp["positions"]).reshape(1, SEQ).astype(np.int32), (128, SEQ)))
    gl = ["ffn1_norm_pre", "ffn1_norm_post", "mix_norm_pre", "mix_norm_post", "xattn_norm_pre", "xattn_norm_post",
          "ffn2_norm_pre", "ffn2_norm_post", "mem_norm"]
    m["gains"] = np.ascontiguousarray(np.concatenate([gain_t(f(inp[k])[0]) for k in gl], axis=1))
    m["hg"] = np.ascontiguousarray(np.stack([f(inp["diff_subln"])[0], f(inp["sb_norm"])[0]], axis=1))
    lv = np.concatenate([f(inp[k])[0] for k in ["lambda_q1", "lambda_k1", "lambda_q2", "lambda_k2"]])
    m["lamv"] = np.ascontiguousarray(np.broadcast_to(lv[None, :], (128, 256)))
    rt = np.zeros((128, 2), np.float32)
    inv = ROPE_THETA ** (-np.arange(0, 16, 2, dtype=np.float32) / 16)
    for base in (0, 64):
        rt[base:base + 8, 0] = inv / np.float32(2 * math.pi)
        rt[base + 8:base + 16, 0] = inv / np.float32(2 * math.pi)
        rt[base:base + 8, 1] = -1.0
        rt[base + 8:base + 16, 1] = 1.0
    m["ropet"] = rt
    kp = np.arange(128)[:, None]
    qf = np.arange(TT)[None, :]
    mk = [((kp + 128 * j) <= qf) for j in range(TT // 128)] + [((kp + 128 * j) < qf) for j in range(TT // 128)]
    m["masks"] = np.ascontiguousarray(np.concatenate(mk, axis=0).astype(np.float32))
    m["ident"] = np.eye(128, dtype=np.float32)
    m["tri"] = (np.arange(128)[:, None] > np.arange(128)[None, :]).astype(np.float32)
    m["wg1"] = tile_w(f(inp["ffn1_w_gate"])[0], ktpD); m["wu1"] = tile_w(f(inp["ffn1_w_up"])[0], ktpD)
    m["wd1"] = tile_w(f(inp["ffn1_w_down"])[0], ktpF)
    m["wg2"] = tile_w(f(inp["ffn2_w_gate"])[0], ktpD); m["wu2"] = tile_w(f(inp["ffn2_w_up"])[0], ktpD)
    m["wd2"] = tile_w(f(inp["ffn2_w_down"])[0], ktpF)
    w_in = f(inp["w_in"])[0]
    m["win"] = tile_w(w_in, ktpD)
    perm = np.arange(128)
    for base in (0, 64):
        perm[base:base + 8] = np.arange(base + 8, base + 16)
        perm[base + 8:base + 16] = np.arange(base, base + 8)
    cols = np.concatenate([h * 128 + perm for h in range(2 * HD)])
    m["wsw"] = tile_w(np.ascontiguousarray(w_in[:, cols]), ktpD)
    m["wout"] = tile_w(f(inp["w_out"])[0], ktpH)
    m["wq"] = tile_w(f(inp["xattn_w_q"])[0], ktpD)
    m["wkv"] = tile_w(f(inp["xattn_w_kv"])[0], ktpD)
    m["wo"] = tile_w(f(inp["xattn_w_o"])[0], XH)
    return m


def run(cfg, inp, lambda_init):
    nc = build(cfg, lambda_init)
    m = prep_inputs(cfg, inp, lambda_init)
    res = run_bass_kernel_spmd(nc, [m], core_ids=[0])
    return res.results[0]["out"]


def kernel(**inputs):
    lambda_init = 0.8 - 0.6 * math.exp(-0.3 * 0)
    o = run(FULL, inputs, lambda_init)
    return np.asarray(o, dtype=np.float32).reshape(1, FULL["SEQ"], FULL["D"])
```
